# Optimizing a Trainium2 kernel written in Bass

```python
import jax, jax.numpy as jnp
from jax import lax
import numpy as np

D_MODEL = 1024
BATCH = 8
SEQ = 2048
DEPTH = 1
DEC_BATCH = 128
DEC_SEQ = 4
PAST_LEN = 16384
PAGE_SIZE = 128

RET_HEADS = 4
RET_DK = D_MODEL // 8
RET_DV = 2 * RET_DK
RET_QK = RET_HEADS * RET_DK
RET_V = RET_HEADS * RET_DV
RET_CHUNK = 128
ROPE_BASE = 10000.0
SG_GROUPS = 4
SG_GW = D_MODEL // 8
SG_W = SG_GROUPS * SG_GW
SG_CHUNK = 128
X_HEADS = 4
X_DH = D_MODEL // 8
X_W = X_HEADS * X_DH
MEM_LEN = 256
N_BRANCH = 3
D_FF = 2816
CONV_W = 3
EPS = 1e-6
IN_WIDTHS = (RET_QK, RET_QK, RET_V, RET_V, SG_W, SG_W, X_W, N_BRANCH * D_MODEL)
IN_COLS = sum(IN_WIDTHS)

kernel_name = "retention_sgu_memxattn_convffn_step"

F32 = jnp.float32


def _rmsnorm(x, g):
    xf = x.astype(F32)
    y = xf * lax.rsqrt(jnp.mean(xf * xf, axis=-1, keepdims=True) + EPS)
    return (y * g.astype(F32)).astype(x.dtype)


def _stdnorm(xf):
    mu = jnp.mean(xf, axis=-1, keepdims=True)
    var = jnp.mean(jnp.square(xf - mu), axis=-1, keepdims=True)
    return (xf - mu) * lax.rsqrt(var + EPS)


def _rope(x, pos):
    d = x.shape[-1]
    inv = ROPE_BASE ** (-jnp.arange(0, d, 2, dtype=F32) / d)
    ang = pos.astype(F32)[:, None] * inv[None, :]
    cos = jnp.cos(ang)[None, :, None, :]
    sin = jnp.sin(ang)[None, :, None, :]
    x1, x2 = x[..., : d // 2], x[..., d // 2:]
    return jnp.concatenate([x1 * cos - x2 * sin, x1 * sin + x2 * cos], axis=-1)


def _retention_chunk(S, qkv, lg):
    q, k, v = qkv
    C = q.shape[1]
    n = jnp.arange(C, dtype=F32)
    diff = n[:, None] - n[None, :]
    causal = diff >= 0
    dmat = jnp.where(causal[None], jnp.exp(jnp.where(causal, diff, 0.0)[None] * lg[:, None, None]), 0.0)
    scores = jnp.einsum('bchd,bshd->bhcs', q, k) * dmat[None]
    inner = jnp.einsum('bhcs,bshe->bche', scores, v)
    q_decay = jnp.exp((n + 1.0)[:, None] * lg[None, :])
    cross = jnp.einsum('bchd,bhde->bche', q, S) * q_decay[None, :, :, None]
    k_decay = jnp.exp((C - 1.0 - n)[:, None] * lg[None, :])
    S_new = jnp.exp(C * lg)[None, :, None, None] * S + jnp.einsum('bchd,bche->bhde', k * k_decay[None, :, :, None], v)
    return S_new, inner + cross


def _retention_branch(q, k, v, g, pos, S0, gn_g, chunk):
    B, L = q.shape[:2]
    lg = jnp.log1p(-jnp.exp2(-5.0 - jnp.arange(RET_HEADS, dtype=F32)))
    qh = _rope(q.astype(F32).reshape(B, L, RET_HEADS, RET_DK), pos)
    kh = _rope(k.astype(F32).reshape(B, L, RET_HEADS, RET_DK), pos) * (RET_DK ** -0.5)
    vh = v.astype(F32).reshape(B, L, RET_HEADS, RET_DV)
    nc = L // chunk
    to_chunks = lambda t: jnp.swapaxes(t.reshape(B, nc, chunk, *t.shape[2:]), 0, 1)
    S_fin, o = lax.scan(lambda S, c: _retention_chunk(S, c, lg), S0.astype(F32),
                        (to_chunks(qh), to_chunks(kh), to_chunks(vh)))
    o = jnp.swapaxes(o, 0, 1).reshape(B, L, RET_HEADS, RET_DV)
    o = _stdnorm(o) * gn_g.astype(F32)
    out = jax.nn.silu(g.astype(F32)) * o.reshape(B, L, RET_V)
    return out.astype(q.dtype), S_fin


def _sgu_branch(u, v, ln_g, ws, bs):
    B, L, _ = u.shape
    C = min(SG_CHUNK, L)
    nc = L // C
    vn = _stdnorm(v.astype(F32)) * ln_g.astype(F32)
    vg = vn.reshape(B, nc, C, SG_GROUPS, SG_GW)
    w = jnp.tril(ws[:, :C, :C].astype(F32))
    bias = jnp.swapaxes(bs[:, :C].astype(F32), 0, 1)
    mixed = jnp.einsum('gts,bnsgc->bntgc', w, vg) + bias[None, None, :, :, None]
    out = u.astype(F32) * mixed.reshape(B, L, SG_W)
    return out.astype(u.dtype), vn.astype(u.dtype)


def _mem_kv(mem, g, w):
    B, M, _ = mem.shape
    kv = _rmsnorm(mem, g) @ w
    k, v = jnp.split(kv, 2, axis=-1)
    return k.reshape(B, M, X_HEADS, X_DH), v.reshape(B, M, X_HEADS, X_DH)


def _xattn(q, mk, mv):
    B, L, _ = q.shape
    qh = q.astype(F32).reshape(B, L, X_HEADS, X_DH)
    s = jnp.einsum('blhd,bmhd->bhlm', qh, mk.astype(F32)) * (X_DH ** -0.5)
    p = jax.nn.softmax(s, axis=-1)
    o = jnp.einsum('bhlm,bmhd->blhd', p, mv.astype(F32))
    return o.reshape(B, L, X_W).astype(q.dtype)


def _mixer_block(x, pos, S0, mk, mv, norm_g, w_in, b_gate, ret_gn_g, sg_ln_g, sg_ws, sg_bs,
                 w_br_ret, w_br_sg, w_br_x, w_o, ret_chunk):
    B, L, _ = x.shape
    h = _rmsnorm(x, norm_g)
    p = h @ w_in
    cuts = np.cumsum(IN_WIDTHS)[:-1].tolist()
    q, k, v, g, su, sv, xq, gt = jnp.split(p, cuts, axis=-1)
    o_ret, S_new = _retention_branch(q, k, v, g, pos, S0, ret_gn_g, ret_chunk)
    o_sg, v_rows = _sgu_branch(jax.nn.gelu(su), jax.nn.gelu(sv), sg_ln_g, sg_ws, sg_bs)
    o_x = _xattn(xq, mk, mv)
    g_ret, g_sg, g_x = jnp.split(jax.nn.sigmoid(gt + b_gate), N_BRANCH, axis=-1)
    merged = g_ret * (o_ret @ w_br_ret) + g_sg * (o_sg @ w_br_sg) + g_x * (o_x @ w_br_x)
    return x + merged @ w_o, S_new, v_rows


def _ffn_block(x, prev, norm_g, w_up, conv_w, conv_b, w_down):
    L = x.shape[1]
    z = _rmsnorm(x, norm_g) @ w_up
    zp = jnp.concatenate([prev.astype(z.dtype), z], axis=1)
    zc = conv_b + sum(conv_w[j] * zp[:, j:j + L] for j in range(CONV_W))
    a, b = jnp.split(zc, 2, axis=-1)
    return x + (jax.nn.gelu(a) * b) @ w_down, zp[:, L:]


def setup_inputs(seed: int = 0) -> dict:
    key = jax.random.key(seed)
    ks = jax.random.split(key, 32)
    nrm = lambda i, shape, s: jax.random.normal(ks[i], shape, F32) * s
    return {
        "x_prompt": nrm(0, (BATCH, SEQ, D_MODEL), 1.0),
        "x_sample": nrm(1, (DEC_BATCH, DEC_SEQ, D_MODEL), 1.0),
        "mem_prompt": nrm(2, (BATCH, MEM_LEN, D_MODEL), 1.0),
        "state_ret": nrm(3, (DEPTH, DEC_BATCH, RET_HEADS, RET_DK, RET_DV), 0.1),
        "state_conv": nrm(4, (DEPTH, DEC_BATCH, CONV_W - 1, 2 * D_FF), 1.0),
        "cache_mem_k": nrm(5, (DEPTH, DEC_BATCH, MEM_LEN, X_HEADS, X_DH), 1.0),
        "cache_mem_v": nrm(6, (DEPTH, DEC_BATCH, MEM_LEN, X_HEADS, X_DH), 1.0),
        "norm_mix_g": 1.0 + nrm(7, (DEPTH, D_MODEL), 0.02),
        "w_in": nrm(8, (DEPTH, D_MODEL, IN_COLS), D_MODEL ** -0.5),
        "b_gate": nrm(9, (DEPTH, N_BRANCH * D_MODEL), 0.01),
        "ret_gn_g": 1.0 + nrm(10, (DEPTH, RET_HEADS, RET_DV), 0.02),
        "sg_ln_g": 1.0 + nrm(11, (DEPTH, SG_W), 0.02),
        "sg_ws": nrm(12, (DEPTH, SG_GROUPS, SG_CHUNK, SG_CHUNK), SG_CHUNK ** -0.5),
        "sg_bs": 1.0 + nrm(13, (DEPTH, SG_GROUPS, SG_CHUNK), 0.01),
        "mem_norm_g": 1.0 + nrm(14, (DEPTH, D_MODEL), 0.02),
        "w_mem_kv": nrm(15, (DEPTH, D_MODEL, 2 * X_W), D_MODEL ** -0.5),
        "w_br_ret": nrm(16, (DEPTH, RET_V, D_MODEL), RET_V ** -0.5),
        "w_br_sg": nrm(17, (DEPTH, SG_W, D_MODEL), SG_W ** -0.5),
        "w_br_x": nrm(18, (DEPTH, X_W, D_MODEL), X_W ** -0.5),
        "w_o": nrm(19, (DEPTH, D_MODEL, D_MODEL), D_MODEL ** -0.5),
        "norm_ffn_g": 1.0 + nrm(20, (DEPTH, D_MODEL), 0.02),
        "w_up": nrm(21, (DEPTH, D_MODEL, 2 * D_FF), D_MODEL ** -0.5),
        "conv_w": nrm(22, (DEPTH, CONV_W, 2 * D_FF), CONV_W ** -0.5),
        "conv_b": nrm(23, (DEPTH, 2 * D_FF), 0.01),
        "w_down": nrm(24, (DEPTH, D_FF, D_MODEL), D_FF ** -0.5),
        "norm_final_g": 1.0 + nrm(25, (D_MODEL,), 0.02),
    }


def reference(x_prompt, x_sample, mem_prompt, state_ret, state_conv, cache_mem_k, cache_mem_v,
              norm_mix_g, w_in, b_gate, ret_gn_g, sg_ln_g, sg_ws, sg_bs, mem_norm_g, w_mem_kv,
              w_br_ret, w_br_sg, w_br_x, w_o, norm_ffn_g, w_up, conv_w, conv_b, w_down, norm_final_g):
    Bp, Lp, _ = x_prompt.shape
    Bs, Ls, _ = x_sample.shape
    pos_p = jnp.arange(Lp, dtype=jnp.int32)
    pos_s = PAST_LEN + jnp.arange(Ls, dtype=jnp.int32)
    xp, xs = x_prompt, x_sample
    ret_p, conv_p, mk_p_l, mv_p_l, ret_s, conv_s, sgv_s = [], [], [], [], [], [], []
    for l in range(DEPTH):
        mk_p, mv_p = _mem_kv(mem_prompt, mem_norm_g[l], w_mem_kv[l])
        S0 = jnp.zeros((Bp, RET_HEADS, RET_DK, RET_DV), F32)
        xp, Sp, _ = _mixer_block(xp, pos_p, S0, mk_p, mv_p, norm_mix_g[l], w_in[l], b_gate[l],
                                 ret_gn_g[l], sg_ln_g[l], sg_ws[l], sg_bs[l], w_br_ret[l],
                                 w_br_sg[l], w_br_x[l], w_o[l], min(RET_CHUNK, Lp))
        c0 = jnp.zeros((Bp, CONV_W - 1, 2 * D_FF), xp.dtype)
        xp, cp = _ffn_block(xp, c0, norm_ffn_g[l], w_up[l], conv_w[l], conv_b[l], w_down[l])
        xs, Ss, vs = _mixer_block(xs, pos_s, state_ret[l], cache_mem_k[l], cache_mem_v[l],
                                  norm_mix_g[l], w_in[l], b_gate[l], ret_gn_g[l], sg_ln_g[l],
                                  sg_ws[l], sg_bs[l], w_br_ret[l], w_br_sg[l], w_br_x[l], w_o[l], Ls)
        xs, cs = _ffn_block(xs, state_conv[l], norm_ffn_g[l], w_up[l], conv_w[l], conv_b[l], w_down[l])
        ret_p.append(Sp.astype(x_prompt.dtype))
        conv_p.append(cp)
        mk_p_l.append(mk_p)
        mv_p_l.append(mv_p)
        ret_s.append(Ss.astype(x_sample.dtype))
        conv_s.append(cs)
        sgv_s.append(vs)
    y_prompt = _rmsnorm(xp, norm_final_g)
    y_sample = _rmsnorm(xs, norm_final_g)
    return (y_prompt, y_sample, jnp.stack(ret_p), jnp.stack(conv_p), jnp.stack(mk_p_l),
            jnp.stack(mv_p_l), jnp.stack(ret_s), jnp.stack(conv_s), jnp.stack(sgv_s))
```

```python
import os
import numpy as np
from contextlib import ExitStack
import concourse.bass as bass
import concourse.mybir as mybir
from concourse.bass_utils import run_bass_kernel_spmd

F32 = mybir.dt.float32
BF16 = mybir.dt.bfloat16
AF = mybir.ActivationFunctionType
ALU = mybir.AluOpType
AX = mybir.AxisListType

NCORES = 8
D = 1024
KC = 8
SEQ = 2048
TS = 512
NST = SEQ // TS
NSAMP = 64
NB = 16
HCOLS = 2 + TS + NSAMP
DFF = 2816
NFC = 44
EPS = 1e-6
RING = 4
GAM = [1.0 - 2.0 ** (-5 - h) for h in range(4)]


class _Stop(Exception):
    pass


class Res:
    __slots__ = ("name", "lw", "rd", "dsem", "dcnt", "excl")

    def __init__(self, name, excl=False):
        self.name = name
        self.excl = excl
        self.lw = None
        self.rd = []
        self.dsem = None
        self.dcnt = 0


class Op:
    __slots__ = ("eng", "fn", "deps", "sig", "val", "dma_owner", "multi")

    def __init__(self, eng, fn):
        self.eng = eng
        self.fn = fn
        self.deps = []
        self.sig = False
        self.val = None
        self.dma_owner = None
        self.multi = False


class Prog:
    ENGS = ("pe", "act", "dve", "pool", "sp")

    def __init__(self, nc, stack):
        self.nc = nc
        self.stack = stack
        self.ops = {e: [] for e in self.ENGS}
        self.esem = {e: stack.enter_context(nc.semaphore("es_" + e)) for e in self.ENGS}
        self.store_owners = []

    def _collect(self, eng, r, w, is_dma):
        deps = []

        def add(tok, raw):
            if tok is None:
                return
            if tok[0] == "op":
                src = tok[1]
                if src.eng == eng and not is_dma:
                    if eng == "pe":
                        return
                src.sig = True
            deps.append(tok)

        for res in r:
            add(res.lw, True)
        for res in w:
            add(res.lw, False)
            for t in res.rd:
                add(t, False)
        return deps

    def _update(self, tok, r, w):
        for res in w:
            res.lw = tok
            res.rd = []
        for res in r:
            if tok[0] == "op":
                res.rd = [t for t in res.rd if not (t[0] == "op" and t[1].eng == tok[1].eng)]
            res.rd.append(tok)

    def op(self, eng, fn, r=(), w=(), multi=False):
        ex = [x for x in r if x.excl and x not in w]
        if ex:
            r = [x for x in r if not x.excl]
            w = list(w) + ex
        o = Op(eng, fn)
        o.multi = multi
        o.deps = self._collect(eng, r, w, False)
        self.ops[eng].append(o)
        self._update(("op", o), r, w)
        return o

    def dma(self, q, out, in_, owner, r=(), w=(), store=False):
        o = Op(q, lambda e: e.dma_start(out=out, in_=in_))
        o.deps = self._collect(q, r, w, True)
        if owner.dsem is None:
            owner.dsem = self.stack.enter_context(self.nc.semaphore("ds_" + owner.name))
        owner.dcnt += 1
        o.dma_owner = owner
        self.ops[q].append(o)
        self._update(("dma", owner, owner.dcnt), r, w)
        if store and owner not in self.store_owners:
            self.store_owners.append(owner)
        return o

    def dma_multi(self, q, pairs, owner, r=(), w=()):
        deps = self._collect(q, r, w, True)
        if owner.dsem is None:
            owner.dsem = self.stack.enter_context(self.nc.semaphore("ds_" + owner.name))
        for out, in_ in pairs:
            o = Op(q, lambda e, out=out, in_=in_: e.dma_start(out=out, in_=in_))
            o.deps = list(deps)
            owner.dcnt += 1
            o.dma_owner = owner
            self.ops[q].append(o)
        self._update(("dma", owner, owner.dcnt), r, w)

    def emit(self, block):
        for e in self.ENGS:
            c = 0
            for o in self.ops[e]:
                if o.sig:
                    c += 1
                    o.val = c
        prog = self

        def run(ename, e):
            waited = {}
            sem_self = prog.esem[ename]
            for o in prog.ops[ename]:
                need = {}
                for t in o.deps:
                    if t[0] == "op":
                        s, v = prog.esem[t[1].eng], t[1].val
                    else:
                        s, v = t[1].dsem, 16 * t[2]
                    k = id(s)
                    if waited.get(k, (None, 0))[1] >= v:
                        continue
                    if k not in need or need[k][1] < v:
                        need[k] = (s, v)
                items = list(need.values())
                for s, v in items:
                    waited[id(s)] = (s, v)
                if o.dma_owner is not None:
                    for s, v in items:
                        e.wait_ge(s, v)
                    o.fn(e).then_inc(o.dma_owner.dsem, 16)
                else:
                    inline = ename in ("dve", "act") and not o.multi and len(items) > 0
                    for s, v in (items[:-1] if inline else items):
                        e.wait_ge(s, v)
                    ins = o.fn(e)
                    if inline:
                        ins._wait_ge(items[-1][0], items[-1][1])
                    if o.sig:
                        ins.then_inc(sem_self, 1)
            if ename == "sp":
                for ow in prog.store_owners:
                    e.wait_ge(ow.dsem, 16 * ow.dcnt)

        @block.tensor
        def _(e):
            run("pe", e)

        @block.scalar
        def _(e):
            run("act", e)

        @block.vector
        def _(e):
            run("dve", e)

        @block.gpsimd
        def _(e):
            run("pool", e)

        @block.sync
        def _(e):
            run("sp", e)


def _const_tables():
    c = {}
    c["c_ident"] = np.eye(128, dtype=np.float32)
    inv = (np.float32(10000.0) ** (-np.arange(0, 128, 2, dtype=np.float32) / np.float32(128))).astype(np.float32)
    cs = np.zeros((128, 17, 2, 64), np.float32)
    for g in range(17):
        if g < 16:
            pos = (g * 128 + np.arange(128)).astype(np.float32)
        else:
            pos = (16384 + (np.arange(128) % 4)).astype(np.float32)
        ang = (pos[:, None] * inv[None, :]).astype(np.float32).astype(np.float64)
        cs[:, g, 0, :] = np.cos(ang)
        cs[:, g, 1, :] = np.sin(ang)
    c["c_cs"] = cs
    dec = np.zeros((128, 2, 8), np.float64)
    for v in range(2):
        n = np.arange(128) if v == 0 else (np.arange(128) % 4)
        for h in range(4):
            lg = np.log1p(-2.0 ** (-5 - h))
            dec[:, v, h] = np.exp((n + 1.0) * lg)
            dec[:, v, 4 + h] = np.exp(-(n + 1.0) * lg) * (128.0 ** -0.5)
    c["c_dec"] = dec.astype(np.float32)
    gc = np.zeros((128, 2, 4), np.float64)
    for h in range(4):
        lg = np.log1p(-2.0 ** (-5 - h))
        gc[:, 0, h] = np.exp(128.0 * lg)
        gc[:, 1, h] = np.exp(4.0 * lg)
    c["c_gc"] = gc.astype(np.float32)
    s = np.arange(128)
    c["c_maskT"] = (s[:, None] <= s[None, :]).astype(np.float32)
    ms = np.zeros((128, 64), np.float32)
    s64 = np.arange(64)
    ms[:64] = ((s64[:, None] <= s64[None, :]) & ((s64[:, None] // 4) == (s64[None, :] // 4))).astype(np.float32)
    c["c_maskS"] = ms
    bmf = np.zeros((128, 16, 64), np.float32)
    for b in range(16):
        bmf[:, b, 4 * b:4 * b + 4] = 1.0
    c["c_bmfree"] = bmf
    bmp = np.zeros((128, 16), np.float32)
    for b in range(16):
        bmp[4 * b:4 * b + 4, b] = 1.0
    c["c_bmpart"] = bmp
    return c


def _fm(v):
    return np.ascontiguousarray(v.reshape(-1, 128).T).astype(np.float32)


def _rep(v):
    return np.ascontiguousarray(np.broadcast_to(v.reshape(1, -1), (128, v.size))).astype(np.float32)


CONST_SPECS = [
    ("c_ident", [128, 128]), ("c_dec", [128, 2, 8]), ("c_gc", [128, 2, 4]),
    ("c_maskT", [128, 128]), ("c_maskS", [128, 64]), ("c_bmpart", [128, 16]),
    ("p_gmix", [128, 8]), ("p_gffn", [128, 8]), ("p_gmem", [128, 8]),
    ("p_gn", [128, 1024]), ("p_ln", [128, 512]), ("p_gfin", [128, 1024]), ("p_convw", [128, 44, 3]),
    ("p_convb", [128, 44]), ("p_sgb", [128, 2, 4]),
]
WEIGHT_SPECS = [("w_in", [1024, 7680]), ("w_mem_kv", [1024, 1024]), ("w_br_ret", [1024, 1024]),
                ("w_br_sg", [512, 1024]), ("w_br_x", [512, 1024]), ("w_o", [1024, 1024]),
                ("w_up", [1024, 5632]), ("w_down", [2816, 1024])]


def build_program():
    nc = bass.Bass("TRN2", target_bir_lowering=False)
    dr = {}

    def din(name, shape, dt=F32):
        dr[name] = nc.dram_tensor(name, shape, dt, kind="ExternalInput").ap()

    def dout(name, shape):
        dr[name] = nc.dram_tensor(name, shape, F32, kind="ExternalOutput").ap()

    din("x_p", [SEQ, D]); din("x_s", [NSAMP, D]); din("mem", [256, D])
    din("st_ret", [NB, 4, 128, 256]); din("st_conv", [32, 2 * DFF])
    din("ck", [NB, 256, 512]); din("cv", [NB, 256, 512])
    for n, s in CONST_SPECS + WEIGHT_SPECS:
        din(n, s)
    din("c_cs", [128, 17, 2, 64]); din("c_bmfree", [128, 16, 64]); din("p_bgate", [1, 3072]); din("p_wt", [128, 4, 128]); din("p_wrep", [128, 4, 64])
    dout("y_p", [SEQ, D]); dout("y_s", [NSAMP, D]); dout("o_ret_p", [4, 128, 256]); dout("o_conv_p", [2, 2 * DFF])
    dout("o_mk", [256, 512]); dout("o_mv", [256, 512]); dout("o_ret_s", [NB, 4, 128, 256])
    dout("o_conv_s", [32, 2 * DFF]); dout("o_sgv", [NSAMP, 512])
    DBG = os.environ.get("KDBG", "")
    if DBG:
        dout("dbg", [128, 5, 1024])
        dr["dbgT"] = nc.dram_tensor("dbgT", [128, 16, HCOLS], BF16, kind="ExternalOutput").ap()
        dr["dbgM"] = nc.dram_tensor("dbgM", [128, 5, 1024], BF16, kind="ExternalOutput").ap()
        dout("dbgZ", [128, NFC, 32]); dr["dbgH"] = nc.dram_tensor("dbgH", [128, 8, 64], BF16, kind="ExternalOutput").ap()

    with ExitStack() as stack:
        P = Prog(nc, stack)

        def sb(name, shape, dt=F32):
            return stack.enter_context(nc.sbuf_tensor("sb_" + name, shape, dt))

        cst = {}
        RC = Res("const")
        for n, s in CONST_SPECS:
            cst[n] = sb(n, s)
            P.dma("sp", cst[n][:], dr[n], RC)
        RC.lw = ("dma", RC, RC.dcnt)
        identb = sb("identb", [128, 128], BF16)
        bmfree = sb("bmfreeb", [128, 16, 64], BF16)
        wtb = sb("wtb", [128, 4, 128], BF16)
        wsb = sb("wsb", [128, 4, 64], BF16)
        RCP = Res("constp"); RC2 = Res("const2")
        P.dma("pool", bmfree[:], dr["c_bmfree"], RCP)
        bgrow = sb("bgrow", [1, 3072], BF16)
        P.dma("pool", bgrow[:], dr["p_bgate"], RCP)
        ones1 = sb("ones1", [1, 128], BF16)
        cs = sb("cs_cur", [128, 5, 2, 64]); RCS = Res("cs_cur")
        P.dma("pool", wtb[:], dr["p_wt"], RCP)
        P.dma("pool", wsb[:], dr["p_wrep"], RCP)
        RC2.lw = ("dma", RCP, RCP.dcnt)
        P.op("dve", lambda e: e.memset(ones1[:], 1.0), r=[RC2], w=[RC])
        P.op("act", lambda e: e.copy(out=identb[:], in_=cst["c_ident"][:]), r=[RC], w=[RC])
        P.op("dve", lambda e: e.tensor_tensor(out=wtb[:], in0=wtb[:], in1=cst["c_maskT"][:].unsqueeze(1).broadcast_to([128, 4, 128]), op=ALU.mult), r=[RC], w=[RC])
        P.op("dve", lambda e: e.tensor_tensor(out=wsb[:64], in0=wsb[:64], in1=cst["c_maskS"][:64].unsqueeze(1).broadcast_to([64, 4, 64]), op=ALU.mult), r=[RC], w=[RC])
        identf = cst["c_ident"]

        psum = stack.enter_context(nc.psum_tensor("psum", [128, 4096], F32))
        RB = [Res("bank%d" % b, excl=True) for b in range(8)]
        bank_ptr = [0]

        reserved = set()

        def banks(n=1, reserve=False):
            p = bank_ptr[0]
            for _ in range(16):
                if p + n > 8:
                    p = 0
                if not any((p + q) in reserved for q in range(n)):
                    break
                p += 1
            else:
                raise RuntimeError("no psum banks")
            b0 = p
            bank_ptr[0] = (b0 + n) % 8
            if reserve:
                reserved.update(range(b0, b0 + n))
            return b0, psum[:, 512 * b0:512 * (b0 + n)], RB[b0:b0 + n]

        def release(b0, n):
            for q in range(n):
                reserved.discard(b0 + q)

        slabs = [sb("slab%d" % i, [128, 8, 512], BF16) for i in range(RING)]
        RS = [Res("slab%d" % i) for i in range(RING)]
        sched = []

        def sched_st():
            for blk in range(9):
                sched.append(("w_in", 0, 8, blk * 512, 512))
            for j in range(2):
                sched.append(("w_br_ret", 0, 8, j * 512, 512))
                sched.append(("w_in", 0, 8, 4608 + j * 512, 512))
                sched.append(("w_br_sg", 0, 4, j * 512, 512))
                sched.append(("w_in", 0, 8, 4608 + 1024 + j * 512, 512))
                sched.append(("w_br_x", 0, 4, j * 512, 512))
                sched.append(("w_in", 0, 8, 4608 + 2048 + j * 512, 512))
            for j in range(2):
                sched.append(("w_o", 0, 8, j * 512, 512))
            for m in range(6):
                nc_ = 512 if m < 5 else 256
                sched.append(("w_up", 0, 8, m * 512, nc_))
                sched.append(("w_up", 0, 8, DFF + m * 512, nc_))
            for j in range(2):
                sched.append(("w_down", 0, 8, j * 512, 512))
                sched.append(("w_down", 1024, 8, j * 512, 512))
                sched.append(("w_down", 2048, 6, j * 512, 512))

        sched.append(("w_mem_kv", 0, 8, 0, 512))
        sched.append(("w_mem_kv", 0, 8, 512, 512))
        sched_st()
        NSL = len(sched) - 2
        sched_meta = [(-1, 0), (-1, 1)] + [(0, e) for e in range(NSL)]
        for st_ in range(1, NST):
            sched_st()
            sched_meta += [(st_, e) for e in range(NSL)]
        wscr = nc.dram_tensor("wscr", [NSL, 128, 8, 512], BF16, kind="Internal").ap()
        RSCR = [Res("scr%d" % e) for e in range(NSL)]
        RSS = [Res("slabst%d" % i) for i in range(RING)]
        RSH = [Res("slabhw%d" % i) for i in range(RING)]
        pending_store = []
        issued = [0]
        taken = [0]

        def issue_next():
            i = issued[0]
            if i >= len(sched):
                return
            wname, r0, nk, c0, ncols = sched[i]
            st_, e_ = sched_meta[i]
            slot = i % RING
            if st_ <= 0:
                pairs = []
                for k0 in range(0, nk, 4):
                    k1 = min(nk, k0 + 4)
                    src = dr[wname][r0 + k0 * 128:r0 + k1 * 128, c0:c0 + ncols].rearrange("(k p) c -> p k c", p=128)
                    pairs.append((slabs[slot][:, k0:k1, 0:ncols], src))
                P.dma_multi("pool", pairs, RS[slot], w=[RS[slot]])
                if st_ == 0:
                    pending_store.append((i, slot, e_, nk, ncols))
            else:
                if slot % 2 == 0:
                    P.dma("sp", slabs[slot][:, 0:nk, 0:ncols], wscr[e_][:, 0:nk, 0:ncols], RSH[slot], r=[RSCR[e_]], w=[RS[slot]])
                else:
                    P.dma("pool", slabs[slot][:, 0:nk, 0:ncols], wscr[e_][:, 0:nk, 0:ncols], RS[slot], r=[RSCR[e_]], w=[RS[slot]])
            issued[0] += 1
            while pending_store and pending_store[0][0] <= i - 2:
                _, sl_, e2, nk2, nc2 = pending_store.pop(0)
                P.dma("sp", wscr[e2][:, 0:nk2, 0:nc2], slabs[sl_][:, 0:nk2, 0:nc2], RSS[sl_], r=[RS[sl_]], w=[RSCR[e2]])

        def next_slab(expect, live=1):
            i = taken[0]
            assert sched[i][0] == expect, (sched[i], expect)
            while issued[0] < min(len(sched), i + RING - live + 1):
                issue_next()
            taken[0] += 1
            return slabs[i % RING], RS[i % RING]

        xres = sb("xres", [128, 5, D]); RX = [Res("xres%d" % i) for i in range(5)]
        hT = sb("hT", [128, 8, HCOLS], BF16); RH = [Res("hT%d" % i) for i in range(5)]; RHH = Res("hThalo")
        hsave = sb("hsave", [128, 8, 2], BF16); RHS = Res("hsave")
        bigB = sb("bigB", [128, 15, 1024], BF16)
        B4 = sb("B4", [128, 5, 512], BF16)
        RBB = [[Res("B%d_%d" % (k, i)) for i in range(5)] for k in range(8)]
        qkT = bigB[:, 0:5]
        vtm = bigB[:, 5:10]
        ggb = bigB[:, 10:15]
        ktm = B4
        kmT = sb("kmT", [128, 4, 256], BF16); vmb = sb("vmb", [128, 2, 512], BF16); RKM = Res("kmT"); RVM = Res("vmb")
        Sf = sb("Sf", [128, 4, 256]); Sbf = sb("Sbf", [128, 4, 256], BF16); RSF = Res("Sf"); RSB = Res("Sbf")
        xsb = [sb("xsb%d" % i, [128, D], BF16) for i in range(2)]; RXS = [Res("xsb%d" % i) for i in range(2)]
        st1 = [sb("st1_%d" % i, [128, 8]) for i in range(4)]; RST = [Res("st1_%d" % i) for i in range(4)]
        t1 = [sb("t1_%d" % i, [128, 512]) for i in range(2)]; RT1 = [Res("t1_%d" % i) for i in range(2)]
        t2 = [sb("t2_%d" % i, [128, 512]) for i in range(2)]; RT2 = [Res("t2_%d" % i) for i in range(2)]
        f32a = [sb("f32a%d" % i, [128, 1024]) for i in range(2)]; RFA = [Res("f32a%d" % i) for i in range(2)]
        bfa = [sb("bfa%d" % i, [128, 1024], BF16) for i in range(2)]; RBA = [Res("bfa%d" % i) for i in range(2)]
        bfb = [sb("bfb%d" % i, [128, 1024], BF16) for i in range(2)]; RBFB = [Res("bfb%d" % i) for i in range(2)]
        bns = [sb("bns%d" % i, [128, 4, 6]) for i in range(2)]; RBN = [Res("bns%d" % i) for i in range(2)]
        mvs = [sb("mvs%d" % i, [128, 4, 2]) for i in range(2)]; RMV = [Res("mvs%d" % i) for i in range(2)]
        sm4 = [sb("sm4_%d" % i, [128, 3, 4]) for i in range(2)]; RSM = [Res("sm4_%d" % i) for i in range(2)]
        stage = [sb("stage%d" % i, [128, 512]) for i in range(2)]; RSG = [Res("stage%d" % i) for i in range(2)]
        ctr = {"xin": 0, "st": 0, "t": 0, "fa": 0, "ba": 0, "bb": 0, "bn": 0, "sm": 0, "sg": 0, "sb": 0}

        held = set()

        def rot(key, n):
            v = ctr[key]
            for _ in range(n):
                if (key, v) not in held:
                    break
                v = (v + 1) % n
            else:
                raise RuntimeError("ring full: " + key)
            ctr[key] = (v + 1) % n
            return v

        def hold(key, v):
            held.add((key, v))

        def drop(key, v):
            held.discard((key, v))

        sbin = [sb("sbin%d" % i, [128, 4, 256]) for i in range(2)]; RSI = [Res("sbin%d" % i) for i in range(2)]
        sbbf = [sb("sbbf%d" % i, [128, 4, 256], BF16) for i in range(2)]; RSBB = [Res("sbbf%d" % i) for i in range(2)]
        _sa = [sbin[i][:].rearrange("p a b -> p (a b)").bitcast(BF16) for i in range(2)]
        ckb = [_sa[i][:, 0:1024].rearrange("p (t c) -> p t c", t=2) for i in range(2)]; RCK = RSI
        cvb = [_sa[i][:, 1024:2048].rearrange("p (t c) -> p t c", t=2) for i in range(2)]; RCV = RSI
        kcT = sbbf; RKT = RSBB
        tailbuf = sb("tailbuf", [128, 10, 512], BF16)
        usg = tailbuf[:, 0:5]; vnb = tailbuf[:, 5:10]
        _tf = tailbuf[:].rearrange("p a b -> p (a b)").bitcast(F32)
        prevT = _tf[:, 0:1408].rearrange("p (c n) -> p c n", c=NFC); RPV = Res("prevT")
        B567 = sb("B567", [128, 16, HCOLS], BF16)
        B5 = B567[:, 0:8]; B6 = B567[:, 8:12]; B7 = B567[:, 12:16]
        zkeep = B567[:, 8:16].rearrange("p a b -> p (a b)").bitcast(F32)[:, 0:1408].rearrange("p (c n) -> p c n", c=NFC); RZK = Res("zkeep")
        zfull = [sb("zfull%d" % i, [128, 16, 6]) for i in range(2)]; RZF = [Res("zfull%d" % i) for i in range(2)]
        ysm = [sb("ysm%d" % i, [128, 3, 64]) for i in range(2)]; RYS = [Res("ysm%d" % i) for i in range(2)]
        gas = [sb("gas%d" % i, [128, 64]) for i in range(2)]; RGS = [Res("gas%d" % i) for i in range(2)]
        gatedT = bigB[:].rearrange("p a b -> p (a b)")[:, 0:22 * 576].rearrange("p (f t) -> p f t", f=22)

        def transposes(src_fn, n, rows, dst_ap_fn, rsrc, rdst, dt=BF16, evac="act"):
            _, pb, rb = banks(1)
            if dt == BF16:
                pv = pb.bitcast(BF16)[:, 0:n * 128].rearrange("p (n r) -> p n r", n=n)
                ident = identb
            else:
                pv = pb[:, 0:n * 128].rearrange("p (n r) -> p n r", n=n)
                ident = identf
            for j in range(n):
                src = src_fn(j)
                P.op("pe", lambda e, o=pv[:, j, 0:rows], s=src, idn=ident[:rows, :rows]: e.transpose(o, s, idn),
                     r=list(rsrc) + [RC], w=rb)
            dst = dst_ap_fn()
            if evac == "act":
                P.op("act", lambda e, d=dst, s=pv[:, :, 0:rows]: e.copy(out=d, in_=s), r=rb, w=rdst)
            else:
                P.op("dve", lambda e, d=dst, s=pv[:, :, 0:rows]: e.tensor_copy(out=d, in_=s), r=rb, w=rdst)
            return pv, rb

        def rstd_from(ssum_ap, out_ap, rres, scale):
            P.op("dve", lambda e: e.tensor_scalar(out=out_ap, in0=ssum_ap, scalar1=scale, scalar2=EPS, op0=ALU.mult, op1=ALU.add), r=[rres], w=[rres])
            P.op("act", lambda e: e.sqrt(out=out_ap, in_=out_ap), r=[rres], w=[rres])
            P.op("dve", lambda e: e.reciprocal(out=out_ap, in_=out_ap), r=[rres], w=[rres])

        def rmsnorm_to_hT(src_ap, rsrc, rows, col0, gtab, dstT, rdst):
            k = rot("st", 4)
            st, rst = st1[k], RST[k]
            b = rot("sb", 2)
            P.op("act", lambda e: e.activation(out=xsb[b][:rows], in_=src_ap, func=AF.Square, accum_out=st[:rows, 0:1]), r=[rsrc], w=[RXS[b], rst], multi=True)
            rstd_from(st[:rows, 0:1], st[:rows, 1:2], rst, 1.0 / D)
            P.op("dve", lambda e: e.tensor_scalar(out=xsb[b][:rows], in0=src_ap, scalar1=st[:rows, 1:2], scalar2=None, op0=ALU.mult), r=[rsrc, rst], w=[RXS[b]])
            _, pb, rb = banks(1)
            pv = pb.bitcast(BF16).rearrange("p (n r) -> p n r", n=8)
            for j in range(8):
                P.op("pe", lambda e, o=pv[:, j, 0:rows], s=xsb[b][:rows, j * 128:(j + 1) * 128]: e.transpose(o, s, identb[:rows, :rows]), r=[RXS[b], RC], w=rb)
            P.op("dve", lambda e: e.tensor_tensor(out=dstT[:, :, col0:col0 + rows], in0=pv[:, :, 0:rows], in1=gtab[:].unsqueeze(2).broadcast_to([128, 8, rows]), op=ALU.mult), r=rb + [RC], w=[rdst])

        def proj(slab, rslab, nk, srcT, rsrc, col0, rows, ncols=512, scol=0):
            _, pb, rb = banks(1)
            for k in range(nk):
                P.op("pe", lambda e, k=k: e.matmul(pb[:rows, 0:ncols], srcT[:, k, col0:col0 + rows], slab[:, k, scol:scol + ncols], start=(k == 0), stop=(k == nk - 1)),
                     r=[rslab] + list(rsrc), w=rb)
            return pb, rb

        STOP = os.environ.get("KSTOP", "")

        def ckpt(name):
            if STOP == name:
                raise _Stop()

        try:
            ckpt('c0')
            hTm = B5
            RHM = Res("hTm")
            for mt in range(2):
                if mt == 1:
                    ckpt('m0')
                P.dma("sp", xres[:, mt, :], dr["mem"][mt * 128:(mt + 1) * 128, :], RX[mt], w=[RX[mt]])
                rmsnorm_to_hT(xres[:, mt, :], RX[mt], 128, mt * 128, cst["p_gmem"], hTm, RHM)
            ckpt('m1')
            for blk in range(2):
                if blk == 1:
                    ckpt('m2')
                slab, rsl = next_slab("w_mem_kv")
                for mt in range(2):
                    pb, rb = proj(slab, rsl, 8, hTm, [RHM], mt * 128, 128)
                    ckpt('m1a')
                    s = rot("sg", 2)
                    P.op("act", lambda e, s=s, pb=pb: e.copy(out=stage[s][:, 0:512], in_=pb[:, 0:512]), r=rb, w=[RSG[s]])
                    P.dma("sp", dr["o_mk" if blk == 0 else "o_mv"][mt * 128:(mt + 1) * 128, :], stage[s][:, 0:512], RSG[s], r=[RSG[s]], store=True)
                    ckpt('m1b')
                    if blk == 0:
                        a = rot("ba", 2)
                        P.op("dve", lambda e, a=a, pb=pb: e.tensor_copy(out=bfa[a][:, 0:512], in_=pb[:, 0:512]), r=rb, w=[RBA[a]])
                        ckpt('m1c')
                        transposes(lambda j, a=a: bfa[a][:, j * 128:(j + 1) * 128], 4, 128,
                                   lambda mt=mt: kmT[:, :, mt * 128:(mt + 1) * 128], [RBA[a]], [RKM])
                    else:
                        P.op("dve", lambda e, pb=pb, mt=mt: e.tensor_copy(out=vmb[:, mt, :], in_=pb[:, 0:512]), r=rb, w=[RVM])

            RSO = [Res("sbout%d" % i) for i in range(2)]
            RUS = [Res("usg%d" % i) for i in range(5)]
            RCKO = [Res("cko%d" % i) for i in range(2)]
            RVN = [Res("vnb%d" % i) for i in range(5)]
            xqs = sb("xqs", [128, 4, 64], BF16); RXQS = Res("xqs")
            zlast = sb("zlast", [128, NFC, 2]); RZL = Res("zlast")
            RKTM = RBB[4]; RQKT = RBB[1]; RV = RBB[2]; RGG = RBB[3]; RORT = RBB[5]; ROST = RBB[6]; ROXT = RBB[7]
            RMG = [Res("merged%d" % i) for i in range(5)]; RMT = [Res("mergedT%d" % i) for i in range(5)]
            RGT = [Res("gated%d" % f) for f in range(22)]
            merged = vtm
            mergedT = bigB[:, 0:5].rearrange("p a b -> p (a b)")[:, 0:8 * HCOLS].rearrange("p (k c) -> p k c", k=8)
            maskT = cst["c_maskT"]; maskS = cst["c_maskS"]; bmpart = cst["c_bmpart"]

            def tr(src_fn, n, sp_, sf, dst, rsrc, rdst, dt=BF16, evac="act"):
                _, pb, rb = banks(1)
                if dt == BF16:
                    pv = pb.bitcast(BF16)[:sf, 0:n * sp_].rearrange("p (n r) -> p n r", n=n)
                    ident = identb
                else:
                    pv = pb[:sf, 0:n * sp_].rearrange("p (n r) -> p n r", n=n)
                    ident = identf
                for j in range(n):
                    if dt == BF16:
                        P.op("pe", lambda e, o=pv[:, j, :], s=src_fn(j), idn=ident[:sp_, :sp_]: e.transpose(o, s, idn), r=list(rsrc) + [RC], w=rb)
                    else:
                        P.op("pe", lambda e, o=pv[:, j, :], s=src_fn(j), idn=ident[:sp_, :sp_]: e.matmul(o, s, idn, start=True, stop=True), r=list(rsrc) + [RC], w=rb)
                if evac == "act":
                    P.op("act", lambda e: e.copy(out=dst, in_=pv), r=rb, w=rdst)
                else:
                    P.op("dve", lambda e: e.tensor_copy(out=dst, in_=pv), r=rb, w=rdst)

            def rstd2(in_ap, out_ap, rin, rout, scale):
                P.op("dve", lambda e: e.tensor_scalar(out=out_ap, in0=in_ap, scalar1=scale, scalar2=EPS, op0=ALU.mult, op1=ALU.add), r=[rin], w=[rout])
                P.op("act", lambda e: e.sqrt(out=out_ap, in_=out_ap), r=[rout], w=[rout])
                P.op("dve", lambda e: e.reciprocal(out=out_ap, in_=out_ap), r=[rout], w=[rout])

            def run_pipe(gens, depth):
                pending = list(gens)
                active = []
                while pending or active:
                    if pending and len(active) < depth:
                        active.append(pending.pop(0))
                    for g_ in list(active):
                        try:
                            next(g_)
                        except StopIteration:
                            active.remove(g_)

            def ret_finish(i, t, po, ro, hstride, rel):
                r = t["rows"]; c0 = t["col0"]
                n = rot("bn", 2); hold("bn", n)
                for h in range(4):
                    P.op("dve", lambda e, h=h: e.bn_stats(out=bns[n][:r, h, :], in_=po[:r, h * hstride:h * hstride + 256]), r=ro, w=[RBN[n]])
                for h in range(4):
                    P.op("dve", lambda e, h=h: e.bn_aggr(out=mvs[n][:r, h, :], in_=bns[n][:r, h, :]), r=[RBN[n]], w=[RMV[n]])
                m = rot("sm", 2); hold("sm", m)
                rstd2(mvs[n][:r, :, 1], sm4[m][:r, 0, :], RMV[n], RSM[m], 1.0)
                yield
                b = rot("fa", 2); hold("fa", b)
                for h in range(4):
                    P.op("dve", lambda e, h=h: e.scalar_tensor_tensor(out=f32a[b][:r, h * 256:(h + 1) * 256], in0=po[:r, h * hstride:h * hstride + 256],
                         scalar=mvs[n][:r, h, 0:1], in1=ggb[:r, i, h * 256:(h + 1) * 256], op0=ALU.subtract, op1=ALU.mult), r=ro + [RMV[n], RGG[i]], w=[RFA[b]])
                release(*rel)
                drop("bn", n)
                yield
                c = rot("bb", 2); hold("bb", c)
                for h in range(4):
                    P.op("act", lambda e, h=h: e.activation(out=bfb[c][:r, h * 256:(h + 1) * 256], in_=f32a[b][:r, h * 256:(h + 1) * 256], func=AF.Copy,
                         scale=sm4[m][:r, 0, h:h + 1]), r=[RFA[b], RSM[m]], w=[RBFB[c]])
                drop("sm", m); drop("fa", b)
                yield
                tr(lambda j: bfb[c][:r, j * 128:(j + 1) * 128], 8, r, 128, B5[:, :, c0:c0 + r], [RBFB[c]], [RORT[i]], evac="dve")
                drop("bb", c)

            ckpt('pre')
            for st_i in range(NST):
                tiles = [dict(kind="p", rows=128, g=st_i * 4 + i, col0=2 + 128 * i, tok0=st_i * TS + 128 * i) for i in range(4)]
                if st_i == 0:
                    tiles.append(dict(kind="s", rows=NSAMP, g=16, col0=2 + TS, tok0=0))
                NT = len(tiles)
                last_st = (st_i == NST - 1)

                if st_i == 0:
                    P.dma("sp", cs[:, 0:4], dr["c_cs"][:, 0:4], RCS, w=[RCS])
                    P.dma("sp", cs[:, 4:5], dr["c_cs"][:, 16:17], RCS, w=[RCS])
                for i, t in enumerate(tiles):
                    r = t["rows"]
                    src = dr["x_p"][t["tok0"]:t["tok0"] + 128, :] if t["kind"] == "p" else dr["x_s"][:, :]
                    if st_i == 0:
                        P.dma("sp", xres[:r, i, :], src, RX[i], w=[RX[i]])
                    rmsnorm_to_hT(xres[:r, i, :], RX[i], r, t["col0"], cst["p_gmix"], hT, RH[i])

                if st_i == 0: ckpt('p0')
                for blk in range(2):
                    slab, rsl = next_slab("w_in")
                    for i, t in enumerate(tiles):
                        r = t["rows"]; v = 0 if t["kind"] == "p" else 1
                        pb, rb = proj(slab, rsl, 8, hT, [RH[i]], t["col0"], r)
                        psv = pb[:r].rearrange("p (h t d) -> p h t d", h=4, t=2)
                        k1 = rot("t", 2)
                        a1 = t1[k1][:r].rearrange("p (h t d) -> p h t d", h=4, t=2)
                        a2 = t2[k1][:r].rearrange("p (h t d) -> p h t d", h=4, t=2)
                        cosb = cs[:r, i, 0, :].unsqueeze(1).unsqueeze(1).broadcast_to([r, 4, 2, 64])
                        sinb = cs[:r, i, 1, :].unsqueeze(1).broadcast_to([r, 4, 64])
                        P.op("dve", lambda e, a1=a1, psv=psv, cosb=cosb: e.tensor_tensor(out=a1, in0=psv, in1=cosb, op=ALU.mult), r=rb + [RCS], w=[RT1[k1]])
                        P.op("dve", lambda e, a2=a2, psv=psv, sinb=sinb: e.scalar_tensor_tensor(out=a2[:, :, 0, :], in0=psv[:, :, 1, :], scalar=-1.0, in1=sinb, op0=ALU.mult, op1=ALU.mult), r=rb + [RCS], w=[RT2[k1]])
                        P.op("dve", lambda e, a2=a2, psv=psv, sinb=sinb: e.tensor_tensor(out=a2[:, :, 1, :], in0=psv[:, :, 0, :], in1=sinb, op=ALU.mult), r=rb + [RCS], w=[RT2[k1]])
                        P.op("dve", lambda e, k1=k1, r=r: e.tensor_tensor(out=t1[k1][:r], in0=t1[k1][:r], in1=t2[k1][:r], op=ALU.add), r=[RT1[k1], RT2[k1]], w=[RT1[k1]])
                        decb = cst["c_dec"][:r, v, blk * 4:(blk + 1) * 4].unsqueeze(2).broadcast_to([r, 4, 128])
                        if blk == 0:
                            dsta = usg[:r, i, :]; rd = RUS[i]
                        else:
                            dsta = ktm[:r, i, :]; rd = RKTM[i]
                        P.op("dve", lambda e, k1=k1, r=r, decb=decb, dsta=dsta: e.tensor_tensor(
                            out=dsta.rearrange("p (h d) -> p h d", h=4),
                            in0=t1[k1][:r].rearrange("p (h d) -> p h d", h=4), in1=decb, op=ALU.mult), r=[RT1[k1], RC], w=[rd])
                for blk in range(2):
                    slab, rsl = next_slab("w_in")
                    for i, t in enumerate(tiles):
                        r = t["rows"]
                        pb, rb = proj(slab, rsl, 8, hT, [RH[i]], t["col0"], r)
                        P.op("act", lambda e, pb=pb, r=r, i=i, blk=blk: e.copy(out=vtm[:r, i, blk * 512:(blk + 1) * 512], in_=pb[:r, 0:512]), r=rb, w=[RV[i]])
                for blk in range(2):
                    slab, rsl = next_slab("w_in")
                    for i, t in enumerate(tiles):
                        r = t["rows"]
                        pb, rb = proj(slab, rsl, 8, hT, [RH[i]], t["col0"], r)
                        k1 = rot("t", 2)
                        P.op("act", lambda e, pb=pb, r=r, k1=k1: e.activation(out=t1[k1][:r], in_=pb[:r, 0:512], func=AF.Silu), r=rb, w=[RT1[k1]])
                        P.op("dve", lambda e, r=r, k1=k1, i=i, blk=blk: e.tensor_tensor(out=ggb[:r, i, blk * 512:(blk + 1) * 512], in0=t1[k1][:r],
                             in1=cst["p_gn"][:r, blk * 512:(blk + 1) * 512], op=ALU.mult), r=[RT1[k1], RC], w=[RGG[i]])
                if st_i == 0:
                    P.op("dve", lambda e: e.memset(Sf[:], 0.0), w=[RSF])
                def ret_gen(i, t):
                    r = t["rows"]
                    tr(lambda j, i=i, r=r: (usg if j < 4 else ktm)[:r, i, (j % 4) * 128:(j % 4 + 1) * 128], 8, r, 128,
                       qkT[:, i, :].rearrange("p (n c) -> p n c", n=8)[:, :, 0:r], [RUS[i], RKTM[i]], [RQKT[i]])
                    yield
                    qT = lambda h, i=i, r=r: qkT[:, i, h * 128:h * 128 + r]
                    kT = lambda h, i=i, r=r: qkT[:, i, (4 + h) * 128:(4 + h) * 128 + r]
                    _, pb, rb = banks(1)
                    for h in range(4):
                        P.op("pe", lambda e, h=h, pb=pb, r=r, qT=qT, kT=kT: e.matmul(pb[:r, h * r:(h + 1) * r], kT(h), qT(h), start=True, stop=True), r=[RQKT[i]], w=rb)
                    a = rot("ba", 2); hold("ba", a)
                    msk = (maskT[:r, :r] if t["kind"] == "p" else maskS[:r, :r]).unsqueeze(1).broadcast_to([r, 4, r])
                    P.op("dve", lambda e, a=a, pb=pb, r=r, msk=msk: e.tensor_tensor(out=bfa[a][:r, 0:4 * r].rearrange("p (h c) -> p h c", h=4),
                         in0=pb[:r, 0:4 * r].rearrange("p (h c) -> p h c", h=4), in1=msk, op=ALU.mult), r=rb + [RC], w=[RBA[a]])
                    yield
                    if t["kind"] == "p":
                        first = (t["g"] == 0)
                        pob0, po, ro = banks(2, reserve=True)
                        for h in range(4):
                            P.op("pe", lambda e, h=h, po=po, a=a, i=i, first=first: e.matmul(po[:, h * 256:(h + 1) * 256], bfa[a][:, h * 128:(h + 1) * 128],
                                 vtm[:, i, h * 256:(h + 1) * 256], start=True, stop=first), r=[RBA[a], RV[i]], w=ro)
                            if not first:
                                P.op("pe", lambda e, h=h, po=po, qT=qT: e.matmul(po[:, h * 256:(h + 1) * 256], qT(h), Sbf[:, h, :], start=False, stop=True), r=[RQKT[i], RSB], w=ro)
                        drop("ba", a)
                        _, pS, rS = banks(2)
                        for h in range(4):
                            P.op("pe", lambda e, h=h, pS=pS, i=i: e.matmul(pS[:, h * 256:(h + 1) * 256], ktm[:, i, h * 128:(h + 1) * 128], vtm[:, i, h * 256:(h + 1) * 256],
                                 start=True, stop=True), r=[RKTM[i], RV[i]], w=rS)
                        gcb = cst["c_gc"][:, 0, :].unsqueeze(2).broadcast_to([128, 4, 256])
                        P.op("dve", lambda e, pS=pS: e.tensor_tensor(out=Sf[:], in0=pS.rearrange("p (h c) -> p h c", h=4), in1=Sf[:], op=ALU.add), r=rS + [RSF], w=[RSF])
                        P.op("dve", lambda e, gcb=gcb: e.tensor_tensor(out=Sf[:], in0=Sf[:], in1=gcb, op=ALU.mult), r=[RSF, RC], w=[RSF])
                        if not (last_st and i == 3):
                            P.op("act", lambda e: e.copy(out=Sbf[:], in_=Sf[:]), r=[RSF], w=[RSB])
                        else:
                            P.dma("sp", dr["o_ret_p"].rearrange("h d e -> d h e"), Sf[:], RSF, r=[RSF], store=True)
                        yield
                        yield from ret_finish(i, t, po, ro, 256, (pob0, 2))
                    else:
                        pob0, po, ro = banks(4, reserve=True)
                        for h in range(4):
                            P.op("pe", lambda e, h=h, po=po, a=a, i=i: e.matmul(po[:64, h * 512:h * 512 + 256], bfa[a][:64, h * 64:(h + 1) * 64],
                                 vtm[:64, i, h * 256:(h + 1) * 256], start=True, stop=False, skip_group_check=True), r=[RBA[a], RV[i]], w=ro)
                        drop("ba", a)
                        gcs = cst["c_gc"][:, 1, :].unsqueeze(2).broadcast_to([128, 4, 256])
                        sbout = [B567[:, 8 + 4 * q:12 + 4 * q].rearrange("p a b -> p (a b)").bitcast(F32)[:, 0:1024].rearrange("p (h c) -> p h c", h=4) for q in range(2)]
                        RSOA = [ROST, ROXT]
                        P.dma("sp", sbin[0][:], dr["st_ret"][0].rearrange("h d e -> d h e"), RSI[0], w=[RSI[0]])
                        for b in range(NB):
                            k = b % 2
                            if b + 1 < NB:
                                P.dma("sp", sbin[1 - k][:], dr["st_ret"][b + 1].rearrange("h d e -> d h e"), RSI[1 - k], w=[RSI[1 - k]])
                            P.op("act", lambda e, k=k: e.copy(out=sbbf[k][:], in_=sbin[k][:]), r=[RSI[k]], w=[RSBB[k]])
                            c = rot("bb", 2)
                            P.op("dve", lambda e, c=c, b=b, i=i: e.tensor_tensor(out=bfb[c][:, 0:256].rearrange("p (h c) -> p h c", h=4),
                                 in0=qkT[:, i, 0:512].rearrange("p (h c) -> p h c", h=4)[:, :, 0:64], in1=bmfree[:, b, :].unsqueeze(1).broadcast_to([128, 4, 64]), op=ALU.mult),
                                 r=[RQKT[i], RC], w=[RBFB[c]])
                            for h in range(4):
                                P.op("pe", lambda e, h=h, po=po, c=c, k=k, b=b: e.matmul(po[:64, h * 512:h * 512 + 256], bfb[c][:, h * 64:(h + 1) * 64], sbbf[k][:, h, :],
                                     start=False, stop=(b == NB - 1), skip_group_check=True), r=[RBFB[c], RSBB[k]], w=ro)
                            a2 = rot("ba", 2)
                            P.op("dve", lambda e, a2=a2, b=b, i=i: e.tensor_scalar(out=bfa[a2][:64, 0:512], in0=ktm[:64, i, :], scalar1=bmpart[:64, b:b + 1], scalar2=None, op0=ALU.mult),
                                 r=[RKTM[i], RC], w=[RBA[a2]])
                            _, pS, rS = banks(2)
                            for h in range(4):
                                P.op("pe", lambda e, h=h, pS=pS, a2=a2, i=i: e.matmul(pS[:, h * 256:(h + 1) * 256], bfa[a2][:64, h * 128:(h + 1) * 128], vtm[:64, i, h * 256:(h + 1) * 256],
                                     start=True, stop=True), r=[RBA[a2], RV[i]], w=rS)
                            P.op("dve", lambda e, pS=pS, k=k, sbout=sbout: e.tensor_tensor(out=sbout[k], in0=pS.rearrange("p (h c) -> p h c", h=4), in1=sbin[k][:], op=ALU.add), r=rS + [RSI[k]], w=RSOA[k])
                            P.op("dve", lambda e, k=k, gcs=gcs, sbout=sbout: e.tensor_tensor(out=sbout[k], in0=sbout[k], in1=gcs, op=ALU.mult), r=RSOA[k] + [RC], w=RSOA[k])
                            P.dma("sp", dr["o_ret_s"][b].rearrange("h d e -> d h e"), sbout[k], RSO[k], r=RSOA[k], store=True)
                            yield
                        yield from ret_finish(i, t, po, ro, 512, (pob0, 4))

                order = [(i, t) for i, t in enumerate(tiles) if t["kind"] == "s"] + [(i, t) for i, t in enumerate(tiles) if t["kind"] == "p"]
                run_pipe([ret_gen(i, t) for i, t in order], 2)

                if st_i == 0: ckpt('R')
                slab, rsl = next_slab("w_in")
                for i, t in enumerate(tiles):
                    r = t["rows"]
                    pb, rb = proj(slab, rsl, 8, hT, [RH[i]], t["col0"], r)
                    P.op("act", lambda e, pb=pb, r=r, i=i: e.activation(out=usg[:r, i, :], in_=pb[:r, 0:512], func=AF.Gelu_apprx_tanh), r=rb, w=[RUS[i]])
                slab, rsl = next_slab("w_in")

                def sgu_gen(i, t, slab=slab, rsl=rsl):
                    r = t["rows"]; c0 = t["col0"]
                    pb, rb = proj(slab, rsl, 8, hT, [RH[i]], t["col0"], r)
                    k1 = rot("t", 2); hold("t", k1)
                    P.op("act", lambda e, pb=pb, r=r, k1=k1: e.activation(out=t1[k1][:r], in_=pb[:r, 0:512], func=AF.Gelu_apprx_tanh), r=rb, w=[RT1[k1]])
                    yield
                    n = rot("bn", 2)
                    P.op("dve", lambda e, n=n, r=r, k1=k1: e.bn_stats(out=bns[n][:r, 0, :], in_=t1[k1][:r]), r=[RT1[k1]], w=[RBN[n]])
                    P.op("dve", lambda e, n=n, r=r: e.bn_aggr(out=mvs[n][:r, 0, :], in_=bns[n][:r, 0, :]), r=[RBN[n]], w=[RMV[n]])
                    m = rot("sm", 2)
                    rstd2(mvs[n][:r, 0, 1:2], sm4[m][:r, 0, 0:1], RMV[n], RSM[m], 1.0)
                    P.op("dve", lambda e, n=n, m=m, r=r, k1=k1: e.tensor_scalar(out=t2[k1][:r], in0=t1[k1][:r], scalar1=mvs[n][:r, 0, 0:1], scalar2=sm4[m][:r, 0, 0:1],
                         op0=ALU.subtract, op1=ALU.mult), r=[RT1[k1], RMV[n], RSM[m]], w=[RT2[k1]])
                    yield
                    P.op("dve", lambda e, r=r, k1=k1, i=i: e.tensor_tensor(out=vnb[:r, i, :], in0=t2[k1][:r], in1=cst["p_ln"][:r], op=ALU.mult), r=[RT2[k1], RC], w=[RVN[i]])
                    if t["kind"] == "s":
                        s = rot("sg", 2)
                        P.op("dve", lambda e, r=r, k1=k1, s=s: e.tensor_tensor(out=stage[s][:r, 0:512], in0=t2[k1][:r], in1=cst["p_ln"][:r], op=ALU.mult), r=[RT2[k1], RC], w=[RSG[s]])
                        P.dma("sp", dr["o_sgv"], stage[s][:r, 0:512], RSG[s], r=[RSG[s]], store=True)
                    drop("t", k1)
                    pmb0, pm, rm = banks(1, reserve=True)
                    for g in range(4):
                        lw = wtb[:, g, :] if t["kind"] == "p" else wsb[:64, g, :]
                        P.op("pe", lambda e, g=g, pm=pm, lw=lw, r=r, i=i: e.matmul(pm[:r, g * 128:(g + 1) * 128], lw, vnb[:r, i, g * 128:(g + 1) * 128], start=True, stop=True),
                             r=[RC, RVN[i]], w=rm)
                    yield
                    a = rot("ba", 2); hold("ba", a)
                    vv = 0 if t["kind"] == "p" else 1
                    for g in range(4):
                        P.op("dve", lambda e, g=g, pm=pm, r=r, i=i, a=a, vv=vv: e.scalar_tensor_tensor(out=bfa[a][:r, g * 128:(g + 1) * 128], in0=pm[:r, g * 128:(g + 1) * 128],
                             scalar=cst["p_sgb"][:r, vv, g:g + 1], in1=usg[:r, i, g * 128:(g + 1) * 128], op0=ALU.add, op1=ALU.mult), r=rm + [RC, RUS[i]], w=[RBA[a]])
                    release(pmb0, 1)
                    yield
                    tr(lambda j, a=a, r=r: bfa[a][:r, j * 128:(j + 1) * 128], 4, r, 128, B6[:, :, c0:c0 + r], [RBA[a]], [ROST[i]])
                    drop("ba", a)


                if st_i == 0: ckpt('S')
                slab, rsl = next_slab("w_in", live=2)

                def xat_gen(i, t, slab=slab, rsl=rsl):
                    r = t["rows"]; c0 = t["col0"]
                    pb, rb = proj(slab, rsl, 8, hT, [RH[i]], t["col0"], r)
                    a = rot("ba", 2)
                    P.op("act", lambda e, pb=pb, r=r, a=a: e.activation(out=bfa[a][:r, 0:512], in_=pb[:r, 0:512], func=AF.Copy, scale=float(128.0 ** -0.5)), r=rb, w=[RBA[a]])
                    yield
                    if t["kind"] == "p":
                        xb = rot("sb", 2); hold("sb", xb)
                        xql = xsb[xb][:, 0:512].rearrange("p (h c) -> p h c", h=4); rxq = RXS[xb]
                    else:
                        xql = xqs[:]; rxq = RXQS
                    tr(lambda j, a=a, r=r: bfa[a][:r, j * 128:(j + 1) * 128], 4, r, 128, xql, [RBA[a]], [rxq])
                    yield
                    if t["kind"] == "p":
                        psb0, ps_, rs_ = banks(2, reserve=True)
                        hs = 256
                        for h in range(4):
                            P.op("pe", lambda e, h=h, ps_=ps_, c0=c0: e.matmul(ps_[:, h * 256:(h + 1) * 256], xql[:, h, :], kmT[:, h, :], start=True, stop=True), r=[rxq, RKM], w=rs_)
                    else:
                        psb0, ps_, rs_ = banks(4, reserve=True)
                        hs = 512
                        for b in range(NB):
                            k = b % 2
                            P.dma("pool", ckb[k], dr["ck"][b].rearrange("(t p) c -> p t c", p=128), RCKO[k], w=[RCK[k]])
                            tr(lambda j, k=k: ckb[k][:, j % 2, (j // 2) * 128:(j // 2 + 1) * 128], 8, 128, 128, kcT[k][:].rearrange("p h (t m) -> p (h t) m", t=2), [RCK[k]], [RKT[k]])
                            c = rot("bb", 2)
                            P.op("dve", lambda e, c=c, b=b, c0=c0: e.tensor_tensor(out=bfb[c][:, 0:256].rearrange("p (h c) -> p h c", h=4), in0=xql,
                                 in1=bmfree[:, b, :].unsqueeze(1).broadcast_to([128, 4, 64]), op=ALU.mult), r=[rxq, RC], w=[RBFB[c]])
                            for h in range(4):
                                P.op("pe", lambda e, h=h, ps_=ps_, c=c, k=k, b=b: e.matmul(ps_[:64, h * 512:h * 512 + 256], bfb[c][:, h * 64:(h + 1) * 64], kcT[k][:, h, :],
                                     start=(b == 0), stop=(b == NB - 1), skip_group_check=True), r=[RBFB[c], RKT[k]], w=rs_)
                            yield
                    if t["kind"] == "p":
                        drop("sb", xb)
                    yield
                    m = rot("sm", 2); hold("sm", m)
                    psv = ps_[:r].rearrange("p (h c) -> p h c", h=4)[:, :, 0:256] if hs == 512 else ps_[:r].rearrange("p (h c) -> p h c", h=4)
                    P.op("dve", lambda e, m=m, psv=psv, r=r: e.tensor_reduce(out=sm4[m][:r, 0, :], in_=psv, axis=AX.X, op=ALU.max), r=rs_, w=[RSM[m]])
                    P.op("dve", lambda e, m=m, r=r: e.tensor_scalar(out=sm4[m][:r, 1, :], in0=sm4[m][:r, 0, :], scalar1=-1.0, scalar2=None, op0=ALU.mult), r=[RSM[m]], w=[RSM[m]])
                    c = rot("bb", 2); hold("bb", c)
                    for h in range(4):
                        P.op("act", lambda e, h=h, m=m, c=c, r=r, ps_=ps_, hs=hs: e.activation(out=bfb[c][:r, h * 256:(h + 1) * 256], in_=ps_[:r, h * hs:h * hs + 256], func=AF.Exp,
                             bias=sm4[m][:r, 1, h:h + 1], accum_out=sm4[m][:r, 2, h:h + 1]), r=rs_ + [RSM[m]], w=[RBFB[c], RSM[m]], multi=True)
                    release(psb0, 2 if t["kind"] == "p" else 4)
                    P.op("dve", lambda e, m=m, r=r: e.reciprocal(out=sm4[m][:r, 0, :], in_=sm4[m][:r, 2, :]), r=[RSM[m]], w=[RSM[m]])
                    yield
                    a = rot("ba", 2); hold("ba", a)
                    tr(lambda j, c=c, r=r: bfb[c][:r, j * 128:(j + 1) * 128], 8, r, 128, bfa[a][:, 0:8 * r].rearrange("p (n c) -> p n c", n=8), [RBFB[c]], [RBA[a]])
                    drop("bb", c)
                    yield
                    if t["kind"] == "p":
                        poxb0, pox, rox = banks(1, reserve=True)
                        for h in range(4):
                            for hf in range(2):
                                P.op("pe", lambda e, h=h, hf=hf, pox=pox, a=a: e.matmul(pox[:, h * 128:(h + 1) * 128], bfa[a][:, (h * 2 + hf) * 128:(h * 2 + hf + 1) * 128],
                                     vmb[:, hf, h * 128:(h + 1) * 128], start=(hf == 0), stop=(hf == 1)), r=[RBA[a], RVM], w=rox)
                        poxv = pox.rearrange("p (h c) -> p h c", h=4)
                    else:
                        poxb0, pox, rox = banks(4, reserve=True)
                        for b in range(NB):
                            k = b % 2
                            P.dma("pool", cvb[k], dr["cv"][b].rearrange("(t p) c -> p t c", p=128), RCKO[k], w=[RCV[k]])
                            c2 = rot("bb", 2)
                            P.op("dve", lambda e, c2=c2, a=a, b=b: e.tensor_tensor(out=bfb[c2][:, 0:512].rearrange("p (n c) -> p n c", n=8), in0=bfa[a][:, 0:512].rearrange("p (n c) -> p n c", n=8),
                                 in1=bmfree[:, b, :].unsqueeze(1).broadcast_to([128, 8, 64]), op=ALU.mult), r=[RBA[a], RC], w=[RBFB[c2]])
                            for h in range(4):
                                for hf in range(2):
                                    P.op("pe", lambda e, h=h, hf=hf, pox=pox, c2=c2, k=k, b=b: e.matmul(pox[:64, h * 512:h * 512 + 128], bfb[c2][:, (h * 2 + hf) * 64:(h * 2 + hf + 1) * 64],
                                         cvb[k][:, hf, h * 128:(h + 1) * 128], start=(b == 0 and hf == 0), stop=(b == NB - 1 and hf == 1), skip_group_check=True), r=[RBFB[c2], RCV[k]], w=rox)
                            yield
                        poxv = pox[:64].rearrange("p (h c) -> p h c", h=4)[:, :, 0:128]
                    drop("ba", a)
                    yield
                    a3 = rot("ba", 2); hold("ba", a3)
                    P.op("dve", lambda e, a3=a3, poxv=poxv, m=m, r=r: e.tensor_tensor(out=bfa[a3][:r, 0:512].rearrange("p (h c) -> p h c", h=4), in0=poxv[:r],
                         in1=sm4[m][:r, 0, :].unsqueeze(2).broadcast_to([r, 4, 128]), op=ALU.mult), r=rox + [RSM[m]], w=[RBA[a3]])
                    release(poxb0, 1 if t["kind"] == "p" else 4)
                    drop("sm", m)
                    yield
                    tr(lambda j, a3=a3, r=r: bfa[a3][:r, j * 128:(j + 1) * 128], 4, r, 128, B7[:, :, c0:c0 + r], [RBA[a3]], [ROXT[i]])
                    drop("ba", a3)

                xorder = [(i, t) for i, t in enumerate(tiles) if t["kind"] == "s"] + [(i, t) for i, t in enumerate(tiles) if t["kind"] == "p"]
                run_pipe([sgu_gen(i, t) for i, t in enumerate(tiles)], 2)
                run_pipe([xat_gen(i, t) for i, t in xorder], 2)

                if st_i == 0: ckpt('X')
                if DBG and st_i == 0:
                    P.dma("sp", dr["dbgT"][:, :, 2:HCOLS], B567[:, :, 2:HCOLS], Res("dbgT"), r=RORT + ROST + ROXT + [RZK], store=True)
                accs = [(f32a[0][:, 0:512], RFA[0]), (f32a[0][:, 512:1024], RFA[0]), (f32a[1][:, 0:512], RFA[1]), (f32a[1][:, 512:1024], RFA[1]), (stage[0][:, 0:512], RSG[0])]
                srcs = [(B5, RORT, 8), (B6, ROST, 4), (B7, ROXT, 4)]
                for j in range(2):
                    for br in range(3):
                        sT, rsT, nk = srcs[br]
                        pslab, prs = next_slab(("w_br_ret", "w_br_sg", "w_br_x")[br])
                        gsl, grs = next_slab("w_in", live=2)
                        for i, t in enumerate(tiles):
                            r = t["rows"]; c0 = t["col0"]
                            acc, racc = accs[i]
                            pP, rP = proj(pslab, prs, nk, sT, [rsT[i]], c0, r)
                            _, pG, rG = banks(1)
                            P.op("pe", lambda e, pG=pG, r=r, br=br, j=j: e.matmul(pG[:r, 0:512], ones1[0:1, 0:r], bgrow[0:1, br * 1024 + j * 512:br * 1024 + (j + 1) * 512], start=True, stop=False), r=[RC], w=rG)
                            for k in range(8):
                                P.op("pe", lambda e, k=k, pG=pG, r=r, gsl=gsl, c0=c0: e.matmul(pG[:r, 0:512], hT[:, k, c0:c0 + r], gsl[:, k, :], start=False, stop=(k == 7)), r=[grs, RH[i]], w=rG)
                            k1 = rot("t", 2)
                            P.op("act", lambda e, pG=pG, r=r, k1=k1: e.activation(out=t1[k1][:r], in_=pG[:r, 0:512], func=AF.Sigmoid), r=rG, w=[RT1[k1]])
                            if br == 0:
                                P.op("dve", lambda e, pP=pP, r=r, k1=k1, acc=acc: e.tensor_tensor(out=acc[:r], in0=pP[:r, 0:512], in1=t1[k1][:r], op=ALU.mult), r=rP + [RT1[k1]], w=[racc])
                            else:
                                P.op("dve", lambda e, pP=pP, r=r, k1=k1: e.tensor_tensor(out=t2[k1][:r], in0=pP[:r, 0:512], in1=t1[k1][:r], op=ALU.mult), r=rP + [RT1[k1]], w=[RT2[k1]])
                                if br == 1:
                                    P.op("dve", lambda e, r=r, k1=k1, acc=acc: e.tensor_tensor(out=acc[:r], in0=acc[:r], in1=t2[k1][:r], op=ALU.add), r=[racc, RT2[k1]], w=[racc])
                                else:
                                    P.op("dve", lambda e, r=r, k1=k1, acc=acc, i=i, j=j: e.tensor_tensor(out=merged[:r, i, j * 512:(j + 1) * 512], in0=acc[:r], in1=t2[k1][:r], op=ALU.add),
                                         r=[racc, RT2[k1]], w=[RMG[i]])
                for i, t in enumerate(tiles):
                    r = t["rows"]; c0 = t["col0"]
                    tr(lambda jj, i=i, r=r: merged[:r, i, jj * 128:(jj + 1) * 128], 8, r, 128, mergedT[:, :, c0:c0 + r], [RMG[i]], [RMT[i]])

                if st_i == 0: ckpt('G')
                if DBG and st_i == 0:
                    for i in range(5):
                        rr = 128 if i < 4 else 64
                        P.dma("sp", dr["dbgM"][:rr, i, :], merged[:rr, i, :], Res("dbgM%d" % i), r=[RMG[i]] + RGT, store=True)
                for j in range(2):
                    slab, rsl = next_slab("w_o")
                    for i, t in enumerate(tiles):
                        r = t["rows"]
                        pb, rb = proj(slab, rsl, 8, mergedT, [RMT[i]], t["col0"], r)
                        P.op("dve", lambda e, pb=pb, r=r, i=i, j=j: e.tensor_tensor(out=xres[:r, i, j * 512:(j + 1) * 512], in0=pb[:r, 0:512], in1=xres[:r, i, j * 512:(j + 1) * 512], op=ALU.add),
                             r=rb + [RX[i]], w=[RX[i]])

                if DBG == "x1" and st_i == 0:
                    for i in range(5):
                        rr = 128 if i < 4 else 64
                        P.dma("sp", dr["dbg"][:rr, i, :], xres[:rr, i, :], Res("dbg%d" % i), r=[RX[i]], store=True)
                if st_i == 0: ckpt('O')
                for i, t in enumerate(tiles):
                    rmsnorm_to_hT(xres[:t["rows"], i, :], RX[i], t["rows"], t["col0"], cst["p_gffn"], hT, RH[i])
                if st_i == 0:
                    P.op("dve", lambda e: e.memset(hT[:, :, 0:2], 0.0), w=[RHH])
                else:
                    P.op("dve", lambda e: e.tensor_copy(out=hT[:, :, 0:2], in_=hsave[:]), r=[RHS], w=[RHH])
                P.op("dve", lambda e: e.tensor_copy(out=hsave[:], in_=hT[:, :, TS:TS + 2]), r=[RH[3]], w=[RHS])
                if DBG and st_i == 0:
                    P.dma("sp", dr["dbgH"], hT[:, :, 514:578], Res("dbgH"), r=[RH[4]], store=True)
                if st_i == 0:
                    for grp in range(11):
                        s = rot("sg", 2)
                        P.dma("sp", stage[s][:32, 0:512], dr["st_conv"][:, grp * 512:(grp + 1) * 512], RSG[s], w=[RSG[s]])
                        tr(lambda jj, s=s: stage[s][:32, jj * 128:(jj + 1) * 128], 4, 32, 128, prevT[:, grp * 4:grp * 4 + 4, :], [RSG[s]], [RPV], dt=F32, evac="dve")
                ncol2 = 2 + (NSAMP if st_i == 0 else 0)
                RHALL = RH[:NT] + [RHH]
                cw = cst["p_convw"]; cb = cst["p_convb"]
                for m_ in range(6):
                    nch = 4 if m_ < 5 else 2
                    sa = next_slab("w_up"); sbb_ = next_slab("w_up", live=2)
                    for sub in range(nch):
                        fidx = m_ * 4 + sub
                        ga_buf = None
                        for half, (slab, rsl) in enumerate((sa, sbb_)):
                            ch = fidx + 22 * half
                            _, U, rU = banks(2)
                            for k in range(8):
                                P.op("pe", lambda e, k=k, U=U, slab=slab, sub=sub: e.matmul(U[:, 0:512], slab[:, k, sub * 128:(sub + 1) * 128], hT[:, k, 0:512], start=(k == 0), stop=(k == 7)), r=[rsl] + RHALL, w=rU)
                                P.op("pe", lambda e, k=k, U=U, slab=slab, sub=sub, ncol2=ncol2: e.matmul(U[:, 512:512 + ncol2], slab[:, k, sub * 128:(sub + 1) * 128], hT[:, k, 512:512 + ncol2], start=(k == 0), stop=(k == 7)), r=[rsl] + RHALL, w=rU)
                            fa = rot("fa", 2)
                            y = f32a[fa]
                            P.op("act", lambda e, U=U, y=y, ch=ch: e.activation(out=y[:, 0:512], in_=U[:, 2:514], func=AF.Identity, scale=cw[:, ch, 2:3], bias=cb[:, ch:ch + 1]), r=rU + [RC], w=[RFA[fa]])
                            P.op("dve", lambda e, U=U, y=y, ch=ch: e.scalar_tensor_tensor(out=y[:, 0:512], in0=U[:, 1:513], scalar=cw[:, ch, 1:2], in1=y[:, 0:512], op0=ALU.mult, op1=ALU.add), r=rU + [RC, RFA[fa]], w=[RFA[fa]])
                            P.op("dve", lambda e, U=U, y=y, ch=ch: e.scalar_tensor_tensor(out=y[:, 0:512], in0=U[:, 0:512], scalar=cw[:, ch, 0:1], in1=y[:, 0:512], op0=ALU.mult, op1=ALU.add), r=rU[0:1] + [RC, RFA[fa]], w=[RFA[fa]])
                            if last_st:
                                P.op("act", lambda e, U=U, ch=ch: e.copy(out=zlast[:, ch, :], in_=U[:, 512:514]), r=rU[1:2], w=[RZL])
                            if st_i == 0:
                                zf = rot("bn", 2)
                                P.op("act", lambda e, U=U, zf=zf: e.copy(out=zfull[zf][:, :, 2:6], in_=U[:, 514:578].rearrange("p (b t) -> p b t", b=16)), r=rU[1:2], w=[RZF[zf]])
                                P.op("act", lambda e, U=U, ch=ch: e.copy(out=zkeep[:, ch, :].rearrange("p (b t) -> p b t", b=16), in_=U[:, 514:578].rearrange("p (b t) -> p b t", b=16)[:, :, 2:4]), r=rU[1:2], w=[RZK])
                                P.op("dve", lambda e, zf=zf, ch=ch: e.tensor_copy(out=zfull[zf][:, :, 0:2], in_=prevT[:, ch, :].rearrange("p (b t) -> p b t", b=16)), r=[RPV], w=[RZF[zf]])
                                ys = ysm[zf]
                                ysv = lambda q, ys=ys: ys[:, q, :].rearrange("p (b t) -> p b t", b=16)
                                P.op("dve", lambda e, zf=zf, ch=ch, ysv=ysv: e.tensor_scalar(out=ysv(0), in0=zfull[zf][:, :, 2:6], scalar1=cw[:, ch, 2:3], scalar2=cb[:, ch:ch + 1], op0=ALU.mult, op1=ALU.add), r=[RZF[zf], RC], w=[RYS[zf]])
                                P.op("dve", lambda e, zf=zf, ch=ch, ysv=ysv: e.scalar_tensor_tensor(out=ysv(1), in0=zfull[zf][:, :, 1:5], scalar=cw[:, ch, 1:2], in1=ysv(0), op0=ALU.mult, op1=ALU.add), r=[RZF[zf], RC, RYS[zf]], w=[RYS[zf]])
                                P.op("dve", lambda e, zf=zf, ch=ch, ysv=ysv: e.scalar_tensor_tensor(out=ysv(2), in0=zfull[zf][:, :, 0:4], scalar=cw[:, ch, 0:1], in1=ysv(1), op0=ALU.mult, op1=ALU.add), r=[RZF[zf], RC, RYS[zf]], w=[RYS[zf]])
                            if half == 0:
                                ga = rot("t", 2)
                                P.op("act", lambda e, y=y, ga=ga: e.activation(out=t1[ga][:, 0:512], in_=y[:, 0:512], func=AF.Gelu_apprx_tanh), r=[RFA[fa]], w=[RT1[ga]])
                                if st_i == 0:
                                    P.op("act", lambda e, zf=zf, ys=ys: e.activation(out=gas[zf][:], in_=ys[:, 2, :], func=AF.Gelu_apprx_tanh), r=[RYS[zf]], w=[RGS[zf]])
                                    gz = zf
                            else:
                                P.op("pool", lambda e, y=y, ga=ga, fidx=fidx: e.tensor_tensor(out=gatedT[:, fidx, 0:512], in0=y[:, 0:512], in1=t1[ga][:, 0:512], op=ALU.mult), r=[RFA[fa], RT1[ga]], w=[RGT[fidx]])
                                if st_i == 0:
                                    P.op("dve", lambda e, ys=ys, gz=gz, fidx=fidx: e.tensor_tensor(out=gatedT[:, fidx, 512:576], in0=ys[:, 2, :], in1=gas[gz][:], op=ALU.mult), r=[RYS[zf], RGS[gz]], w=[RGT[fidx]])
                if DBG and st_i == 0:
                    P.dma("sp", dr["dbgZ"], zkeep, Res("dbgZ"), r=[RZK] + ROST + ROXT, store=True)
                if st_i == 0: ckpt('Fup')
                outs = []
                if st_i == 0:
                    outs.append((zkeep, RZK, 32, "o_conv_s"))
                if last_st:
                    outs.append((zlast, RZL, 2, "o_conv_p"))
                for zb, rz, nr, oname in outs:
                    for grp in range(11):
                        s = rot("sg", 2)
                        tr(lambda jj, grp=grp, zb=zb: zb[:, grp * 4 + jj, :], 4, 128, nr, stage[s][:nr, 0:512].rearrange("p (n c) -> p n c", n=4), [rz], [RSG[s]], dt=F32, evac="dve")
                        P.dma("sp", dr[oname][:, grp * 512:(grp + 1) * 512], stage[s][:nr, 0:512], RSG[s], r=[RSG[s]], store=True)
                if st_i == 0: ckpt('Fout')
                for j in range(2):
                    sl = [next_slab("w_down", live=q + 1) for q in range(3)]
                    for i, t in enumerate(tiles):
                        r = t["rows"]
                        gc0 = 128 * i if t["kind"] == "p" else TS
                        _, pb, rb = banks(1)
                        for f in range(22):
                            slab, rsl = sl[f // 8]
                            P.op("pe", lambda e, f=f, pb=pb, r=r, gc0=gc0, slab=slab: e.matmul(pb[:r, 0:512], gatedT[:, f, gc0:gc0 + r], slab[:, f % 8, :], start=(f == 0), stop=(f == 21)), r=[rsl, RGT[f]], w=rb)
                        P.op("dve", lambda e, pb=pb, r=r, i=i, j=j: e.tensor_tensor(out=xres[:r, i, j * 512:(j + 1) * 512], in0=pb[:r, 0:512], in1=xres[:r, i, j * 512:(j + 1) * 512], op=ALU.add),
                             r=rb + [RX[i]], w=[RX[i]])
                        if j == 0:
                            continue
                        k = rot("st", 4)
                        b2 = rot("sb", 2)
                        P.op("act", lambda e, r=r, i=i, k=k, b2=b2: e.activation(out=xsb[b2][:r], in_=xres[:r, i, :], func=AF.Square, accum_out=st1[k][:r, 0:1]), r=[RX[i]], w=[RXS[b2], RST[k]], multi=True)
                        rstd2(st1[k][:r, 0:1], st1[k][:r, 1:2], RST[k], RST[k], 1.0 / D)
                        P.op("dve", lambda e, r=r, i=i, k=k: e.scalar_tensor_tensor(out=xres[:r, i, :], in0=xres[:r, i, :], scalar=st1[k][:r, 1:2], in1=cst["p_gfin"][:r], op0=ALU.mult, op1=ALU.mult),
                             r=[RX[i], RST[k], RC], w=[RX[i]])
                        dst = dr["y_p"][t["tok0"]:t["tok0"] + 128, :] if t["kind"] == "p" else dr["y_s"][:, :]
                        P.dma("sp", dst, xres[:r, i, :], RX[i], r=[RX[i]], store=True)
                        if (not last_st) and t["kind"] == "p":
                            if i == 0:
                                P.dma("sp", cs[:, 0:4], dr["c_cs"][:, (st_i + 1) * 4:(st_i + 1) * 4 + 4], RCS, w=[RCS])
                            ntok = (st_i + 1) * TS + 128 * i
                            P.dma("sp", xres[:, i, :], dr["x_p"][ntok:ntok + 128, :], RX[i], w=[RX[i]])
                if st_i == 0: ckpt('Fdown')

            assert taken[0] == len(sched), (taken[0], len(sched))
        except _Stop:
            pass
        with nc.Block() as block:
            P.emit(block)
    return nc


_NC_CACHE = {}


def kernel(**inp):
    f = lambda k: np.ascontiguousarray(np.asarray(inp[k], dtype=np.float32))
    cst = _const_tables()
    par = {
        "p_gmix": _fm(f("norm_mix_g")[0]), "p_gffn": _fm(f("norm_ffn_g")[0]), "p_gmem": _fm(f("mem_norm_g")[0]),
        "p_bgate": np.ascontiguousarray(f("b_gate")[0].reshape(1, 3072)), "p_gn": _rep(f("ret_gn_g")[0].reshape(-1)), "p_ln": _rep(f("sg_ln_g")[0]),
        "p_gfin": _rep(f("norm_final_g")),
        "p_convw": np.ascontiguousarray(f("conv_w")[0].reshape(3, NFC, 128).transpose(2, 1, 0)),
        "p_convb": np.ascontiguousarray(f("conv_b")[0].reshape(NFC, 128).T),
        "p_wt": np.ascontiguousarray(f("sg_ws")[0].transpose(2, 0, 1)),
    }
    ws = f("sg_ws")[0]
    idx = np.arange(64) % 4
    wrep = np.zeros((128, 4, 64), np.float32)
    wrep[:64] = ws[:, idx[None, :], idx[:, None]].transpose(1, 0, 2)
    par["p_wrep"] = wrep
    sgb = np.zeros((128, 2, 4), np.float32)
    bs = f("sg_bs")[0]
    sgb[:, 0, :] = bs.T
    sgb[:, 1, :] = bs[:, np.arange(128) % 4].T
    par["p_sgb"] = sgb
    weights = {n: f(n)[0] for n, _ in WEIGHT_SPECS}
    xp, xs, mem = f("x_prompt"), f("x_sample"), f("mem_prompt")
    sr, sc, ck, cv = f("state_ret")[0], f("state_conv")[0], f("cache_mem_k")[0], f("cache_mem_v")[0]
    in_maps = []
    for c in range(NCORES):
        m = {"x_p": xp[c], "x_s": np.ascontiguousarray(xs[16 * c:16 * c + 16].reshape(NSAMP, D)), "mem": mem[c],
             "st_ret": sr[16 * c:16 * c + 16], "st_conv": np.ascontiguousarray(sc[16 * c:16 * c + 16].reshape(32, 2 * DFF)),
             "ck": np.ascontiguousarray(ck[16 * c:16 * c + 16].reshape(NB, 256, 512)), "cv": np.ascontiguousarray(cv[16 * c:16 * c + 16].reshape(NB, 256, 512))}
        m.update(cst); m.update(par); m.update(weights)
        in_maps.append(m)
    if "nc" not in _NC_CACHE:
        _NC_CACHE["nc"] = build_program()
    res = run_bass_kernel_spmd(_NC_CACHE["nc"], in_maps, core_ids=list(range(NCORES)))
    R = res.results
    g = lambda k: np.stack([np.asarray(R[c][k], dtype=np.float32) for c in range(NCORES)])
    y_p = g("y_p")
    y_s = g("y_s").reshape(128, 4, D)
    ret_p = g("o_ret_p")[None]
    conv_p = g("o_conv_p")[None]
    mk = g("o_mk").reshape(1, 8, 256, 4, 128)
    mv = g("o_mv").reshape(1, 8, 256, 4, 128)
    ret_s = g("o_ret_s").reshape(1, 128, 4, 128, 256)
    conv_s = g("o_conv_s").reshape(1, 128, 2, 2 * DFF)
    sgv = g("o_sgv").reshape(1, 128, 4, 512)
    return (y_p, y_s, ret_p, conv_p, mk, mv, ret_s, conv_s, sgv)
```

```python
import os
import numpy as np
from contextlib import ExitStack
import concourse.bass as bass
import concourse.mybir as mybir
from concourse.bass_utils import run_bass_kernel_spmd

F32 = mybir.dt.float32
BF16 = mybir.dt.bfloat16
AF = mybir.ActivationFunctionType
ALU = mybir.AluOpType
AX = mybir.AxisListType

NCORES = 8
D = 1024
KC = 8
SEQ = 2048
TS = 512
NST = SEQ // TS
NSAMP = 64
NB = 16
HCOLS = 2 + TS + NSAMP
DFF = 2816
NFC = 44
EPS = 1e-6
RING = 4
GAM = [1.0 - 2.0 ** (-5 - h) for h in range(4)]


class _Stop(Exception):
    pass


class Res:
    __slots__ = ("name", "lw", "rd", "dsem", "dcnt", "excl")

    def __init__(self, name, excl=False):
        self.name = name
        self.excl = excl
        self.lw = None
        self.rd = []
        self.dsem = None
        self.dcnt = 0


class Op:
    __slots__ = ("eng", "fn", "deps", "sig", "val", "dma_owner", "multi")

    def __init__(self, eng, fn):
        self.eng = eng
        self.fn = fn
        self.deps = []
        self.sig = False
        self.val = None
        self.dma_owner = None
        self.multi = False


class Prog:
    ENGS = ("pe", "act", "dve", "pool", "sp")

    def __init__(self, nc, stack):
        self.nc = nc
        self.stack = stack
        self.ops = {e: [] for e in self.ENGS}
        self.esem = {e: stack.enter_context(nc.semaphore("es_" + e)) for e in self.ENGS}
        self.store_owners = []

    def _collect(self, eng, r, w, is_dma):
        deps = []

        def add(tok, raw):
            if tok is None:
                return
            if tok[0] == "op":
                src = tok[1]
                if src.eng == eng and not is_dma:
                    if eng == "pe":
                        return
                src.sig = True
            deps.append(tok)

        for res in r:
            add(res.lw, True)
        for res in w:
            add(res.lw, False)
            for t in res.rd:
                add(t, False)
        return deps

    def _update(self, tok, r, w):
        for res in w:
            res.lw = tok
            res.rd = []
        for res in r:
            if tok[0] == "op":
                res.rd = [t for t in res.rd if not (t[0] == "op" and t[1].eng == tok[1].eng)]
            res.rd.append(tok)

    def op(self, eng, fn, r=(), w=(), multi=False):
        ex = [x for x in r if x.excl and x not in w]
        if ex:
            r = [x for x in r if not x.excl]
            w = list(w) + ex
        o = Op(eng, fn)
        o.multi = multi
        o.deps = self._collect(eng, r, w, False)
        self.ops[eng].append(o)
        self._update(("op", o), r, w)
        return o

    def dma(self, q, out, in_, owner, r=(), w=(), store=False):
        o = Op(q, lambda e: e.dma_start(out=out, in_=in_))
        o.deps = self._collect(q, r, w, True)
        if owner.dsem is None:
            owner.dsem = self.stack.enter_context(self.nc.semaphore("ds_" + owner.name))
        owner.dcnt += 1
        o.dma_owner = owner
        self.ops[q].append(o)
        self._update(("dma", owner, owner.dcnt), r, w)
        if store and owner not in self.store_owners:
            self.store_owners.append(owner)
        return o

    def dma_multi(self, q, pairs, owner, r=(), w=()):
        deps = self._collect(q, r, w, True)
        if owner.dsem is None:
            owner.dsem = self.stack.enter_context(self.nc.semaphore("ds_" + owner.name))
        for out, in_ in pairs:
            o = Op(q, lambda e, out=out, in_=in_: e.dma_start(out=out, in_=in_))
            o.deps = list(deps)
            owner.dcnt += 1
            o.dma_owner = owner
            self.ops[q].append(o)
        self._update(("dma", owner, owner.dcnt), r, w)

    def emit(self, block):
        for e in self.ENGS:
            c = 0
            for o in self.ops[e]:
                if o.sig:
                    c += 1
                    o.val = c
        prog = self

        def run(ename, e):
            waited = {}
            sem_self = prog.esem[ename]
            for o in prog.ops[ename]:
                need = {}
                for t in o.deps:
                    if t[0] == "op":
                        s, v = prog.esem[t[1].eng], t[1].val
                    else:
                        s, v = t[1].dsem, 16 * t[2]
                    k = id(s)
                    if waited.get(k, (None, 0))[1] >= v:
                        continue
                    if k not in need or need[k][1] < v:
                        need[k] = (s, v)
                items = list(need.values())
                for s, v in items:
                    waited[id(s)] = (s, v)
                if o.dma_owner is not None:
                    for s, v in items:
                        e.wait_ge(s, v)
                    o.fn(e).then_inc(o.dma_owner.dsem, 16)
                else:
                    inline = ename in ("dve", "act") and not o.multi and len(items) > 0
                    for s, v in (items[:-1] if inline else items):
                        e.wait_ge(s, v)
                    ins = o.fn(e)
                    if inline:
                        ins._wait_ge(items[-1][0], items[-1][1])
                    if o.sig:
                        ins.then_inc(sem_self, 1)
            if ename == "sp":
                for ow in prog.store_owners:
                    e.wait_ge(ow.dsem, 16 * ow.dcnt)

        @block.tensor
        def _(e):
            run("pe", e)

        @block.scalar
        def _(e):
            run("act", e)

        @block.vector
        def _(e):
            run("dve", e)

        @block.gpsimd
        def _(e):
            run("pool", e)

        @block.sync
        def _(e):
            run("sp", e)


def _const_tables():
    c = {}
    c["c_ident"] = np.eye(128, dtype=np.float32)
    inv = (np.float32(10000.0) ** (-np.arange(0, 128, 2, dtype=np.float32) / np.float32(128))).astype(np.float32)
    cs = np.zeros((128, 17, 2, 64), np.float32)
    for g in range(17):
        if g < 16:
            pos = (g * 128 + np.arange(128)).astype(np.float32)
        else:
            pos = (16384 + (np.arange(128) % 4)).astype(np.float32)
        ang = (pos[:, None] * inv[None, :]).astype(np.float32).astype(np.float64)
        cs[:, g, 0, :] = np.cos(ang)
        cs[:, g, 1, :] = np.sin(ang)
    c["c_cs"] = cs
    dec = np.zeros((128, 2, 8), np.float64)
    for v in range(2):
        n = np.arange(128) if v == 0 else (np.arange(128) % 4)
        for h in range(4):
            lg = np.log1p(-2.0 ** (-5 - h))
            dec[:, v, h] = np.exp((n + 1.0) * lg)
            dec[:, v, 4 + h] = np.exp(-(n + 1.0) * lg) * (128.0 ** -0.5)
    c["c_dec"] = dec.astype(np.float32)
    gc = np.zeros((128, 2, 4), np.float64)
    for h in range(4):
        lg = np.log1p(-2.0 ** (-5 - h))
        gc[:, 0, h] = np.exp(128.0 * lg)
        gc[:, 1, h] = np.exp(4.0 * lg)
    c["c_gc"] = gc.astype(np.float32)
    s = np.arange(128)
    c["c_maskT"] = (s[:, None] <= s[None, :]).astype(np.float32)
    ms = np.zeros((128, 64), np.float32)
    s64 = np.arange(64)
    ms[:64] = ((s64[:, None] <= s64[None, :]) & ((s64[:, None] // 4) == (s64[None, :] // 4))).astype(np.float32)
    c["c_maskS"] = ms
    bmf = np.zeros((128, 16, 64), np.float32)
    for b in range(16):
        bmf[:, b, 4 * b:4 * b + 4] = 1.0
    c["c_bmfree"] = bmf
    bmp = np.zeros((128, 16), np.float32)
    for b in range(16):
        bmp[4 * b:4 * b + 4, b] = 1.0
    c["c_bmpart"] = bmp
    return c


def _fm(v):
    return np.ascontiguousarray(v.reshape(-1, 128).T).astype(np.float32)


def _rep(v):
    return np.ascontiguousarray(np.broadcast_to(v.reshape(1, -1), (128, v.size))).astype(np.float32)


CONST_SPECS = [
    ("c_ident", [128, 128]), ("c_dec", [128, 2, 8]), ("c_gc", [128, 2, 4]),
    ("c_maskT", [128, 128]), ("c_maskS", [128, 64]), ("c_bmpart", [128, 16]),
    ("p_gmix", [128, 8]), ("p_gffn", [128, 8]), ("p_gmem", [128, 8]),
    ("p_gn", [128, 1024]), ("p_ln", [128, 512]), ("p_gfin", [128, 1024]), ("p_convw", [128, 44, 3]),
    ("p_convb", [128, 44]), ("p_sgb", [128, 2, 4]),
]
WEIGHT_SPECS = [("w_in", [1024, 7680]), ("w_mem_kv", [1024, 1024]), ("w_br_ret", [1024, 1024]),
                ("w_br_sg", [512, 1024]), ("w_br_x", [512, 1024]), ("w_o", [1024, 1024]),
                ("w_up", [1024, 5632]), ("w_down", [2816, 1024])]


def build_program():
    nc = bass.Bass("TRN2", target_bir_lowering=False)
    dr = {}

    def din(name, shape, dt=F32):
        dr[name] = nc.dram_tensor(name, shape, dt, kind="ExternalInput").ap()

    def dout(name, shape):
        dr[name] = nc.dram_tensor(name, shape, F32, kind="ExternalOutput").ap()

    din("x_p", [SEQ, D]); din("x_s", [NSAMP, D]); din("mem", [256, D])
    din("st_ret", [NB, 4, 128, 256]); din("st_conv", [32, 2 * DFF])
    din("ck", [NB, 256, 512]); din("cv", [NB, 256, 512])
    for n, s in CONST_SPECS + WEIGHT_SPECS:
        din(n, s)
    din("c_cs", [128, 17, 2, 64]); din("c_bmfree", [128, 16, 64]); din("p_bgate", [1, 3072]); din("p_wt", [128, 4, 128]); din("p_wrep", [128, 4, 64])
    dout("y_p", [SEQ, D]); dout("y_s", [NSAMP, D]); dout("o_ret_p", [4, 128, 256]); dout("o_conv_p", [2, 2 * DFF])
    dout("o_mk", [256, 512]); dout("o_mv", [256, 512]); dout("o_ret_s", [NB, 4, 128, 256])
    dout("o_conv_s", [32, 2 * DFF]); dout("o_sgv", [NSAMP, 512])
    DBG = os.environ.get("KDBG", "")
    if DBG:
        dout("dbg", [128, 5, 1024])
        dr["dbgT"] = nc.dram_tensor("dbgT", [128, 16, HCOLS], BF16, kind="ExternalOutput").ap()
        dr["dbgM"] = nc.dram_tensor("dbgM", [128, 5, 1024], BF16, kind="ExternalOutput").ap()
        dout("dbgZ", [128, NFC, 32]); dr["dbgH"] = nc.dram_tensor("dbgH", [128, 8, 64], BF16, kind="ExternalOutput").ap()

    with ExitStack() as stack:
        P = Prog(nc, stack)

        def sb(name, shape, dt=F32):
            return stack.enter_context(nc.sbuf_tensor("sb_" + name, shape, dt))

        cst = {}
        RC = Res("const")
        for n, s in CONST_SPECS:
            cst[n] = sb(n, s)
            P.dma("sp", cst[n][:], dr[n], RC)
        RC.lw = ("dma", RC, RC.dcnt)
        identb = sb("identb", [128, 128], BF16)
        bmfree = sb("bmfreeb", [128, 16, 64], BF16)
        wtb = sb("wtb", [128, 4, 128], BF16)
        wsb = sb("wsb", [128, 4, 64], BF16)
        RCP = Res("constp"); RC2 = Res("const2")
        P.dma("pool", bmfree[:], dr["c_bmfree"], RCP)
        bgrow = sb("bgrow", [1, 3072], BF16)
        P.dma("pool", bgrow[:], dr["p_bgate"], RCP)
        ones1 = sb("ones1", [1, 128], BF16)
        cs = sb("cs_cur", [128, 5, 2, 64]); RCS = Res("cs_cur")
        P.dma("pool", wtb[:], dr["p_wt"], RCP)
        P.dma("pool", wsb[:], dr["p_wrep"], RCP)
        RC2.lw = ("dma", RCP, RCP.dcnt)
        P.op("dve", lambda e: e.memset(ones1[:], 1.0), r=[RC2], w=[RC])
        P.op("act", lambda e: e.copy(out=identb[:], in_=cst["c_ident"][:]), r=[RC], w=[RC])
        P.op("dve", lambda e: e.tensor_tensor(out=wtb[:], in0=wtb[:], in1=cst["c_maskT"][:].unsqueeze(1).broadcast_to([128, 4, 128]), op=ALU.mult), r=[RC], w=[RC])
        P.op("dve", lambda e: e.tensor_tensor(out=wsb[:64], in0=wsb[:64], in1=cst["c_maskS"][:64].unsqueeze(1).broadcast_to([64, 4, 64]), op=ALU.mult), r=[RC], w=[RC])
        identf = cst["c_ident"]

        psum = stack.enter_context(nc.psum_tensor("psum", [128, 4096], F32))
        RB = [Res("bank%d" % b, excl=True) for b in range(8)]
        bank_ptr = [0]

        reserved = set()

        def banks(n=1, reserve=False):
            p = bank_ptr[0]
            for _ in range(16):
                if p + n > 8:
                    p = 0
                if not any((p + q) in reserved for q in range(n)):
                    break
                p += 1
            else:
                raise RuntimeError("no psum banks")
            b0 = p
            bank_ptr[0] = (b0 + n) % 8
            if reserve:
                reserved.update(range(b0, b0 + n))
            return b0, psum[:, 512 * b0:512 * (b0 + n)], RB[b0:b0 + n]

        def release(b0, n):
            for q in range(n):
                reserved.discard(b0 + q)

        slabs = [sb("slab%d" % i, [128, 8, 512], BF16) for i in range(RING)]
        RS = [Res("slab%d" % i) for i in range(RING)]
        sched = []

        def sched_st():
            for blk in range(9):
                sched.append(("w_in", 0, 8, blk * 512, 512))
            for j in range(2):
                sched.append(("w_br_ret", 0, 8, j * 512, 512))
                sched.append(("w_in", 0, 8, 4608 + j * 512, 512))
                sched.append(("w_br_sg", 0, 4, j * 512, 512))
                sched.append(("w_in", 0, 8, 4608 + 1024 + j * 512, 512))
                sched.append(("w_br_x", 0, 4, j * 512, 512))
                sched.append(("w_in", 0, 8, 4608 + 2048 + j * 512, 512))
            for j in range(2):
                sched.append(("w_o", 0, 8, j * 512, 512))
            for m in range(6):
                nc_ = 512 if m < 5 else 256
                sched.append(("w_up", 0, 8, m * 512, nc_))
                sched.append(("w_up", 0, 8, DFF + m * 512, nc_))
            for j in range(2):
                sched.append(("w_down", 0, 8, j * 512, 512))
                sched.append(("w_down", 1024, 8, j * 512, 512))
                sched.append(("w_down", 2048, 6, j * 512, 512))

        sched.append(("w_mem_kv", 0, 8, 0, 512))
        sched.append(("w_mem_kv", 0, 8, 512, 512))
        sched_st()
        NSL = len(sched) - 2
        sched_meta = [(-1, 0), (-1, 1)] + [(0, e) for e in range(NSL)]
        for st_ in range(1, NST):
            sched_st()
            sched_meta += [(st_, e) for e in range(NSL)]
        wscr = nc.dram_tensor("wscr", [NSL, 128, 8, 512], BF16, kind="Internal").ap()
        RSCR = [Res("scr%d" % e) for e in range(NSL)]
        RSS = [Res("slabst%d" % i) for i in range(RING)]
        RSH = [Res("slabhw%d" % i) for i in range(RING)]
        RXP = [Res("xresp%d" % i) for i in range(5)]; RCSP = Res("cs_p")
        pending_store = []
        issued = [0]
        taken = [0]

        def issue_next():
            i = issued[0]
            if i >= len(sched):
                return
            wname, r0, nk, c0, ncols = sched[i]
            st_, e_ = sched_meta[i]
            slot = i % RING
            if st_ <= 0:
                pairs = []
                for k0 in range(0, nk, 4):
                    k1 = min(nk, k0 + 4)
                    src = dr[wname][r0 + k0 * 128:r0 + k1 * 128, c0:c0 + ncols].rearrange("(k p) c -> p k c", p=128)
                    pairs.append((slabs[slot][:, k0:k1, 0:ncols], src))
                P.dma_multi("pool", pairs, RS[slot], w=[RS[slot]])
                if st_ == 0:
                    pending_store.append((i, slot, e_, nk, ncols))
            else:
                P.dma("sp", slabs[slot][:, 0:nk, 0:ncols], wscr[e_][:, 0:nk, 0:ncols], RSH[slot], r=[RSCR[e_]], w=[RS[slot]])
            issued[0] += 1
            while pending_store and pending_store[0][0] <= i - 2:
                _, sl_, e2, nk2, nc2 = pending_store.pop(0)
                P.dma("sp", wscr[e2][:, 0:nk2, 0:nc2], slabs[sl_][:, 0:nk2, 0:nc2], RSS[sl_], r=[RS[sl_]], w=[RSCR[e2]])

        def next_slab(expect, live=1):
            i = taken[0]
            assert sched[i][0] == expect, (sched[i], expect)
            while issued[0] < min(len(sched), i + RING - live + 1):
                issue_next()
            taken[0] += 1
            return slabs[i % RING], RS[i % RING]

        xres = sb("xres", [128, 5, D]); RX = [Res("xres%d" % i) for i in range(5)]
        hT = sb("hT", [128, 8, HCOLS], BF16); RH = [Res("hT%d" % i) for i in range(5)]; RHH = Res("hThalo")
        hsave = sb("hsave", [128, 8, 2], BF16); RHS = Res("hsave")
        bigB = sb("bigB", [128, 15, 1024], BF16)
        B4 = sb("B4", [128, 5, 512], BF16)
        RBB = [[Res("B%d_%d" % (k, i)) for i in range(5)] for k in range(8)]
        qkT = bigB[:, 0:5]
        vtm = bigB[:, 5:10]
        ggb = bigB[:, 10:15]
        ktm = B4
        kmT = sb("kmT", [128, 4, 256], BF16); vmb = sb("vmb", [128, 2, 512], BF16); RKM = Res("kmT"); RVM = Res("vmb")
        Sf = sb("Sf", [128, 4, 256]); Sbf = sb("Sbf", [128, 4, 256], BF16); RSF = Res("Sf"); RSB = Res("Sbf")
        xsb = [sb("xsb%d" % i, [128, D], BF16) for i in range(2)]; RXS = [Res("xsb%d" % i) for i in range(2)]
        st1 = [sb("st1_%d" % i, [128, 8]) for i in range(4)]; RST = [Res("st1_%d" % i) for i in range(4)]
        t1 = [sb("t1_%d" % i, [128, 512]) for i in range(2)]; RT1 = [Res("t1_%d" % i) for i in range(2)]
        t2 = [sb("t2_%d" % i, [128, 512]) for i in range(2)]; RT2 = [Res("t2_%d" % i) for i in range(2)]
        f32a = [sb("f32a%d" % i, [128, 1024]) for i in range(2)]; RFA = [Res("f32a%d" % i) for i in range(2)]
        bfa = [sb("bfa%d" % i, [128, 1024], BF16) for i in range(2)]; RBA = [Res("bfa%d" % i) for i in range(2)]
        bfb = [sb("bfb%d" % i, [128, 1024], BF16) for i in range(2)]; RBFB = [Res("bfb%d" % i) for i in range(2)]
        bns = [sb("bns%d" % i, [128, 4, 6]) for i in range(2)]; RBN = [Res("bns%d" % i) for i in range(2)]
        mvs = [sb("mvs%d" % i, [128, 4, 2]) for i in range(2)]; RMV = [Res("mvs%d" % i) for i in range(2)]
        sm4 = [sb("sm4_%d" % i, [128, 3, 4]) for i in range(2)]; RSM = [Res("sm4_%d" % i) for i in range(2)]
        stage = [sb("stage%d" % i, [128, 512]) for i in range(2)]; RSG = [Res("stage%d" % i) for i in range(2)]
        ctr = {"xin": 0, "st": 0, "t": 0, "fa": 0, "ba": 0, "bb": 0, "bn": 0, "sm": 0, "sg": 0, "sb": 0}

        held = set()

        def rot(key, n):
            v = ctr[key]
            for _ in range(n):
                if (key, v) not in held:
                    break
                v = (v + 1) % n
            else:
                raise RuntimeError("ring full: " + key)
            ctr[key] = (v + 1) % n
            return v

        def hold(key, v):
            held.add((key, v))

        def drop(key, v):
            held.discard((key, v))

        sbin = [sb("sbin%d" % i, [128, 4, 256]) for i in range(2)]; RSI = [Res("sbin%d" % i) for i in range(2)]
        sbbf = [sb("sbbf%d" % i, [128, 4, 256], BF16) for i in range(2)]; RSBB = [Res("sbbf%d" % i) for i in range(2)]
        _sa = [sbin[i][:].rearrange("p a b -> p (a b)").bitcast(BF16) for i in range(2)]
        ckb = [_sa[i][:, 0:1024].rearrange("p (t c) -> p t c", t=2) for i in range(2)]; RCK = RSI
        cvb = [_sa[i][:, 1024:2048].rearrange("p (t c) -> p t c", t=2) for i in range(2)]; RCV = RSI
        kcT = sbbf; RKT = RSBB
        tailbuf = sb("tailbuf", [128, 10, 512], BF16)
        usg = tailbuf[:, 0:5]; vnb = tailbuf[:, 5:10]
        _tf = tailbuf[:].rearrange("p a b -> p (a b)").bitcast(F32)
        prevT = _tf[:, 0:1408].rearrange("p (c n) -> p c n", c=NFC); RPV = Res("prevT")
        B567 = sb("B567", [128, 16, HCOLS], BF16)
        B5 = B567[:, 0:8]; B6 = B567[:, 8:12]; B7 = B567[:, 12:16]
        zkeep = B567[:, 8:16].rearrange("p a b -> p (a b)").bitcast(F32)[:, 0:1408].rearrange("p (c n) -> p c n", c=NFC); RZK = Res("zkeep")
        zfull = [sb("zfull%d" % i, [128, 16, 6]) for i in range(2)]; RZF = [Res("zfull%d" % i) for i in range(2)]
        ysm = [sb("ysm%d" % i, [128, 3, 64]) for i in range(2)]; RYS = [Res("ysm%d" % i) for i in range(2)]
        gas = [sb("gas%d" % i, [128, 64]) for i in range(2)]; RGS = [Res("gas%d" % i) for i in range(2)]
        gatedT = bigB[:].rearrange("p a b -> p (a b)")[:, 0:22 * 576].rearrange("p (f t) -> p f t", f=22)

        def transposes(src_fn, n, rows, dst_ap_fn, rsrc, rdst, dt=BF16, evac="act"):
            _, pb, rb = banks(1)
            if dt == BF16:
                pv = pb.bitcast(BF16)[:, 0:n * 128].rearrange("p (n r) -> p n r", n=n)
                ident = identb
            else:
                pv = pb[:, 0:n * 128].rearrange("p (n r) -> p n r", n=n)
                ident = identf
            for j in range(n):
                src = src_fn(j)
                P.op("pe", lambda e, o=pv[:, j, 0:rows], s=src, idn=ident[:rows, :rows]: e.transpose(o, s, idn),
                     r=list(rsrc) + [RC], w=rb)
            dst = dst_ap_fn()
            if evac == "act":
                P.op("act", lambda e, d=dst, s=pv[:, :, 0:rows]: e.copy(out=d, in_=s), r=rb, w=rdst)
            else:
                P.op("dve", lambda e, d=dst, s=pv[:, :, 0:rows]: e.tensor_copy(out=d, in_=s), r=rb, w=rdst)
            return pv, rb

        def rstd_from(ssum_ap, out_ap, rres, scale):
            P.op("dve", lambda e: e.tensor_scalar(out=out_ap, in0=ssum_ap, scalar1=scale, scalar2=EPS, op0=ALU.mult, op1=ALU.add), r=[rres], w=[rres])
            P.op("act", lambda e: e.sqrt(out=out_ap, in_=out_ap), r=[rres], w=[rres])
            P.op("dve", lambda e: e.reciprocal(out=out_ap, in_=out_ap), r=[rres], w=[rres])

        def rmsnorm_to_hT(src_ap, rsrc, rows, col0, gtab, dstT, rdst):
            k = rot("st", 4)
            st, rst = st1[k], RST[k]
            b = rot("sb", 2)
            P.op("act", lambda e: e.activation(out=xsb[b][:rows], in_=src_ap, func=AF.Square, accum_out=st[:rows, 0:1]), r=[rsrc], w=[RXS[b], rst], multi=True)
            rstd_from(st[:rows, 0:1], st[:rows, 1:2], rst, 1.0 / D)
            P.op("dve", lambda e: e.tensor_scalar(out=xsb[b][:rows], in0=src_ap, scalar1=st[:rows, 1:2], scalar2=None, op0=ALU.mult), r=[rsrc, rst], w=[RXS[b]])
            _, pb, rb = banks(1)
            pv = pb.bitcast(BF16).rearrange("p (n r) -> p n r", n=8)
            for j in range(8):
                P.op("pe", lambda e, o=pv[:, j, 0:rows], s=xsb[b][:rows, j * 128:(j + 1) * 128]: e.transpose(o, s, identb[:rows, :rows]), r=[RXS[b], RC], w=rb)
            P.op("dve", lambda e: e.tensor_tensor(out=dstT[:, :, col0:col0 + rows], in0=pv[:, :, 0:rows], in1=gtab[:].unsqueeze(2).broadcast_to([128, 8, rows]), op=ALU.mult), r=rb + [RC], w=[rdst])

        def proj(slab, rslab, nk, srcT, rsrc, col0, rows, ncols=512, scol=0):
            _, pb, rb = banks(1)
            for k in range(nk):
                P.op("pe", lambda e, k=k: e.matmul(pb[:rows, 0:ncols], srcT[:, k, col0:col0 + rows], slab[:, k, scol:scol + ncols], start=(k == 0), stop=(k == nk - 1)),
                     r=[rslab] + list(rsrc), w=rb)
            return pb, rb

        STOP = os.environ.get("KSTOP", "")

        def ckpt(name):
            if STOP == name:
                raise _Stop()

        try:
            ckpt('c0')
            hTm = B5
            RHM = Res("hTm")
            for mt in range(2):
                if mt == 1:
                    ckpt('m0')
                P.dma("sp", xres[:, mt, :], dr["mem"][mt * 128:(mt + 1) * 128, :], RX[mt], w=[RX[mt]])
                rmsnorm_to_hT(xres[:, mt, :], RX[mt], 128, mt * 128, cst["p_gmem"], hTm, RHM)
            ckpt('m1')
            for blk in range(2):
                if blk == 1:
                    ckpt('m2')
                slab, rsl = next_slab("w_mem_kv")
                for mt in range(2):
                    pb, rb = proj(slab, rsl, 8, hTm, [RHM], mt * 128, 128)
                    ckpt('m1a')
                    s = rot("sg", 2)
                    P.op("act", lambda e, s=s, pb=pb: e.copy(out=stage[s][:, 0:512], in_=pb[:, 0:512]), r=rb, w=[RSG[s]])
                    P.dma("sp", dr["o_mk" if blk == 0 else "o_mv"][mt * 128:(mt + 1) * 128, :], stage[s][:, 0:512], RSG[s], r=[RSG[s]], store=True)
                    ckpt('m1b')
                    if blk == 0:
                        a = rot("ba", 2)
                        P.op("dve", lambda e, a=a, pb=pb: e.tensor_copy(out=bfa[a][:, 0:512], in_=pb[:, 0:512]), r=rb, w=[RBA[a]])
                        ckpt('m1c')
                        transposes(lambda j, a=a: bfa[a][:, j * 128:(j + 1) * 128], 4, 128,
                                   lambda mt=mt: kmT[:, :, mt * 128:(mt + 1) * 128], [RBA[a]], [RKM])
                    else:
                        P.op("dve", lambda e, pb=pb, mt=mt: e.tensor_copy(out=vmb[:, mt, :], in_=pb[:, 0:512]), r=rb, w=[RVM])

            RSO = [Res("sbout%d" % i) for i in range(2)]
            RUS = [Res("usg%d" % i) for i in range(5)]
            RCKO = [Res("cko%d" % i) for i in range(2)]
            RVN = [Res("vnb%d" % i) for i in range(5)]
            xqs = sb("xqs", [128, 4, 64], BF16); RXQS = Res("xqs")
            zlast = sb("zlast", [128, NFC, 2]); RZL = Res("zlast")
            RKTM = RBB[4]; RQKT = RBB[1]; RV = RBB[2]; RGG = RBB[3]; RORT = RBB[5]; ROST = RBB[6]; ROXT = RBB[7]
            RMG = [Res("merged%d" % i) for i in range(5)]; RMT = [Res("mergedT%d" % i) for i in range(5)]
            RGT = [Res("gated%d" % f) for f in range(22)]
            merged = vtm
            mergedT = bigB[:, 0:5].rearrange("p a b -> p (a b)")[:, 0:8 * HCOLS].rearrange("p (k c) -> p k c", k=8)
            maskT = cst["c_maskT"]; maskS = cst["c_maskS"]; bmpart = cst["c_bmpart"]

            def tr(src_fn, n, sp_, sf, dst, rsrc, rdst, dt=BF16, evac="act"):
                _, pb, rb = banks(1)
                if dt == BF16:
                    pv = pb.bitcast(BF16)[:sf, 0:n * sp_].rearrange("p (n r) -> p n r", n=n)
                    ident = identb
                else:
                    pv = pb[:sf, 0:n * sp_].rearrange("p (n r) -> p n r", n=n)
                    ident = identf
                for j in range(n):
                    if dt == BF16:
                        P.op("pe", lambda e, o=pv[:, j, :], s=src_fn(j), idn=ident[:sp_, :sp_]: e.transpose(o, s, idn), r=list(rsrc) + [RC], w=rb)
                    else:
                        P.op("pe", lambda e, o=pv[:, j, :], s=src_fn(j), idn=ident[:sp_, :sp_]: e.matmul(o, s, idn, start=True, stop=True), r=list(rsrc) + [RC], w=rb)
                if evac == "act":
                    P.op("act", lambda e: e.copy(out=dst, in_=pv), r=rb, w=rdst)
                else:
                    P.op("dve", lambda e: e.tensor_copy(out=dst, in_=pv), r=rb, w=rdst)

            def rstd2(in_ap, out_ap, rin, rout, scale):
                P.op("dve", lambda e: e.tensor_scalar(out=out_ap, in0=in_ap, scalar1=scale, scalar2=EPS, op0=ALU.mult, op1=ALU.add), r=[rin], w=[rout])
                P.op("act", lambda e: e.sqrt(out=out_ap, in_=out_ap), r=[rout], w=[rout])
                P.op("dve", lambda e: e.reciprocal(out=out_ap, in_=out_ap), r=[rout], w=[rout])

            def run_pipe(gens, depth):
                pending = list(gens)
                active = []
                while pending or active:
                    if pending and len(active) < depth:
                        active.append(pending.pop(0))
                    for g_ in list(active):
                        try:
                            next(g_)
                        except StopIteration:
                            active.remove(g_)

            def ret_finish(i, t, po, ro, hstride, rel):
                r = t["rows"]; c0 = t["col0"]
                n = rot("bn", 2); hold("bn", n)
                for h in range(4):
                    P.op("dve", lambda e, h=h: e.bn_stats(out=bns[n][:r, h, :], in_=po[:r, h * hstride:h * hstride + 256]), r=ro, w=[RBN[n]])
                for h in range(4):
                    P.op("dve", lambda e, h=h: e.bn_aggr(out=mvs[n][:r, h, :], in_=bns[n][:r, h, :]), r=[RBN[n]], w=[RMV[n]])
                m = rot("sm", 2); hold("sm", m)
                rstd2(mvs[n][:r, :, 1], sm4[m][:r, 0, :], RMV[n], RSM[m], 1.0)
                yield
                b = rot("fa", 2); hold("fa", b)
                for h in range(4):
                    P.op("dve", lambda e, h=h: e.scalar_tensor_tensor(out=f32a[b][:r, h * 256:(h + 1) * 256], in0=po[:r, h * hstride:h * hstride + 256],
                         scalar=mvs[n][:r, h, 0:1], in1=ggb[:r, i, h * 256:(h + 1) * 256], op0=ALU.subtract, op1=ALU.mult), r=ro + [RMV[n], RGG[i]], w=[RFA[b]])
                release(*rel)
                drop("bn", n)
                yield
                c = rot("bb", 2); hold("bb", c)
                for h in range(4):
                    P.op("act", lambda e, h=h: e.activation(out=bfb[c][:r, h * 256:(h + 1) * 256], in_=f32a[b][:r, h * 256:(h + 1) * 256], func=AF.Copy,
                         scale=sm4[m][:r, 0, h:h + 1]), r=[RFA[b], RSM[m]], w=[RBFB[c]])
                drop("sm", m); drop("fa", b)
                yield
                tr(lambda j: bfb[c][:r, j * 128:(j + 1) * 128], 8, r, 128, B5[:, :, c0:c0 + r], [RBFB[c]], [RORT[i]], evac="dve")
                drop("bb", c)

            ckpt('pre')
            for st_i in range(NST):
                tiles = [dict(kind="p", rows=128, g=st_i * 4 + i, col0=2 + 128 * i, tok0=st_i * TS + 128 * i) for i in range(4)]
                if st_i == 0:
                    tiles.append(dict(kind="s", rows=NSAMP, g=16, col0=2 + TS, tok0=0))
                NT = len(tiles)
                last_st = (st_i == NST - 1)

                if st_i == 0:
                    P.dma("sp", cs[:, 0:4], dr["c_cs"][:, 0:4], RCS, w=[RCS])
                    P.dma("sp", cs[:, 4:5], dr["c_cs"][:, 16:17], RCS, w=[RCS])
                for i, t in enumerate(tiles):
                    r = t["rows"]
                    src = dr["x_p"][t["tok0"]:t["tok0"] + 128, :] if t["kind"] == "p" else dr["x_s"][:, :]
                    if st_i == 0:
                        P.dma("sp", xres[:r, i, :], src, RX[i], w=[RX[i]])
                    rmsnorm_to_hT(xres[:r, i, :], RX[i], r, t["col0"], cst["p_gmix"], hT, RH[i])

                if st_i == 0: ckpt('p0')
                for blk in range(2):
                    slab, rsl = next_slab("w_in")
                    for i, t in enumerate(tiles):
                        r = t["rows"]; v = 0 if t["kind"] == "p" else 1
                        pb, rb = proj(slab, rsl, 8, hT, [RH[i]], t["col0"], r)
                        psv = pb[:r].rearrange("p (h t d) -> p h t d", h=4, t=2)
                        k1 = rot("t", 2)
                        a1 = t1[k1][:r].rearrange("p (h t d) -> p h t d", h=4, t=2)
                        a2 = t2[k1][:r].rearrange("p (h t d) -> p h t d", h=4, t=2)
                        cosb = cs[:r, i, 0, :].unsqueeze(1).unsqueeze(1).broadcast_to([r, 4, 2, 64])
                        sinb = cs[:r, i, 1, :].unsqueeze(1).broadcast_to([r, 4, 64])
                        P.op("dve", lambda e, a1=a1, psv=psv, cosb=cosb: e.tensor_tensor(out=a1, in0=psv, in1=cosb, op=ALU.mult), r=rb + [RCS], w=[RT1[k1]])
                        P.op("dve", lambda e, a2=a2, psv=psv, sinb=sinb: e.scalar_tensor_tensor(out=a2[:, :, 0, :], in0=psv[:, :, 1, :], scalar=-1.0, in1=sinb, op0=ALU.mult, op1=ALU.mult), r=rb + [RCS], w=[RT2[k1]])
                        P.op("dve", lambda e, a2=a2, psv=psv, sinb=sinb: e.tensor_tensor(out=a2[:, :, 1, :], in0=psv[:, :, 0, :], in1=sinb, op=ALU.mult), r=rb + [RCS], w=[RT2[k1]])
                        P.op("dve", lambda e, k1=k1, r=r: e.tensor_tensor(out=t1[k1][:r], in0=t1[k1][:r], in1=t2[k1][:r], op=ALU.add), r=[RT1[k1], RT2[k1]], w=[RT1[k1]])
                        decb = cst["c_dec"][:r, v, blk * 4:(blk + 1) * 4].unsqueeze(2).broadcast_to([r, 4, 128])
                        if blk == 0:
                            dsta = usg[:r, i, :]; rd = RUS[i]
                        else:
                            dsta = ktm[:r, i, :]; rd = RKTM[i]
                        P.op("dve", lambda e, k1=k1, r=r, decb=decb, dsta=dsta: e.tensor_tensor(
                            out=dsta.rearrange("p (h d) -> p h d", h=4),
                            in0=t1[k1][:r].rearrange("p (h d) -> p h d", h=4), in1=decb, op=ALU.mult), r=[RT1[k1], RC], w=[rd])
                for blk in range(2):
                    slab, rsl = next_slab("w_in")
                    for i, t in enumerate(tiles):
                        r = t["rows"]
                        pb, rb = proj(slab, rsl, 8, hT, [RH[i]], t["col0"], r)
                        P.op("act", lambda e, pb=pb, r=r, i=i, blk=blk: e.copy(out=vtm[:r, i, blk * 512:(blk + 1) * 512], in_=pb[:r, 0:512]), r=rb, w=[RV[i]])
                for blk in range(2):
                    slab, rsl = next_slab("w_in")
                    for i, t in enumerate(tiles):
                        r = t["rows"]
                        pb, rb = proj(slab, rsl, 8, hT, [RH[i]], t["col0"], r)
                        k1 = rot("t", 2)
                        P.op("act", lambda e, pb=pb, r=r, k1=k1: e.activation(out=t1[k1][:r], in_=pb[:r, 0:512], func=AF.Silu), r=rb, w=[RT1[k1]])
                        P.op("dve", lambda e, r=r, k1=k1, i=i, blk=blk: e.tensor_tensor(out=ggb[:r, i, blk * 512:(blk + 1) * 512], in0=t1[k1][:r],
                             in1=cst["p_gn"][:r, blk * 512:(blk + 1) * 512], op=ALU.mult), r=[RT1[k1], RC], w=[RGG[i]])
                if st_i == 0:
                    P.op("dve", lambda e: e.memset(Sf[:], 0.0), w=[RSF])
                def ret_gen(i, t):
                    r = t["rows"]
                    tr(lambda j, i=i, r=r: (usg if j < 4 else ktm)[:r, i, (j % 4) * 128:(j % 4 + 1) * 128], 8, r, 128,
                       qkT[:, i, :].rearrange("p (n c) -> p n c", n=8)[:, :, 0:r], [RUS[i], RKTM[i]], [RQKT[i]])
                    yield
                    qT = lambda h, i=i, r=r: qkT[:, i, h * 128:h * 128 + r]
                    kT = lambda h, i=i, r=r: qkT[:, i, (4 + h) * 128:(4 + h) * 128 + r]
                    _, pb, rb = banks(1)
                    for h in range(4):
                        P.op("pe", lambda e, h=h, pb=pb, r=r, qT=qT, kT=kT: e.matmul(pb[:r, h * r:(h + 1) * r], kT(h), qT(h), start=True, stop=True), r=[RQKT[i]], w=rb)
                    a = rot("ba", 2); hold("ba", a)
                    msk = (maskT[:r, :r] if t["kind"] == "p" else maskS[:r, :r]).unsqueeze(1).broadcast_to([r, 4, r])
                    P.op("dve", lambda e, a=a, pb=pb, r=r, msk=msk: e.tensor_tensor(out=bfa[a][:r, 0:4 * r].rearrange("p (h c) -> p h c", h=4),
                         in0=pb[:r, 0:4 * r].rearrange("p (h c) -> p h c", h=4), in1=msk, op=ALU.mult), r=rb + [RC], w=[RBA[a]])
                    yield
                    if t["kind"] == "p":
                        first = (t["g"] == 0)
                        pob0, po, ro = banks(2, reserve=True)
                        for h in range(4):
                            P.op("pe", lambda e, h=h, po=po, a=a, i=i, first=first: e.matmul(po[:, h * 256:(h + 1) * 256], bfa[a][:, h * 128:(h + 1) * 128],
                                 vtm[:, i, h * 256:(h + 1) * 256], start=True, stop=first), r=[RBA[a], RV[i]], w=ro)
                            if not first:
                                P.op("pe", lambda e, h=h, po=po, qT=qT: e.matmul(po[:, h * 256:(h + 1) * 256], qT(h), Sbf[:, h, :], start=False, stop=True), r=[RQKT[i], RSB], w=ro)
                        drop("ba", a)
                        _, pS, rS = banks(2)
                        for h in range(4):
                            P.op("pe", lambda e, h=h, pS=pS, i=i: e.matmul(pS[:, h * 256:(h + 1) * 256], ktm[:, i, h * 128:(h + 1) * 128], vtm[:, i, h * 256:(h + 1) * 256],
                                 start=True, stop=True), r=[RKTM[i], RV[i]], w=rS)
                        gcb = cst["c_gc"][:, 0, :].unsqueeze(2).broadcast_to([128, 4, 256])
                        P.op("dve", lambda e, pS=pS: e.tensor_tensor(out=Sf[:], in0=pS.rearrange("p (h c) -> p h c", h=4), in1=Sf[:], op=ALU.add), r=rS + [RSF], w=[RSF])
                        P.op("dve", lambda e, gcb=gcb: e.tensor_tensor(out=Sf[:], in0=Sf[:], in1=gcb, op=ALU.mult), r=[RSF, RC], w=[RSF])
                        if not (last_st and i == 3):
                            P.op("act", lambda e: e.copy(out=Sbf[:], in_=Sf[:]), r=[RSF], w=[RSB])
                        else:
                            P.dma("sp", dr["o_ret_p"].rearrange("h d e -> d h e"), Sf[:], RSF, r=[RSF], store=True)
                        yield
                        yield from ret_finish(i, t, po, ro, 256, (pob0, 2))
                    else:
                        pob0, po, ro = banks(4, reserve=True)
                        for h in range(4):
                            P.op("pe", lambda e, h=h, po=po, a=a, i=i: e.matmul(po[:64, h * 512:h * 512 + 256], bfa[a][:64, h * 64:(h + 1) * 64],
                                 vtm[:64, i, h * 256:(h + 1) * 256], start=True, stop=False, skip_group_check=True), r=[RBA[a], RV[i]], w=ro)
                        drop("ba", a)
                        gcs = cst["c_gc"][:, 1, :].unsqueeze(2).broadcast_to([128, 4, 256])
                        sbout = [B567[:, 8 + 4 * q:12 + 4 * q].rearrange("p a b -> p (a b)").bitcast(F32)[:, 0:1024].rearrange("p (h c) -> p h c", h=4) for q in range(2)]
                        RSOA = [ROST, ROXT]
                        P.dma("sp", sbin[0][:], dr["st_ret"][0].rearrange("h d e -> d h e"), RSI[0], w=[RSI[0]])
                        for b in range(NB):
                            k = b % 2
                            if b + 1 < NB:
                                P.dma("sp", sbin[1 - k][:], dr["st_ret"][b + 1].rearrange("h d e -> d h e"), RSI[1 - k], w=[RSI[1 - k]])
                            P.op("act", lambda e, k=k: e.copy(out=sbbf[k][:], in_=sbin[k][:]), r=[RSI[k]], w=[RSBB[k]])
                            c = rot("bb", 2)
                            P.op("dve", lambda e, c=c, b=b, i=i: e.tensor_tensor(out=bfb[c][:, 0:256].rearrange("p (h c) -> p h c", h=4),
                                 in0=qkT[:, i, 0:512].rearrange("p (h c) -> p h c", h=4)[:, :, 0:64], in1=bmfree[:, b, :].unsqueeze(1).broadcast_to([128, 4, 64]), op=ALU.mult),
                                 r=[RQKT[i], RC], w=[RBFB[c]])
                            for h in range(4):
                                P.op("pe", lambda e, h=h, po=po, c=c, k=k, b=b: e.matmul(po[:64, h * 512:h * 512 + 256], bfb[c][:, h * 64:(h + 1) * 64], sbbf[k][:, h, :],
                                     start=False, stop=(b == NB - 1), skip_group_check=True), r=[RBFB[c], RSBB[k]], w=ro)
                            a2 = rot("ba", 2)
                            P.op("dve", lambda e, a2=a2, b=b, i=i: e.tensor_scalar(out=bfa[a2][:64, 0:512], in0=ktm[:64, i, :], scalar1=bmpart[:64, b:b + 1], scalar2=None, op0=ALU.mult),
                                 r=[RKTM[i], RC], w=[RBA[a2]])
                            _, pS, rS = banks(2)
                            for h in range(4):
                                P.op("pe", lambda e, h=h, pS=pS, a2=a2, i=i: e.matmul(pS[:, h * 256:(h + 1) * 256], bfa[a2][:64, h * 128:(h + 1) * 128], vtm[:64, i, h * 256:(h + 1) * 256],
                                     start=True, stop=True), r=[RBA[a2], RV[i]], w=rS)
                            P.op("dve", lambda e, pS=pS, k=k, sbout=sbout: e.tensor_tensor(out=sbout[k], in0=pS.rearrange("p (h c) -> p h c", h=4), in1=sbin[k][:], op=ALU.add), r=rS + [RSI[k]], w=RSOA[k])
                            P.op("dve", lambda e, k=k, gcs=gcs, sbout=sbout: e.tensor_tensor(out=sbout[k], in0=sbout[k], in1=gcs, op=ALU.mult), r=RSOA[k] + [RC], w=RSOA[k])
                            P.dma("sp", dr["o_ret_s"][b].rearrange("h d e -> d h e"), sbout[k], RSO[k], r=RSOA[k], store=True)
                            yield
                        yield from ret_finish(i, t, po, ro, 512, (pob0, 4))

                order = [(i, t) for i, t in enumerate(tiles) if t["kind"] == "s"] + [(i, t) for i, t in enumerate(tiles) if t["kind"] == "p"]
                run_pipe([ret_gen(i, t) for i, t in order], 2)

                if st_i == 0: ckpt('R')
                slab, rsl = next_slab("w_in")
                for i, t in enumerate(tiles):
                    r = t["rows"]
                    pb, rb = proj(slab, rsl, 8, hT, [RH[i]], t["col0"], r)
                    P.op("act", lambda e, pb=pb, r=r, i=i: e.activation(out=usg[:r, i, :], in_=pb[:r, 0:512], func=AF.Gelu_apprx_tanh), r=rb, w=[RUS[i]])
                slab, rsl = next_slab("w_in")

                def sgu_gen(i, t, slab=slab, rsl=rsl):
                    r = t["rows"]; c0 = t["col0"]
                    pb, rb = proj(slab, rsl, 8, hT, [RH[i]], t["col0"], r)
                    k1 = rot("t", 2); hold("t", k1)
                    P.op("act", lambda e, pb=pb, r=r, k1=k1: e.activation(out=t1[k1][:r], in_=pb[:r, 0:512], func=AF.Gelu_apprx_tanh), r=rb, w=[RT1[k1]])
                    yield
                    n = rot("bn", 2)
                    P.op("dve", lambda e, n=n, r=r, k1=k1: e.bn_stats(out=bns[n][:r, 0, :], in_=t1[k1][:r]), r=[RT1[k1]], w=[RBN[n]])
                    P.op("dve", lambda e, n=n, r=r: e.bn_aggr(out=mvs[n][:r, 0, :], in_=bns[n][:r, 0, :]), r=[RBN[n]], w=[RMV[n]])
                    m = rot("sm", 2)
                    rstd2(mvs[n][:r, 0, 1:2], sm4[m][:r, 0, 0:1], RMV[n], RSM[m], 1.0)
                    P.op("dve", lambda e, n=n, m=m, r=r, k1=k1: e.tensor_scalar(out=t2[k1][:r], in0=t1[k1][:r], scalar1=mvs[n][:r, 0, 0:1], scalar2=sm4[m][:r, 0, 0:1],
                         op0=ALU.subtract, op1=ALU.mult), r=[RT1[k1], RMV[n], RSM[m]], w=[RT2[k1]])
                    yield
                    P.op("dve", lambda e, r=r, k1=k1, i=i: e.tensor_tensor(out=vnb[:r, i, :], in0=t2[k1][:r], in1=cst["p_ln"][:r], op=ALU.mult), r=[RT2[k1], RC], w=[RVN[i]])
                    if t["kind"] == "s":
                        s = rot("sg", 2)
                        P.op("dve", lambda e, r=r, k1=k1, s=s: e.tensor_tensor(out=stage[s][:r, 0:512], in0=t2[k1][:r], in1=cst["p_ln"][:r], op=ALU.mult), r=[RT2[k1], RC], w=[RSG[s]])
                        P.dma("sp", dr["o_sgv"], stage[s][:r, 0:512], RSG[s], r=[RSG[s]], store=True)
                    drop("t", k1)
                    pmb0, pm, rm = banks(1, reserve=True)
                    for g in range(4):
                        lw = wtb[:, g, :] if t["kind"] == "p" else wsb[:64, g, :]
                        P.op("pe", lambda e, g=g, pm=pm, lw=lw, r=r, i=i: e.matmul(pm[:r, g * 128:(g + 1) * 128], lw, vnb[:r, i, g * 128:(g + 1) * 128], start=True, stop=True),
                             r=[RC, RVN[i]], w=rm)
                    yield
                    a = rot("ba", 2); hold("ba", a)
                    vv = 0 if t["kind"] == "p" else 1
                    for g in range(4):
                        P.op("dve", lambda e, g=g, pm=pm, r=r, i=i, a=a, vv=vv: e.scalar_tensor_tensor(out=bfa[a][:r, g * 128:(g + 1) * 128], in0=pm[:r, g * 128:(g + 1) * 128],
                             scalar=cst["p_sgb"][:r, vv, g:g + 1], in1=usg[:r, i, g * 128:(g + 1) * 128], op0=ALU.add, op1=ALU.mult), r=rm + [RC, RUS[i]], w=[RBA[a]])
                    release(pmb0, 1)
                    yield
                    tr(lambda j, a=a, r=r: bfa[a][:r, j * 128:(j + 1) * 128], 4, r, 128, B6[:, :, c0:c0 + r], [RBA[a]], [ROST[i]])
                    drop("ba", a)


                if st_i == 0: ckpt('S')
                slab, rsl = next_slab("w_in", live=2)

                def xat_gen(i, t, slab=slab, rsl=rsl):
                    r = t["rows"]; c0 = t["col0"]
                    pb, rb = proj(slab, rsl, 8, hT, [RH[i]], t["col0"], r)
                    a = rot("ba", 2)
                    P.op("act", lambda e, pb=pb, r=r, a=a: e.activation(out=bfa[a][:r, 0:512], in_=pb[:r, 0:512], func=AF.Copy, scale=float(128.0 ** -0.5)), r=rb, w=[RBA[a]])
                    yield
                    if t["kind"] == "p":
                        xb = rot("sb", 2); hold("sb", xb)
                        xql = xsb[xb][:, 0:512].rearrange("p (h c) -> p h c", h=4); rxq = RXS[xb]
                    else:
                        xql = xqs[:]; rxq = RXQS
                    tr(lambda j, a=a, r=r: bfa[a][:r, j * 128:(j + 1) * 128], 4, r, 128, xql, [RBA[a]], [rxq])
                    yield
                    if t["kind"] == "p":
                        psb0, ps_, rs_ = banks(2, reserve=True)
                        hs = 256
                        for h in range(4):
                            P.op("pe", lambda e, h=h, ps_=ps_, c0=c0: e.matmul(ps_[:, h * 256:(h + 1) * 256], xql[:, h, :], kmT[:, h, :], start=True, stop=True), r=[rxq, RKM], w=rs_)
                    else:
                        psb0, ps_, rs_ = banks(4, reserve=True)
                        hs = 512
                        for b in range(NB):
                            k = b % 2
                            P.dma("pool", ckb[k], dr["ck"][b].rearrange("(t p) c -> p t c", p=128), RCKO[k], w=[RCK[k]])
                            tr(lambda j, k=k: ckb[k][:, j % 2, (j // 2) * 128:(j // 2 + 1) * 128], 8, 128, 128, kcT[k][:].rearrange("p h (t m) -> p (h t) m", t=2), [RCK[k]], [RKT[k]])
                            c = rot("bb", 2)
                            P.op("dve", lambda e, c=c, b=b, c0=c0: e.tensor_tensor(out=bfb[c][:, 0:256].rearrange("p (h c) -> p h c", h=4), in0=xql,
                                 in1=bmfree[:, b, :].unsqueeze(1).broadcast_to([128, 4, 64]), op=ALU.mult), r=[rxq, RC], w=[RBFB[c]])
                            for h in range(4):
                                P.op("pe", lambda e, h=h, ps_=ps_, c=c, k=k, b=b: e.matmul(ps_[:64, h * 512:h * 512 + 256], bfb[c][:, h * 64:(h + 1) * 64], kcT[k][:, h, :],
                                     start=(b == 0), stop=(b == NB - 1), skip_group_check=True), r=[RBFB[c], RKT[k]], w=rs_)
                            yield
                    if t["kind"] == "p":
                        drop("sb", xb)
                    yield
                    m = rot("sm", 2); hold("sm", m)
                    psv = ps_[:r].rearrange("p (h c) -> p h c", h=4)[:, :, 0:256] if hs == 512 else ps_[:r].rearrange("p (h c) -> p h c", h=4)
                    P.op("dve", lambda e, m=m, psv=psv, r=r: e.tensor_reduce(out=sm4[m][:r, 0, :], in_=psv, axis=AX.X, op=ALU.max), r=rs_, w=[RSM[m]])
                    P.op("dve", lambda e, m=m, r=r: e.tensor_scalar(out=sm4[m][:r, 1, :], in0=sm4[m][:r, 0, :], scalar1=-1.0, scalar2=None, op0=ALU.mult), r=[RSM[m]], w=[RSM[m]])
                    c = rot("bb", 2); hold("bb", c)
                    for h in range(4):
                        P.op("act", lambda e, h=h, m=m, c=c, r=r, ps_=ps_, hs=hs: e.activation(out=bfb[c][:r, h * 256:(h + 1) * 256], in_=ps_[:r, h * hs:h * hs + 256], func=AF.Exp,
                             bias=sm4[m][:r, 1, h:h + 1], accum_out=sm4[m][:r, 2, h:h + 1]), r=rs_ + [RSM[m]], w=[RBFB[c], RSM[m]], multi=True)
                    release(psb0, 2 if t["kind"] == "p" else 4)
                    P.op("dve", lambda e, m=m, r=r: e.reciprocal(out=sm4[m][:r, 0, :], in_=sm4[m][:r, 2, :]), r=[RSM[m]], w=[RSM[m]])
                    yield
                    a = rot("ba", 2); hold("ba", a)
                    tr(lambda j, c=c, r=r: bfb[c][:r, j * 128:(j + 1) * 128], 8, r, 128, bfa[a][:, 0:8 * r].rearrange("p (n c) -> p n c", n=8), [RBFB[c]], [RBA[a]])
                    drop("bb", c)
                    yield
                    if t["kind"] == "p":
                        poxb0, pox, rox = banks(1, reserve=True)
                        for h in range(4):
                            for hf in range(2):
                                P.op("pe", lambda e, h=h, hf=hf, pox=pox, a=a: e.matmul(pox[:, h * 128:(h + 1) * 128], bfa[a][:, (h * 2 + hf) * 128:(h * 2 + hf + 1) * 128],
                                     vmb[:, hf, h * 128:(h + 1) * 128], start=(hf == 0), stop=(hf == 1)), r=[RBA[a], RVM], w=rox)
                        poxv = pox.rearrange("p (h c) -> p h c", h=4)
                    else:
                        poxb0, pox, rox = banks(4, reserve=True)
                        for b in range(NB):
                            k = b % 2
                            P.dma("pool", cvb[k], dr["cv"][b].rearrange("(t p) c -> p t c", p=128), RCKO[k], w=[RCV[k]])
                            c2 = rot("bb", 2)
                            P.op("dve", lambda e, c2=c2, a=a, b=b: e.tensor_tensor(out=bfb[c2][:, 0:512].rearrange("p (n c) -> p n c", n=8), in0=bfa[a][:, 0:512].rearrange("p (n c) -> p n c", n=8),
                                 in1=bmfree[:, b, :].unsqueeze(1).broadcast_to([128, 8, 64]), op=ALU.mult), r=[RBA[a], RC], w=[RBFB[c2]])
                            for h in range(4):
                                for hf in range(2):
                                    P.op("pe", lambda e, h=h, hf=hf, pox=pox, c2=c2, k=k, b=b: e.matmul(pox[:64, h * 512:h * 512 + 128], bfb[c2][:, (h * 2 + hf) * 64:(h * 2 + hf + 1) * 64],
                                         cvb[k][:, hf, h * 128:(h + 1) * 128], start=(b == 0 and hf == 0), stop=(b == NB - 1 and hf == 1), skip_group_check=True), r=[RBFB[c2], RCV[k]], w=rox)
                            yield
                        poxv = pox[:64].rearrange("p (h c) -> p h c", h=4)[:, :, 0:128]
                    drop("ba", a)
                    yield
                    a3 = rot("ba", 2); hold("ba", a3)
                    P.op("dve", lambda e, a3=a3, poxv=poxv, m=m, r=r: e.tensor_tensor(out=bfa[a3][:r, 0:512].rearrange("p (h c) -> p h c", h=4), in0=poxv[:r],
                         in1=sm4[m][:r, 0, :].unsqueeze(2).broadcast_to([r, 4, 128]), op=ALU.mult), r=rox + [RSM[m]], w=[RBA[a3]])
                    release(poxb0, 1 if t["kind"] == "p" else 4)
                    drop("sm", m)
                    yield
                    tr(lambda j, a3=a3, r=r: bfa[a3][:r, j * 128:(j + 1) * 128], 4, r, 128, B7[:, :, c0:c0 + r], [RBA[a3]], [ROXT[i]])
                    drop("ba", a3)

                xorder = [(i, t) for i, t in enumerate(tiles) if t["kind"] == "s"] + [(i, t) for i, t in enumerate(tiles) if t["kind"] == "p"]
                run_pipe([sgu_gen(i, t) for i, t in enumerate(tiles)], 2)
                run_pipe([xat_gen(i, t) for i, t in xorder], 2)

                if st_i == 0: ckpt('X')
                if DBG and st_i == 0:
                    P.dma("sp", dr["dbgT"][:, :, 2:HCOLS], B567[:, :, 2:HCOLS], Res("dbgT"), r=RORT + ROST + ROXT + [RZK], store=True)
                accs = [(f32a[0][:, 0:512], RFA[0]), (f32a[0][:, 512:1024], RFA[0]), (f32a[1][:, 0:512], RFA[1]), (f32a[1][:, 512:1024], RFA[1]), (stage[0][:, 0:512], RSG[0])]
                srcs = [(B5, RORT, 8), (B6, ROST, 4), (B7, ROXT, 4)]
                for j in range(2):
                    for br in range(3):
                        sT, rsT, nk = srcs[br]
                        pslab, prs = next_slab(("w_br_ret", "w_br_sg", "w_br_x")[br])
                        gsl, grs = next_slab("w_in", live=2)
                        for i, t in enumerate(tiles):
                            r = t["rows"]; c0 = t["col0"]
                            acc, racc = accs[i]
                            pP, rP = proj(pslab, prs, nk, sT, [rsT[i]], c0, r)
                            _, pG, rG = banks(1)
                            P.op("pe", lambda e, pG=pG, r=r, br=br, j=j: e.matmul(pG[:r, 0:512], ones1[0:1, 0:r], bgrow[0:1, br * 1024 + j * 512:br * 1024 + (j + 1) * 512], start=True, stop=False), r=[RC], w=rG)
                            for k in range(8):
                                P.op("pe", lambda e, k=k, pG=pG, r=r, gsl=gsl, c0=c0: e.matmul(pG[:r, 0:512], hT[:, k, c0:c0 + r], gsl[:, k, :], start=False, stop=(k == 7)), r=[grs, RH[i]], w=rG)
                            k1 = rot("t", 2)
                            P.op("act", lambda e, pG=pG, r=r, k1=k1: e.activation(out=t1[k1][:r], in_=pG[:r, 0:512], func=AF.Sigmoid), r=rG, w=[RT1[k1]])
                            if br == 0:
                                P.op("dve", lambda e, pP=pP, r=r, k1=k1, acc=acc: e.tensor_tensor(out=acc[:r], in0=pP[:r, 0:512], in1=t1[k1][:r], op=ALU.mult), r=rP + [RT1[k1]], w=[racc])
                            else:
                                P.op("dve", lambda e, pP=pP, r=r, k1=k1: e.tensor_tensor(out=t2[k1][:r], in0=pP[:r, 0:512], in1=t1[k1][:r], op=ALU.mult), r=rP + [RT1[k1]], w=[RT2[k1]])
                                if br == 1:
                                    P.op("dve", lambda e, r=r, k1=k1, acc=acc: e.tensor_tensor(out=acc[:r], in0=acc[:r], in1=t2[k1][:r], op=ALU.add), r=[racc, RT2[k1]], w=[racc])
                                else:
                                    P.op("dve", lambda e, r=r, k1=k1, acc=acc, i=i, j=j: e.tensor_tensor(out=merged[:r, i, j * 512:(j + 1) * 512], in0=acc[:r], in1=t2[k1][:r], op=ALU.add),
                                         r=[racc, RT2[k1]], w=[RMG[i]])
                for i, t in enumerate(tiles):
                    r = t["rows"]; c0 = t["col0"]
                    tr(lambda jj, i=i, r=r: merged[:r, i, jj * 128:(jj + 1) * 128], 8, r, 128, mergedT[:, :, c0:c0 + r], [RMG[i]], [RMT[i]])

                if st_i == 0: ckpt('G')
                if DBG and st_i == 0:
                    for i in range(5):
                        rr = 128 if i < 4 else 64
                        P.dma("sp", dr["dbgM"][:rr, i, :], merged[:rr, i, :], Res("dbgM%d" % i), r=[RMG[i]] + RGT, store=True)
                for j in range(2):
                    slab, rsl = next_slab("w_o")
                    for i, t in enumerate(tiles):
                        r = t["rows"]
                        pb, rb = proj(slab, rsl, 8, mergedT, [RMT[i]], t["col0"], r)
                        P.op("dve", lambda e, pb=pb, r=r, i=i, j=j: e.tensor_tensor(out=xres[:r, i, j * 512:(j + 1) * 512], in0=pb[:r, 0:512], in1=xres[:r, i, j * 512:(j + 1) * 512], op=ALU.add),
                             r=rb + [RX[i]], w=[RX[i]])

                if DBG == "x1" and st_i == 0:
                    for i in range(5):
                        rr = 128 if i < 4 else 64
                        P.dma("sp", dr["dbg"][:rr, i, :], xres[:rr, i, :], Res("dbg%d" % i), r=[RX[i]], store=True)
                if st_i == 0: ckpt('O')
                for i, t in enumerate(tiles):
                    rmsnorm_to_hT(xres[:t["rows"], i, :], RX[i], t["rows"], t["col0"], cst["p_gffn"], hT, RH[i])
                if st_i == 0:
                    P.op("dve", lambda e: e.memset(hT[:, :, 0:2], 0.0), w=[RHH])
                else:
                    P.op("dve", lambda e: e.tensor_copy(out=hT[:, :, 0:2], in_=hsave[:]), r=[RHS], w=[RHH])
                P.op("dve", lambda e: e.tensor_copy(out=hsave[:], in_=hT[:, :, TS:TS + 2]), r=[RH[3]], w=[RHS])
                if DBG and st_i == 0:
                    P.dma("sp", dr["dbgH"], hT[:, :, 514:578], Res("dbgH"), r=[RH[4]], store=True)
                if st_i == 0:
                    for grp in range(11):
                        s = rot("sg", 2)
                        P.dma("sp", stage[s][:32, 0:512], dr["st_conv"][:, grp * 512:(grp + 1) * 512], RSG[s], w=[RSG[s]])
                        tr(lambda jj, s=s: stage[s][:32, jj * 128:(jj + 1) * 128], 4, 32, 128, prevT[:, grp * 4:grp * 4 + 4, :], [RSG[s]], [RPV], dt=F32, evac="dve")
                ncol2 = 2 + (NSAMP if st_i == 0 else 0)
                RHALL = RH[:NT] + [RHH]
                cw = cst["p_convw"]; cb = cst["p_convb"]
                for m_ in range(6):
                    nch = 4 if m_ < 5 else 2
                    sa = next_slab("w_up"); sbb_ = next_slab("w_up", live=2)
                    for sub in range(nch):
                        fidx = m_ * 4 + sub
                        ga_buf = None
                        for half, (slab, rsl) in enumerate((sa, sbb_)):
                            ch = fidx + 22 * half
                            _, U, rU = banks(2)
                            for k in range(8):
                                P.op("pe", lambda e, k=k, U=U, slab=slab, sub=sub: e.matmul(U[:, 0:512], slab[:, k, sub * 128:(sub + 1) * 128], hT[:, k, 0:512], start=(k == 0), stop=(k == 7)), r=[rsl] + RHALL, w=rU)
                                P.op("pe", lambda e, k=k, U=U, slab=slab, sub=sub, ncol2=ncol2: e.matmul(U[:, 512:512 + ncol2], slab[:, k, sub * 128:(sub + 1) * 128], hT[:, k, 512:512 + ncol2], start=(k == 0), stop=(k == 7)), r=[rsl] + RHALL, w=rU)
                            fa = rot("fa", 2)
                            y = f32a[fa]
                            P.op("act", lambda e, U=U, y=y, ch=ch: e.activation(out=y[:, 0:512], in_=U[:, 2:514], func=AF.Identity, scale=cw[:, ch, 2:3], bias=cb[:, ch:ch + 1]), r=rU + [RC], w=[RFA[fa]])
                            P.op("dve", lambda e, U=U, y=y, ch=ch: e.scalar_tensor_tensor(out=y[:, 0:512], in0=U[:, 1:513], scalar=cw[:, ch, 1:2], in1=y[:, 0:512], op0=ALU.mult, op1=ALU.add), r=rU + [RC, RFA[fa]], w=[RFA[fa]])
                            P.op("dve", lambda e, U=U, y=y, ch=ch: e.scalar_tensor_tensor(out=y[:, 0:512], in0=U[:, 0:512], scalar=cw[:, ch, 0:1], in1=y[:, 0:512], op0=ALU.mult, op1=ALU.add), r=rU[0:1] + [RC, RFA[fa]], w=[RFA[fa]])
                            if last_st:
                                P.op("act", lambda e, U=U, ch=ch: e.copy(out=zlast[:, ch, :], in_=U[:, 512:514]), r=rU[1:2], w=[RZL])
                            if st_i == 0:
                                zf = rot("bn", 2)
                                P.op("act", lambda e, U=U, zf=zf: e.copy(out=zfull[zf][:, :, 2:6], in_=U[:, 514:578].rearrange("p (b t) -> p b t", b=16)), r=rU[1:2], w=[RZF[zf]])
                                P.op("act", lambda e, U=U, ch=ch: e.copy(out=zkeep[:, ch, :].rearrange("p (b t) -> p b t", b=16), in_=U[:, 514:578].rearrange("p (b t) -> p b t", b=16)[:, :, 2:4]), r=rU[1:2], w=[RZK])
                                P.op("dve", lambda e, zf=zf, ch=ch: e.tensor_copy(out=zfull[zf][:, :, 0:2], in_=prevT[:, ch, :].rearrange("p (b t) -> p b t", b=16)), r=[RPV], w=[RZF[zf]])
                                ys = ysm[zf]
                                ysv = lambda q, ys=ys: ys[:, q, :].rearrange("p (b t) -> p b t", b=16)
                                P.op("dve", lambda e, zf=zf, ch=ch, ysv=ysv: e.tensor_scalar(out=ysv(0), in0=zfull[zf][:, :, 2:6], scalar1=cw[:, ch, 2:3], scalar2=cb[:, ch:ch + 1], op0=ALU.mult, op1=ALU.add), r=[RZF[zf], RC], w=[RYS[zf]])
                                P.op("dve", lambda e, zf=zf, ch=ch, ysv=ysv: e.scalar_tensor_tensor(out=ysv(1), in0=zfull[zf][:, :, 1:5], scalar=cw[:, ch, 1:2], in1=ysv(0), op0=ALU.mult, op1=ALU.add), r=[RZF[zf], RC, RYS[zf]], w=[RYS[zf]])
                                P.op("dve", lambda e, zf=zf, ch=ch, ysv=ysv: e.scalar_tensor_tensor(out=ysv(2), in0=zfull[zf][:, :, 0:4], scalar=cw[:, ch, 0:1], in1=ysv(1), op0=ALU.mult, op1=ALU.add), r=[RZF[zf], RC, RYS[zf]], w=[RYS[zf]])
                            if half == 0:
                                ga = rot("t", 2)
                                P.op("act", lambda e, y=y, ga=ga: e.activation(out=t1[ga][:, 0:512], in_=y[:, 0:512], func=AF.Gelu_apprx_tanh), r=[RFA[fa]], w=[RT1[ga]])
                                if st_i == 0:
                                    P.op("act", lambda e, zf=zf, ys=ys: e.activation(out=gas[zf][:], in_=ys[:, 2, :], func=AF.Gelu_apprx_tanh), r=[RYS[zf]], w=[RGS[zf]])
                                    gz = zf
                            else:
                                P.op("pool", lambda e, y=y, ga=ga, fidx=fidx: e.tensor_tensor(out=gatedT[:, fidx, 0:512], in0=y[:, 0:512], in1=t1[ga][:, 0:512], op=ALU.mult), r=[RFA[fa], RT1[ga]], w=[RGT[fidx]])
                                if st_i == 0:
                                    P.op("dve", lambda e, ys=ys, gz=gz, fidx=fidx: e.tensor_tensor(out=gatedT[:, fidx, 512:576], in0=ys[:, 2, :], in1=gas[gz][:], op=ALU.mult), r=[RYS[zf], RGS[gz]], w=[RGT[fidx]])
                if DBG and st_i == 0:
                    P.dma("sp", dr["dbgZ"], zkeep, Res("dbgZ"), r=[RZK] + ROST + ROXT, store=True)
                if st_i == 0: ckpt('Fup')
                outs = []
                if st_i == 0:
                    outs.append((zkeep, RZK, 32, "o_conv_s"))
                if last_st:
                    outs.append((zlast, RZL, 2, "o_conv_p"))
                for zb, rz, nr, oname in outs:
                    for grp in range(11):
                        s = rot("sg", 2)
                        tr(lambda jj, grp=grp, zb=zb: zb[:, grp * 4 + jj, :], 4, 128, nr, stage[s][:nr, 0:512].rearrange("p (n c) -> p n c", n=4), [rz], [RSG[s]], dt=F32, evac="dve")
                        P.dma("sp", dr[oname][:, grp * 512:(grp + 1) * 512], stage[s][:nr, 0:512], RSG[s], r=[RSG[s]], store=True)
                if st_i == 0: ckpt('Fout')
                for j in range(2):
                    sl = [next_slab("w_down", live=q + 1) for q in range(3)]
                    for i, t in enumerate(tiles):
                        r = t["rows"]
                        gc0 = 128 * i if t["kind"] == "p" else TS
                        _, pb, rb = banks(1)
                        for f in range(22):
                            slab, rsl = sl[f // 8]
                            P.op("pe", lambda e, f=f, pb=pb, r=r, gc0=gc0, slab=slab: e.matmul(pb[:r, 0:512], gatedT[:, f, gc0:gc0 + r], slab[:, f % 8, :], start=(f == 0), stop=(f == 21)), r=[rsl, RGT[f]], w=rb)
                        P.op("dve", lambda e, pb=pb, r=r, i=i, j=j: e.tensor_tensor(out=xres[:r, i, j * 512:(j + 1) * 512], in0=pb[:r, 0:512], in1=xres[:r, i, j * 512:(j + 1) * 512], op=ALU.add),
                             r=rb + [RX[i]], w=[RX[i]])
                        if j == 0:
                            continue
                        k = rot("st", 4)
                        b2 = rot("sb", 2)
                        P.op("act", lambda e, r=r, i=i, k=k, b2=b2: e.activation(out=xsb[b2][:r], in_=xres[:r, i, :], func=AF.Square, accum_out=st1[k][:r, 0:1]), r=[RX[i]], w=[RXS[b2], RST[k]], multi=True)
                        rstd2(st1[k][:r, 0:1], st1[k][:r, 1:2], RST[k], RST[k], 1.0 / D)
                        P.op("dve", lambda e, r=r, i=i, k=k: e.scalar_tensor_tensor(out=xres[:r, i, :], in0=xres[:r, i, :], scalar=st1[k][:r, 1:2], in1=cst["p_gfin"][:r], op0=ALU.mult, op1=ALU.mult),
                             r=[RX[i], RST[k], RC], w=[RX[i]])
                        dst = dr["y_p"][t["tok0"]:t["tok0"] + 128, :] if t["kind"] == "p" else dr["y_s"][:, :]
                        P.dma("pool", dst, xres[:r, i, :], RXP[i], r=[RX[i]], store=True)
                        if (not last_st) and t["kind"] == "p":
                            if i == 0:
                                P.dma("pool", cs[:, 0:4], dr["c_cs"][:, (st_i + 1) * 4:(st_i + 1) * 4 + 4], RCSP, w=[RCS])
                            ntok = (st_i + 1) * TS + 128 * i
                            P.dma("pool", xres[:, i, :], dr["x_p"][ntok:ntok + 128, :], RXP[i], w=[RX[i]])
                if st_i == 0: ckpt('Fdown')

            assert taken[0] == len(sched), (taken[0], len(sched))
        except _Stop:
            pass
        with nc.Block() as block:
            P.emit(block)
    return nc


_NC_CACHE = {}


def kernel(**inp):
    f = lambda k: np.ascontiguousarray(np.asarray(inp[k], dtype=np.float32))
    cst = _const_tables()
    par = {
        "p_gmix": _fm(f("norm_mix_g")[0]), "p_gffn": _fm(f("norm_ffn_g")[0]), "p_gmem": _fm(f("mem_norm_g")[0]),
        "p_bgate": np.ascontiguousarray(f("b_gate")[0].reshape(1, 3072)), "p_gn": _rep(f("ret_gn_g")[0].reshape(-1)), "p_ln": _rep(f("sg_ln_g")[0]),
        "p_gfin": _rep(f("norm_final_g")),
        "p_convw": np.ascontiguousarray(f("conv_w")[0].reshape(3, NFC, 128).transpose(2, 1, 0)),
        "p_convb": np.ascontiguousarray(f("conv_b")[0].reshape(NFC, 128).T),
        "p_wt": np.ascontiguousarray(f("sg_ws")[0].transpose(2, 0, 1)),
    }
    ws = f("sg_ws")[0]
    idx = np.arange(64) % 4
    wrep = np.zeros((128, 4, 64), np.float32)
    wrep[:64] = ws[:, idx[None, :], idx[:, None]].transpose(1, 0, 2)
    par["p_wrep"] = wrep
    sgb = np.zeros((128, 2, 4), np.float32)
    bs = f("sg_bs")[0]
    sgb[:, 0, :] = bs.T
    sgb[:, 1, :] = bs[:, np.arange(128) % 4].T
    par["p_sgb"] = sgb
    weights = {n: f(n)[0] for n, _ in WEIGHT_SPECS}
    xp, xs, mem = f("x_prompt"), f("x_sample"), f("mem_prompt")
    sr, sc, ck, cv = f("state_ret")[0], f("state_conv")[0], f("cache_mem_k")[0], f("cache_mem_v")[0]
    in_maps = []
    for c in range(NCORES):
        m = {"x_p": xp[c], "x_s": np.ascontiguousarray(xs[16 * c:16 * c + 16].reshape(NSAMP, D)), "mem": mem[c],
             "st_ret": sr[16 * c:16 * c + 16], "st_conv": np.ascontiguousarray(sc[16 * c:16 * c + 16].reshape(32, 2 * DFF)),
             "ck": np.ascontiguousarray(ck[16 * c:16 * c + 16].reshape(NB, 256, 512)), "cv": np.ascontiguousarray(cv[16 * c:16 * c + 16].reshape(NB, 256, 512))}
        m.update(cst); m.update(par); m.update(weights)
        in_maps.append(m)
    if "nc" not in _NC_CACHE:
        _NC_CACHE["nc"] = build_program()
    res = run_bass_kernel_spmd(_NC_CACHE["nc"], in_maps, core_ids=list(range(NCORES)))
    R = res.results
    g = lambda k: np.stack([np.asarray(R[c][k], dtype=np.float32) for c in range(NCORES)])
    y_p = g("y_p")
    y_s = g("y_s").reshape(128, 4, D)
    ret_p = g("o_ret_p")[None]
    conv_p = g("o_conv_p")[None]
    mk = g("o_mk").reshape(1, 8, 256, 4, 128)
    mv = g("o_mv").reshape(1, 8, 256, 4, 128)
    ret_s = g("o_ret_s").reshape(1, 128, 4, 128, 256)
    conv_s = g("o_conv_s").reshape(1, 128, 2, 2 * DFF)
    sgv = g("o_sgv").reshape(1, 128, 4, 512)
    return (y_p, y_s, ret_p, conv_p, mk, mv, ret_s, conv_s, sgv)
```

```python
import os
import numpy as np
from contextlib import ExitStack
import concourse.bass as bass
import concourse.mybir as mybir
from concourse.bass_utils import run_bass_kernel_spmd

F32 = mybir.dt.float32
BF16 = mybir.dt.bfloat16
AF = mybir.ActivationFunctionType
ALU = mybir.AluOpType
AX = mybir.AxisListType

NCORES = 8
D = 1024
KC = 8
SEQ = 2048
TS = 512
NST = SEQ // TS
NSAMP = 64
NB = 16
HCOLS = 2 + TS + NSAMP
DFF = 2816
NFC = 44
EPS = 1e-6
RING = 4
GAM = [1.0 - 2.0 ** (-5 - h) for h in range(4)]


class _Stop(Exception):
    pass


class Res:
    __slots__ = ("name", "lw", "rd", "dsem", "dcnt", "excl")

    def __init__(self, name, excl=False):
        self.name = name
        self.excl = excl
        self.lw = None
        self.rd = []
        self.dsem = None
        self.dcnt = 0


class Op:
    __slots__ = ("eng", "fn", "deps", "sig", "val", "dma_owner", "multi")

    def __init__(self, eng, fn):
        self.eng = eng
        self.fn = fn
        self.deps = []
        self.sig = False
        self.val = None
        self.dma_owner = None
        self.multi = False


class Prog:
    ENGS = ("pe", "act", "dve", "pool", "sp")

    def __init__(self, nc, stack):
        self.nc = nc
        self.stack = stack
        self.ops = {e: [] for e in self.ENGS}
        self.esem = {e: stack.enter_context(nc.semaphore("es_" + e)) for e in self.ENGS}
        self.store_owners = []

    def _collect(self, eng, r, w, is_dma):
        deps = []

        def add(tok, raw):
            if tok is None:
                return
            if tok[0] == "op":
                src = tok[1]
                if src.eng == eng and not is_dma:
                    if eng == "pe":
                        return
                src.sig = True
            deps.append(tok)

        for res in r:
            add(res.lw, True)
        for res in w:
            add(res.lw, False)
            for t in res.rd:
                add(t, False)
        return deps

    def _update(self, tok, r, w):
        for res in w:
            res.lw = tok
            res.rd = []
        for res in r:
            if tok[0] == "op":
                res.rd = [t for t in res.rd if not (t[0] == "op" and t[1].eng == tok[1].eng)]
            res.rd.append(tok)

    def op(self, eng, fn, r=(), w=(), multi=False):
        ex = [x for x in r if x.excl and x not in w]
        if ex:
            r = [x for x in r if not x.excl]
            w = list(w) + ex
        o = Op(eng, fn)
        o.multi = multi
        o.deps = self._collect(eng, r, w, False)
        self.ops[eng].append(o)
        self._update(("op", o), r, w)
        return o

    def dma(self, q, out, in_, owner, r=(), w=(), store=False):
        o = Op(q, lambda e: e.dma_start(out=out, in_=in_))
        o.deps = self._collect(q, r, w, True)
        if owner.dsem is None:
            owner.dsem = self.stack.enter_context(self.nc.semaphore("ds_" + owner.name))
        owner.dcnt += 1
        o.dma_owner = owner
        self.ops[q].append(o)
        self._update(("dma", owner, owner.dcnt), r, w)
        if store and owner not in self.store_owners:
            self.store_owners.append(owner)
        return o

    def dma_multi(self, q, pairs, owner, r=(), w=()):
        deps = self._collect(q, r, w, True)
        if owner.dsem is None:
            owner.dsem = self.stack.enter_context(self.nc.semaphore("ds_" + owner.name))
        for out, in_ in pairs:
            o = Op(q, lambda e, out=out, in_=in_: e.dma_start(out=out, in_=in_))
            o.deps = list(deps)
            owner.dcnt += 1
            o.dma_owner = owner
            self.ops[q].append(o)
        self._update(("dma", owner, owner.dcnt), r, w)

    def emit(self, block):
        for e in self.ENGS:
            c = 0
            for o in self.ops[e]:
                if o.sig:
                    c += 1
                    o.val = c
        prog = self

        def run(ename, e):
            waited = {}
            sem_self = prog.esem[ename]
            for o in prog.ops[ename]:
                need = {}
                for t in o.deps:
                    if t[0] == "op":
                        s, v = prog.esem[t[1].eng], t[1].val
                    else:
                        s, v = t[1].dsem, 16 * t[2]
                    k = id(s)
                    if waited.get(k, (None, 0))[1] >= v:
                        continue
                    if k not in need or need[k][1] < v:
                        need[k] = (s, v)
                items = list(need.values())
                for s, v in items:
                    waited[id(s)] = (s, v)
                if o.dma_owner is not None:
                    for s, v in items:
                        e.wait_ge(s, v)
                    o.fn(e).then_inc(o.dma_owner.dsem, 16)
                else:
                    inline = ename in ("dve", "act") and not o.multi and len(items) > 0
                    for s, v in (items[:-1] if inline else items):
                        e.wait_ge(s, v)
                    ins = o.fn(e)
                    if inline:
                        ins._wait_ge(items[-1][0], items[-1][1])
                    if o.sig:
                        ins.then_inc(sem_self, 1)
            if ename == "sp":
                for ow in prog.store_owners:
                    e.wait_ge(ow.dsem, 16 * ow.dcnt)

        @block.tensor
        def _(e):
            run("pe", e)

        @block.scalar
        def _(e):
            run("act", e)

        @block.vector
        def _(e):
            run("dve", e)

        @block.gpsimd
        def _(e):
            run("pool", e)

        @block.sync
        def _(e):
            run("sp", e)


def _const_tables():
    c = {}
    c["c_ident"] = np.eye(128, dtype=np.float32)
    inv = (np.float32(10000.0) ** (-np.arange(0, 128, 2, dtype=np.float32) / np.float32(128))).astype(np.float32)
    cs = np.zeros((128, 17, 2, 64), np.float32)
    for g in range(17):
        if g < 16:
            pos = (g * 128 + np.arange(128)).astype(np.float32)
        else:
            pos = (16384 + (np.arange(128) % 4)).astype(np.float32)
        ang = (pos[:, None] * inv[None, :]).astype(np.float32).astype(np.float64)
        cs[:, g, 0, :] = np.cos(ang)
        cs[:, g, 1, :] = np.sin(ang)
    c["c_cs"] = cs
    dec = np.zeros((128, 2, 8), np.float64)
    for v in range(2):
        n = np.arange(128) if v == 0 else (np.arange(128) % 4)
        for h in range(4):
            lg = np.log1p(-2.0 ** (-5 - h))
            dec[:, v, h] = np.exp((n + 1.0) * lg)
            dec[:, v, 4 + h] = np.exp(-(n + 1.0) * lg) * (128.0 ** -0.5)
    c["c_dec"] = dec.astype(np.float32)
    gc = np.zeros((128, 2, 4), np.float64)
    for h in range(4):
        lg = np.log1p(-2.0 ** (-5 - h))
        gc[:, 0, h] = np.exp(128.0 * lg)
        gc[:, 1, h] = np.exp(4.0 * lg)
    c["c_gc"] = gc.astype(np.float32)
    s = np.arange(128)
    c["c_maskT"] = (s[:, None] <= s[None, :]).astype(np.float32)
    ms = np.zeros((128, 64), np.float32)
    s64 = np.arange(64)
    ms[:64] = ((s64[:, None] <= s64[None, :]) & ((s64[:, None] // 4) == (s64[None, :] // 4))).astype(np.float32)
    c["c_maskS"] = ms
    bmf = np.zeros((128, 16, 64), np.float32)
    for b in range(16):
        bmf[:, b, 4 * b:4 * b + 4] = 1.0
    c["c_bmfree"] = bmf
    bmp = np.zeros((128, 16), np.float32)
    for b in range(16):
        bmp[4 * b:4 * b + 4, b] = 1.0
    c["c_bmpart"] = bmp
    return c


def _fm(v):
    return np.ascontiguousarray(v.reshape(-1, 128).T).astype(np.float32)


def _rep(v):
    return np.ascontiguousarray(np.broadcast_to(v.reshape(1, -1), (128, v.size))).astype(np.float32)


CONST_SPECS = [
    ("c_ident", [128, 128]), ("c_dec", [128, 2, 8]), ("c_gc", [128, 2, 4]),
    ("c_maskT", [128, 128]), ("c_maskS", [128, 64]), ("c_bmpart", [128, 16]),
    ("p_gmix", [128, 8]), ("p_gffn", [128, 8]), ("p_gmem", [128, 8]),
    ("p_gn", [128, 1024]), ("p_ln", [128, 512]), ("p_gfin", [128, 1024]), ("p_convw", [128, 44, 3]),
    ("p_convb", [128, 44]), ("p_sgb", [128, 2, 4]),
]
WEIGHT_SPECS = [("w_in", [1024, 7680]), ("w_mem_kv", [1024, 1024]), ("w_br_ret", [1024, 1024]),
                ("w_br_sg", [512, 1024]), ("w_br_x", [512, 1024]), ("w_o", [1024, 1024]),
                ("w_up", [1024, 5632]), ("w_down", [2816, 1024])]


def build_program():
    nc = bass.Bass("TRN2", target_bir_lowering=False)
    dr = {}

    def din(name, shape, dt=F32):
        dr[name] = nc.dram_tensor(name, shape, dt, kind="ExternalInput").ap()

    def dout(name, shape):
        dr[name] = nc.dram_tensor(name, shape, F32, kind="ExternalOutput").ap()

    din("x_p", [SEQ, D]); din("x_s", [NSAMP, D]); din("mem", [256, D])
    din("st_ret", [NB, 4, 128, 256]); din("st_conv", [32, 2 * DFF])
    din("ck", [NB, 256, 512]); din("cv", [NB, 256, 512])
    for n, s in CONST_SPECS + WEIGHT_SPECS:
        din(n, s)
    din("c_cs", [128, 17, 2, 64]); din("c_bmfree", [128, 16, 64]); din("p_bgate", [1, 3072]); din("p_wt", [128, 4, 128]); din("p_wrep", [128, 4, 64])
    dout("y_p", [SEQ, D]); dout("y_s", [NSAMP, D]); dout("o_ret_p", [4, 128, 256]); dout("o_conv_p", [2, 2 * DFF])
    dout("o_mk", [256, 512]); dout("o_mv", [256, 512]); dout("o_ret_s", [NB, 4, 128, 256])
    dout("o_conv_s", [32, 2 * DFF]); dout("o_sgv", [NSAMP, 512])
    DBG = os.environ.get("KDBG", "")
    if DBG:
        dout("dbg", [128, 5, 1024])
        dr["dbgT"] = nc.dram_tensor("dbgT", [128, 16, HCOLS], BF16, kind="ExternalOutput").ap()
        dr["dbgM"] = nc.dram_tensor("dbgM", [128, 5, 1024], BF16, kind="ExternalOutput").ap()
        dout("dbgZ", [128, NFC, 32]); dr["dbgH"] = nc.dram_tensor("dbgH", [128, 8, 64], BF16, kind="ExternalOutput").ap()

    with ExitStack() as stack:
        P = Prog(nc, stack)

        def sb(name, shape, dt=F32):
            return stack.enter_context(nc.sbuf_tensor("sb_" + name, shape, dt))

        cst = {}
        RC = Res("const")
        for n, s in CONST_SPECS:
            cst[n] = sb(n, s)
            P.dma("sp", cst[n][:], dr[n], RC)
        RC.lw = ("dma", RC, RC.dcnt)
        identb = sb("identb", [128, 128], BF16)
        bmfree = sb("bmfreeb", [128, 16, 64], BF16)
        wtb = sb("wtb", [128, 4, 128], BF16)
        wsb = sb("wsb", [128, 4, 64], BF16)
        RCP = Res("constp"); RC2 = Res("const2")
        P.dma("pool", bmfree[:], dr["c_bmfree"], RCP)
        bgrow = sb("bgrow", [1, 3072], BF16)
        P.dma("pool", bgrow[:], dr["p_bgate"], RCP)
        ones1 = sb("ones1", [1, 128], BF16)
        cs = sb("cs_cur", [128, 5, 2, 64]); RCS = Res("cs_cur")
        P.dma("pool", wtb[:], dr["p_wt"], RCP)
        P.dma("pool", wsb[:], dr["p_wrep"], RCP)
        RC2.lw = ("dma", RCP, RCP.dcnt)
        P.op("dve", lambda e: e.memset(ones1[:], 1.0), r=[RC2], w=[RC])
        P.op("act", lambda e: e.copy(out=identb[:], in_=cst["c_ident"][:]), r=[RC], w=[RC])
        P.op("dve", lambda e: e.tensor_tensor(out=wtb[:], in0=wtb[:], in1=cst["c_maskT"][:].unsqueeze(1).broadcast_to([128, 4, 128]), op=ALU.mult), r=[RC], w=[RC])
        P.op("dve", lambda e: e.tensor_tensor(out=wsb[:64], in0=wsb[:64], in1=cst["c_maskS"][:64].unsqueeze(1).broadcast_to([64, 4, 64]), op=ALU.mult), r=[RC], w=[RC])
        identf = cst["c_ident"]

        psum = stack.enter_context(nc.psum_tensor("psum", [128, 4096], F32))
        RB = [Res("bank%d" % b, excl=True) for b in range(8)]
        bank_ptr = [0]

        reserved = set()

        def banks(n=1, reserve=False):
            p = bank_ptr[0]
            for _ in range(16):
                if p + n > 8:
                    p = 0
                if not any((p + q) in reserved for q in range(n)):
                    break
                p += 1
            else:
                raise RuntimeError("no psum banks")
            b0 = p
            bank_ptr[0] = (b0 + n) % 8
            if reserve:
                reserved.update(range(b0, b0 + n))
            return b0, psum[:, 512 * b0:512 * (b0 + n)], RB[b0:b0 + n]

        def release(b0, n):
            for q in range(n):
                reserved.discard(b0 + q)

        slabs = [sb("slab%d" % i, [128, 8, 512], BF16) for i in range(RING)]
        RS = [Res("slab%d" % i) for i in range(RING)]
        sched = []

        def sched_st():
            for blk in range(9):
                sched.append(("w_in", 0, 8, blk * 512, 512))
            for j in range(2):
                sched.append(("w_br_ret", 0, 8, j * 512, 512))
                sched.append(("w_in", 0, 8, 4608 + j * 512, 512))
                sched.append(("w_br_sg", 0, 4, j * 512, 512))
                sched.append(("w_in", 0, 8, 4608 + 1024 + j * 512, 512))
                sched.append(("w_br_x", 0, 4, j * 512, 512))
                sched.append(("w_in", 0, 8, 4608 + 2048 + j * 512, 512))
            for j in range(2):
                sched.append(("w_o", 0, 8, j * 512, 512))
            for m in range(6):
                nc_ = 512 if m < 5 else 256
                sched.append(("w_up", 0, 8, m * 512, nc_))
                sched.append(("w_up", 0, 8, DFF + m * 512, nc_))
            for j in range(2):
                sched.append(("w_down", 0, 8, j * 512, 512))
                sched.append(("w_down", 1024, 8, j * 512, 512))
                sched.append(("w_down", 2048, 6, j * 512, 512))

        sched.append(("w_mem_kv", 0, 8, 0, 512))
        sched.append(("w_mem_kv", 0, 8, 512, 512))
        sched_st()
        NSL = len(sched) - 2
        sched_meta = [(-1, 0), (-1, 1)] + [(0, e) for e in range(NSL)]
        for st_ in range(1, NST):
            sched_st()
            sched_meta += [(st_, e) for e in range(NSL)]
        wscr = nc.dram_tensor("wscr", [NSL, 128, 8, 512], BF16, kind="Internal").ap()
        RSCR = [Res("scr%d" % e) for e in range(NSL)]
        RSS = [Res("slabst%d" % i) for i in range(RING)]
        RSH = [Res("slabhw%d" % i) for i in range(RING)]
        pending_store = []
        issued = [0]
        taken = [0]

        def issue_next():
            i = issued[0]
            if i >= len(sched):
                return
            wname, r0, nk, c0, ncols = sched[i]
            st_, e_ = sched_meta[i]
            slot = i % RING
            if st_ <= 0:
                pairs = []
                for k0 in range(0, nk, 4):
                    k1 = min(nk, k0 + 4)
                    src = dr[wname][r0 + k0 * 128:r0 + k1 * 128, c0:c0 + ncols].rearrange("(k p) c -> p k c", p=128)
                    pairs.append((slabs[slot][:, k0:k1, 0:ncols], src))
                P.dma_multi("pool", pairs, RS[slot], w=[RS[slot]])
                if st_ == 0:
                    pending_store.append((i, slot, e_, nk, ncols))
            else:
                P.dma("sp", slabs[slot][:, 0:nk, 0:ncols], wscr[e_][:, 0:nk, 0:ncols], RSH[slot], r=[RSCR[e_]], w=[RS[slot]])
            issued[0] += 1
            while pending_store and pending_store[0][0] <= i - 3:
                _, sl_, e2, nk2, nc2 = pending_store.pop(0)
                P.dma("sp", wscr[e2][:, 0:nk2, 0:nc2], slabs[sl_][:, 0:nk2, 0:nc2], RSS[sl_], r=[RS[sl_]], w=[RSCR[e2]])

        def next_slab(expect, live=1):
            i = taken[0]
            assert sched[i][0] == expect, (sched[i], expect)
            while issued[0] < min(len(sched), i + RING - live + 1):
                issue_next()
            taken[0] += 1
            return slabs[i % RING], RS[i % RING]

        xres = sb("xres", [128, 5, D]); RX = [Res("xres%d" % i) for i in range(5)]
        hT = sb("hT", [128, 8, HCOLS], BF16); RH = [Res("hT%d" % i) for i in range(5)]; RHH = Res("hThalo")
        hsave = sb("hsave", [128, 8, 2], BF16); RHS = Res("hsave")
        bigB = sb("bigB", [128, 15, 1024], BF16)
        B4 = sb("B4", [128, 5, 512], BF16)
        RBB = [[Res("B%d_%d" % (k, i)) for i in range(5)] for k in range(8)]
        qkT = bigB[:, 0:5]
        vtm = bigB[:, 5:10]
        ggb = bigB[:, 10:15]
        ktm = B4
        kmT = sb("kmT", [128, 4, 256], BF16); vmb = sb("vmb", [128, 2, 512], BF16); RKM = Res("kmT"); RVM = Res("vmb")
        Sf = sb("Sf", [128, 4, 256]); Sbf = sb("Sbf", [128, 4, 256], BF16); RSF = Res("Sf"); RSB = Res("Sbf")
        xsb = [sb("xsb%d" % i, [128, D], BF16) for i in range(2)]; RXS = [Res("xsb%d" % i) for i in range(2)]
        st1 = [sb("st1_%d" % i, [128, 8]) for i in range(4)]; RST = [Res("st1_%d" % i) for i in range(4)]
        t1 = [sb("t1_%d" % i, [128, 512]) for i in range(2)]; RT1 = [Res("t1_%d" % i) for i in range(2)]
        t2 = [sb("t2_%d" % i, [128, 512]) for i in range(2)]; RT2 = [Res("t2_%d" % i) for i in range(2)]
        f32a = [sb("f32a%d" % i, [128, 1024]) for i in range(2)]; RFA = [Res("f32a%d" % i) for i in range(2)]
        bfa = [sb("bfa%d" % i, [128, 1024], BF16) for i in range(2)]; RBA = [Res("bfa%d" % i) for i in range(2)]
        bfb = [sb("bfb%d" % i, [128, 1024], BF16) for i in range(2)]; RBFB = [Res("bfb%d" % i) for i in range(2)]
        bns = [sb("bns%d" % i, [128, 4, 6]) for i in range(2)]; RBN = [Res("bns%d" % i) for i in range(2)]
        mvs = [sb("mvs%d" % i, [128, 4, 2]) for i in range(2)]; RMV = [Res("mvs%d" % i) for i in range(2)]
        sm4 = [sb("sm4_%d" % i, [128, 3, 4]) for i in range(2)]; RSM = [Res("sm4_%d" % i) for i in range(2)]
        stage = [sb("stage%d" % i, [128, 512]) for i in range(2)]; RSG = [Res("stage%d" % i) for i in range(2)]
        ctr = {"xin": 0, "st": 0, "t": 0, "fa": 0, "ba": 0, "bb": 0, "bn": 0, "sm": 0, "sg": 0, "sb": 0}

        held = set()

        def rot(key, n):
            v = ctr[key]
            for _ in range(n):
                if (key, v) not in held:
                    break
                v = (v + 1) % n
            else:
                raise RuntimeError("ring full: " + key)
            ctr[key] = (v + 1) % n
            return v

        def hold(key, v):
            held.add((key, v))

        def drop(key, v):
            held.discard((key, v))

        sbin = [sb("sbin%d" % i, [128, 4, 256]) for i in range(2)]; RSI = [Res("sbin%d" % i) for i in range(2)]
        sbbf = [sb("sbbf%d" % i, [128, 4, 256], BF16) for i in range(2)]; RSBB = [Res("sbbf%d" % i) for i in range(2)]
        _sa = [sbin[i][:].rearrange("p a b -> p (a b)").bitcast(BF16) for i in range(2)]
        ckb = [_sa[i][:, 0:1024].rearrange("p (t c) -> p t c", t=2) for i in range(2)]; RCK = RSI
        cvb = [_sa[i][:, 1024:2048].rearrange("p (t c) -> p t c", t=2) for i in range(2)]; RCV = RSI
        kcT = sbbf; RKT = RSBB
        tailbuf = sb("tailbuf", [128, 10, 512], BF16)
        usg = tailbuf[:, 0:5]; vnb = tailbuf[:, 5:10]
        _tf = tailbuf[:].rearrange("p a b -> p (a b)").bitcast(F32)
        prevT = _tf[:, 0:1408].rearrange("p (c n) -> p c n", c=NFC); RPV = Res("prevT")
        B567 = sb("B567", [128, 16, HCOLS], BF16)
        B5 = B567[:, 0:8]; B6 = B567[:, 8:12]; B7 = B567[:, 12:16]
        zkeep = B567[:, 8:16].rearrange("p a b -> p (a b)").bitcast(F32)[:, 0:1408].rearrange("p (c n) -> p c n", c=NFC); RZK = Res("zkeep")
        zfull = [sb("zfull%d" % i, [128, 16, 6]) for i in range(2)]; RZF = [Res("zfull%d" % i) for i in range(2)]
        ysm = [sb("ysm%d" % i, [128, 3, 64]) for i in range(2)]; RYS = [Res("ysm%d" % i) for i in range(2)]
        gas = [sb("gas%d" % i, [128, 64]) for i in range(2)]; RGS = [Res("gas%d" % i) for i in range(2)]
        gatedT = bigB[:].rearrange("p a b -> p (a b)")[:, 0:22 * 576].rearrange("p (f t) -> p f t", f=22)

        def transposes(src_fn, n, rows, dst_ap_fn, rsrc, rdst, dt=BF16, evac="act"):
            _, pb, rb = banks(1)
            if dt == BF16:
                pv = pb.bitcast(BF16)[:, 0:n * 128].rearrange("p (n r) -> p n r", n=n)
                ident = identb
            else:
                pv = pb[:, 0:n * 128].rearrange("p (n r) -> p n r", n=n)
                ident = identf
            for j in range(n):
                src = src_fn(j)
                P.op("pe", lambda e, o=pv[:, j, 0:rows], s=src, idn=ident[:rows, :rows]: e.transpose(o, s, idn),
                     r=list(rsrc) + [RC], w=rb)
            dst = dst_ap_fn()
            if evac == "act":
                P.op("act", lambda e, d=dst, s=pv[:, :, 0:rows]: e.copy(out=d, in_=s), r=rb, w=rdst)
            else:
                P.op("dve", lambda e, d=dst, s=pv[:, :, 0:rows]: e.tensor_copy(out=d, in_=s), r=rb, w=rdst)
            return pv, rb

        def rstd_from(ssum_ap, out_ap, rres, scale):
            P.op("dve", lambda e: e.tensor_scalar(out=out_ap, in0=ssum_ap, scalar1=scale, scalar2=EPS, op0=ALU.mult, op1=ALU.add), r=[rres], w=[rres])
            P.op("act", lambda e: e.sqrt(out=out_ap, in_=out_ap), r=[rres], w=[rres])
            P.op("dve", lambda e: e.reciprocal(out=out_ap, in_=out_ap), r=[rres], w=[rres])

        def rmsnorm_to_hT(src_ap, rsrc, rows, col0, gtab, dstT, rdst):
            k = rot("st", 4)
            st, rst = st1[k], RST[k]
            b = rot("sb", 2)
            P.op("act", lambda e: e.activation(out=xsb[b][:rows], in_=src_ap, func=AF.Square, accum_out=st[:rows, 0:1]), r=[rsrc], w=[RXS[b], rst], multi=True)
            rstd_from(st[:rows, 0:1], st[:rows, 1:2], rst, 1.0 / D)
            P.op("dve", lambda e: e.tensor_scalar(out=xsb[b][:rows], in0=src_ap, scalar1=st[:rows, 1:2], scalar2=None, op0=ALU.mult), r=[rsrc, rst], w=[RXS[b]])
            _, pb, rb = banks(1)
            pv = pb.bitcast(BF16).rearrange("p (n r) -> p n r", n=8)
            for j in range(8):
                P.op("pe", lambda e, o=pv[:, j, 0:rows], s=xsb[b][:rows, j * 128:(j + 1) * 128]: e.transpose(o, s, identb[:rows, :rows]), r=[RXS[b], RC], w=rb)
            P.op("dve", lambda e: e.tensor_tensor(out=dstT[:, :, col0:col0 + rows], in0=pv[:, :, 0:rows], in1=gtab[:].unsqueeze(2).broadcast_to([128, 8, rows]), op=ALU.mult), r=rb + [RC], w=[rdst])

        def proj(slab, rslab, nk, srcT, rsrc, col0, rows, ncols=512, scol=0):
            _, pb, rb = banks(1)
            for k in range(nk):
                P.op("pe", lambda e, k=k: e.matmul(pb[:rows, 0:ncols], srcT[:, k, col0:col0 + rows], slab[:, k, scol:scol + ncols], start=(k == 0), stop=(k == nk - 1)),
                     r=[rslab] + list(rsrc), w=rb)
            return pb, rb

        STOP = os.environ.get("KSTOP", "")

        def ckpt(name):
            if STOP == name:
                raise _Stop()

        try:
            ckpt('c0')
            hTm = B5
            RHM = Res("hTm")
            for mt in range(2):
                if mt == 1:
                    ckpt('m0')
                P.dma("sp", xres[:, mt, :], dr["mem"][mt * 128:(mt + 1) * 128, :], RX[mt], w=[RX[mt]])
                rmsnorm_to_hT(xres[:, mt, :], RX[mt], 128, mt * 128, cst["p_gmem"], hTm, RHM)
            ckpt('m1')
            for blk in range(2):
                if blk == 1:
                    ckpt('m2')
                slab, rsl = next_slab("w_mem_kv")
                for mt in range(2):
                    pb, rb = proj(slab, rsl, 8, hTm, [RHM], mt * 128, 128)
                    ckpt('m1a')
                    s = rot("sg", 2)
                    P.op("act", lambda e, s=s, pb=pb: e.copy(out=stage[s][:, 0:512], in_=pb[:, 0:512]), r=rb, w=[RSG[s]])
                    P.dma("sp", dr["o_mk" if blk == 0 else "o_mv"][mt * 128:(mt + 1) * 128, :], stage[s][:, 0:512], RSG[s], r=[RSG[s]], store=True)
                    ckpt('m1b')
                    if blk == 0:
                        a = rot("ba", 2)
                        P.op("dve", lambda e, a=a, pb=pb: e.tensor_copy(out=bfa[a][:, 0:512], in_=pb[:, 0:512]), r=rb, w=[RBA[a]])
                        ckpt('m1c')
                        transposes(lambda j, a=a: bfa[a][:, j * 128:(j + 1) * 128], 4, 128,
                                   lambda mt=mt: kmT[:, :, mt * 128:(mt + 1) * 128], [RBA[a]], [RKM])
                    else:
                        P.op("dve", lambda e, pb=pb, mt=mt: e.tensor_copy(out=vmb[:, mt, :], in_=pb[:, 0:512]), r=rb, w=[RVM])

            RSO = [Res("sbout%d" % i) for i in range(2)]
            RUS = [Res("usg%d" % i) for i in range(5)]
            RCKO = [Res("cko%d" % i) for i in range(2)]
            RVN = [Res("vnb%d" % i) for i in range(5)]
            xqs = sb("xqs", [128, 4, 64], BF16); RXQS = Res("xqs")
            zlast = sb("zlast", [128, NFC, 2]); RZL = Res("zlast")
            RKTM = RBB[4]; RQKT = RBB[1]; RV = RBB[2]; RGG = RBB[3]; RORT = RBB[5]; ROST = RBB[6]; ROXT = RBB[7]
            RMG = [Res("merged%d" % i) for i in range(5)]; RMT = [Res("mergedT%d" % i) for i in range(5)]
            RGT = [Res("gated%d" % f) for f in range(22)]
            merged = vtm
            mergedT = bigB[:, 0:5].rearrange("p a b -> p (a b)")[:, 0:8 * HCOLS].rearrange("p (k c) -> p k c", k=8)
            maskT = cst["c_maskT"]; maskS = cst["c_maskS"]; bmpart = cst["c_bmpart"]

            def tr(src_fn, n, sp_, sf, dst, rsrc, rdst, dt=BF16, evac="act"):
                _, pb, rb = banks(1)
                if dt == BF16:
                    pv = pb.bitcast(BF16)[:sf, 0:n * sp_].rearrange("p (n r) -> p n r", n=n)
                    ident = identb
                else:
                    pv = pb[:sf, 0:n * sp_].rearrange("p (n r) -> p n r", n=n)
                    ident = identf
                for j in range(n):
                    if dt == BF16:
                        P.op("pe", lambda e, o=pv[:, j, :], s=src_fn(j), idn=ident[:sp_, :sp_]: e.transpose(o, s, idn), r=list(rsrc) + [RC], w=rb)
                    else:
                        P.op("pe", lambda e, o=pv[:, j, :], s=src_fn(j), idn=ident[:sp_, :sp_]: e.matmul(o, s, idn, start=True, stop=True), r=list(rsrc) + [RC], w=rb)
                if evac == "act":
                    P.op("act", lambda e: e.copy(out=dst, in_=pv), r=rb, w=rdst)
                else:
                    P.op("dve", lambda e: e.tensor_copy(out=dst, in_=pv), r=rb, w=rdst)

            def rstd2(in_ap, out_ap, rin, rout, scale):
                P.op("dve", lambda e: e.tensor_scalar(out=out_ap, in0=in_ap, scalar1=scale, scalar2=EPS, op0=ALU.mult, op1=ALU.add), r=[rin], w=[rout])
                P.op("act", lambda e: e.sqrt(out=out_ap, in_=out_ap), r=[rout], w=[rout])
                P.op("dve", lambda e: e.reciprocal(out=out_ap, in_=out_ap), r=[rout], w=[rout])

            def run_pipe(gens, depth):
                pending = list(gens)
                active = []
                while pending or active:
                    if pending and len(active) < depth:
                        active.append(pending.pop(0))
                    for g_ in list(active):
                        try:
                            next(g_)
                        except StopIteration:
                            active.remove(g_)

            def ret_finish(i, t, po, ro, hstride, rel):
                r = t["rows"]; c0 = t["col0"]
                n = rot("bn", 2); hold("bn", n)
                for h in range(4):
                    P.op("dve", lambda e, h=h: e.bn_stats(out=bns[n][:r, h, :], in_=po[:r, h * hstride:h * hstride + 256]), r=ro, w=[RBN[n]])
                for h in range(4):
                    P.op("dve", lambda e, h=h: e.bn_aggr(out=mvs[n][:r, h, :], in_=bns[n][:r, h, :]), r=[RBN[n]], w=[RMV[n]])
                m = rot("sm", 2); hold("sm", m)
                rstd2(mvs[n][:r, :, 1], sm4[m][:r, 0, :], RMV[n], RSM[m], 1.0)
                yield
                b = rot("fa", 2); hold("fa", b)
                for h in range(4):
                    P.op("dve", lambda e, h=h: e.scalar_tensor_tensor(out=f32a[b][:r, h * 256:(h + 1) * 256], in0=po[:r, h * hstride:h * hstride + 256],
                         scalar=mvs[n][:r, h, 0:1], in1=ggb[:r, i, h * 256:(h + 1) * 256], op0=ALU.subtract, op1=ALU.mult), r=ro + [RMV[n], RGG[i]], w=[RFA[b]])
                release(*rel)
                drop("bn", n)
                yield
                c = rot("bb", 2); hold("bb", c)
                for h in range(4):
                    P.op("act", lambda e, h=h: e.activation(out=bfb[c][:r, h * 256:(h + 1) * 256], in_=f32a[b][:r, h * 256:(h + 1) * 256], func=AF.Copy,
                         scale=sm4[m][:r, 0, h:h + 1]), r=[RFA[b], RSM[m]], w=[RBFB[c]])
                drop("sm", m); drop("fa", b)
                yield
                tr(lambda j: bfb[c][:r, j * 128:(j + 1) * 128], 8, r, 128, B5[:, :, c0:c0 + r], [RBFB[c]], [RORT[i]], evac="dve")
                drop("bb", c)

            ckpt('pre')
            for st_i in range(NST):
                tiles = [dict(kind="p", rows=128, g=st_i * 4 + i, col0=2 + 128 * i, tok0=st_i * TS + 128 * i) for i in range(4)]
                if st_i == 0:
                    tiles.append(dict(kind="s", rows=NSAMP, g=16, col0=2 + TS, tok0=0))
                NT = len(tiles)
                last_st = (st_i == NST - 1)

                if st_i == 0:
                    P.dma("sp", cs[:, 0:4], dr["c_cs"][:, 0:4], RCS, w=[RCS])
                    P.dma("sp", cs[:, 4:5], dr["c_cs"][:, 16:17], RCS, w=[RCS])
                for i, t in enumerate(tiles):
                    r = t["rows"]
                    src = dr["x_p"][t["tok0"]:t["tok0"] + 128, :] if t["kind"] == "p" else dr["x_s"][:, :]
                    if st_i == 0:
                        P.dma("sp", xres[:r, i, :], src, RX[i], w=[RX[i]])
                    rmsnorm_to_hT(xres[:r, i, :], RX[i], r, t["col0"], cst["p_gmix"], hT, RH[i])

                if st_i == 0: ckpt('p0')
                for blk in range(2):
                    slab, rsl = next_slab("w_in")
                    for i, t in enumerate(tiles):
                        r = t["rows"]; v = 0 if t["kind"] == "p" else 1
                        pb, rb = proj(slab, rsl, 8, hT, [RH[i]], t["col0"], r)
                        psv = pb[:r].rearrange("p (h t d) -> p h t d", h=4, t=2)
                        k1 = rot("t", 2)
                        a1 = t1[k1][:r].rearrange("p (h t d) -> p h t d", h=4, t=2)
                        a2 = t2[k1][:r].rearrange("p (h t d) -> p h t d", h=4, t=2)
                        cosb = cs[:r, i, 0, :].unsqueeze(1).unsqueeze(1).broadcast_to([r, 4, 2, 64])
                        sinb = cs[:r, i, 1, :].unsqueeze(1).broadcast_to([r, 4, 64])
                        P.op("dve", lambda e, a1=a1, psv=psv, cosb=cosb: e.tensor_tensor(out=a1, in0=psv, in1=cosb, op=ALU.mult), r=rb + [RCS], w=[RT1[k1]])
                        P.op("dve", lambda e, a2=a2, psv=psv, sinb=sinb: e.scalar_tensor_tensor(out=a2[:, :, 0, :], in0=psv[:, :, 1, :], scalar=-1.0, in1=sinb, op0=ALU.mult, op1=ALU.mult), r=rb + [RCS], w=[RT2[k1]])
                        P.op("dve", lambda e, a2=a2, psv=psv, sinb=sinb: e.tensor_tensor(out=a2[:, :, 1, :], in0=psv[:, :, 0, :], in1=sinb, op=ALU.mult), r=rb + [RCS], w=[RT2[k1]])
                        P.op("dve", lambda e, k1=k1, r=r: e.tensor_tensor(out=t1[k1][:r], in0=t1[k1][:r], in1=t2[k1][:r], op=ALU.add), r=[RT1[k1], RT2[k1]], w=[RT1[k1]])
                        decb = cst["c_dec"][:r, v, blk * 4:(blk + 1) * 4].unsqueeze(2).broadcast_to([r, 4, 128])
                        if blk == 0:
                            dsta = usg[:r, i, :]; rd = RUS[i]
                        else:
                            dsta = ktm[:r, i, :]; rd = RKTM[i]
                        P.op("dve", lambda e, k1=k1, r=r, decb=decb, dsta=dsta: e.tensor_tensor(
                            out=dsta.rearrange("p (h d) -> p h d", h=4),
                            in0=t1[k1][:r].rearrange("p (h d) -> p h d", h=4), in1=decb, op=ALU.mult), r=[RT1[k1], RC], w=[rd])
                for blk in range(2):
                    slab, rsl = next_slab("w_in")
                    for i, t in enumerate(tiles):
                        r = t["rows"]
                        pb, rb = proj(slab, rsl, 8, hT, [RH[i]], t["col0"], r)
                        P.op("act", lambda e, pb=pb, r=r, i=i, blk=blk: e.copy(out=vtm[:r, i, blk * 512:(blk + 1) * 512], in_=pb[:r, 0:512]), r=rb, w=[RV[i]])
                for blk in range(2):
                    slab, rsl = next_slab("w_in")
                    for i, t in enumerate(tiles):
                        r = t["rows"]
                        pb, rb = proj(slab, rsl, 8, hT, [RH[i]], t["col0"], r)
                        k1 = rot("t", 2)
                        P.op("act", lambda e, pb=pb, r=r, k1=k1: e.activation(out=t1[k1][:r], in_=pb[:r, 0:512], func=AF.Silu), r=rb, w=[RT1[k1]])
                        P.op("dve", lambda e, r=r, k1=k1, i=i, blk=blk: e.tensor_tensor(out=ggb[:r, i, blk * 512:(blk + 1) * 512], in0=t1[k1][:r],
                             in1=cst["p_gn"][:r, blk * 512:(blk + 1) * 512], op=ALU.mult), r=[RT1[k1], RC], w=[RGG[i]])
                if st_i == 0:
                    P.op("dve", lambda e: e.memset(Sf[:], 0.0), w=[RSF])
                def ret_gen(i, t):
                    r = t["rows"]
                    tr(lambda j, i=i, r=r: (usg if j < 4 else ktm)[:r, i, (j % 4) * 128:(j % 4 + 1) * 128], 8, r, 128,
                       qkT[:, i, :].rearrange("p (n c) -> p n c", n=8)[:, :, 0:r], [RUS[i], RKTM[i]], [RQKT[i]])
                    yield
                    qT = lambda h, i=i, r=r: qkT[:, i, h * 128:h * 128 + r]
                    kT = lambda h, i=i, r=r: qkT[:, i, (4 + h) * 128:(4 + h) * 128 + r]
                    _, pb, rb = banks(1)
                    for h in range(4):
                        P.op("pe", lambda e, h=h, pb=pb, r=r, qT=qT, kT=kT: e.matmul(pb[:r, h * r:(h + 1) * r], kT(h), qT(h), start=True, stop=True), r=[RQKT[i]], w=rb)
                    a = rot("ba", 2); hold("ba", a)
                    msk = (maskT[:r, :r] if t["kind"] == "p" else maskS[:r, :r]).unsqueeze(1).broadcast_to([r, 4, r])
                    P.op("dve", lambda e, a=a, pb=pb, r=r, msk=msk: e.tensor_tensor(out=bfa[a][:r, 0:4 * r].rearrange("p (h c) -> p h c", h=4),
                         in0=pb[:r, 0:4 * r].rearrange("p (h c) -> p h c", h=4), in1=msk, op=ALU.mult), r=rb + [RC], w=[RBA[a]])
                    yield
                    if t["kind"] == "p":
                        first = (t["g"] == 0)
                        pob0, po, ro = banks(2, reserve=True)
                        for h in range(4):
                            P.op("pe", lambda e, h=h, po=po, a=a, i=i, first=first: e.matmul(po[:, h * 256:(h + 1) * 256], bfa[a][:, h * 128:(h + 1) * 128],
                                 vtm[:, i, h * 256:(h + 1) * 256], start=True, stop=first), r=[RBA[a], RV[i]], w=ro)
                            if not first:
                                P.op("pe", lambda e, h=h, po=po, qT=qT: e.matmul(po[:, h * 256:(h + 1) * 256], qT(h), Sbf[:, h, :], start=False, stop=True), r=[RQKT[i], RSB], w=ro)
                        drop("ba", a)
                        _, pS, rS = banks(2)
                        for h in range(4):
                            P.op("pe", lambda e, h=h, pS=pS, i=i: e.matmul(pS[:, h * 256:(h + 1) * 256], ktm[:, i, h * 128:(h + 1) * 128], vtm[:, i, h * 256:(h + 1) * 256],
                                 start=True, stop=True), r=[RKTM[i], RV[i]], w=rS)
                        gcb = cst["c_gc"][:, 0, :].unsqueeze(2).broadcast_to([128, 4, 256])
                        P.op("dve", lambda e, pS=pS: e.tensor_tensor(out=Sf[:], in0=pS.rearrange("p (h c) -> p h c", h=4), in1=Sf[:], op=ALU.add), r=rS + [RSF], w=[RSF])
                        P.op("dve", lambda e, gcb=gcb: e.tensor_tensor(out=Sf[:], in0=Sf[:], in1=gcb, op=ALU.mult), r=[RSF, RC], w=[RSF])
                        if not (last_st and i == 3):
                            P.op("act", lambda e: e.copy(out=Sbf[:], in_=Sf[:]), r=[RSF], w=[RSB])
                        else:
                            P.dma("sp", dr["o_ret_p"].rearrange("h d e -> d h e"), Sf[:], RSF, r=[RSF], store=True)
                        yield
                        yield from ret_finish(i, t, po, ro, 256, (pob0, 2))
                    else:
                        pob0, po, ro = banks(4, reserve=True)
                        for h in range(4):
                            P.op("pe", lambda e, h=h, po=po, a=a, i=i: e.matmul(po[:64, h * 512:h * 512 + 256], bfa[a][:64, h * 64:(h + 1) * 64],
                                 vtm[:64, i, h * 256:(h + 1) * 256], start=True, stop=False, skip_group_check=True), r=[RBA[a], RV[i]], w=ro)
                        drop("ba", a)
                        gcs = cst["c_gc"][:, 1, :].unsqueeze(2).broadcast_to([128, 4, 256])
                        sbout = [B567[:, 8 + 4 * q:12 + 4 * q].rearrange("p a b -> p (a b)").bitcast(F32)[:, 0:1024].rearrange("p (h c) -> p h c", h=4) for q in range(2)]
                        RSOA = [ROST, ROXT]
                        P.dma("sp", sbin[0][:], dr["st_ret"][0].rearrange("h d e -> d h e"), RSI[0], w=[RSI[0]])
                        for b in range(NB):
                            k = b % 2
                            if b + 1 < NB:
                                P.dma("sp", sbin[1 - k][:], dr["st_ret"][b + 1].rearrange("h d e -> d h e"), RSI[1 - k], w=[RSI[1 - k]])
                            P.op("act", lambda e, k=k: e.copy(out=sbbf[k][:], in_=sbin[k][:]), r=[RSI[k]], w=[RSBB[k]])
                            c = rot("bb", 2)
                            P.op("dve", lambda e, c=c, b=b, i=i: e.tensor_tensor(out=bfb[c][:, 0:256].rearrange("p (h c) -> p h c", h=4),
                                 in0=qkT[:, i, 0:512].rearrange("p (h c) -> p h c", h=4)[:, :, 0:64], in1=bmfree[:, b, :].unsqueeze(1).broadcast_to([128, 4, 64]), op=ALU.mult),
                                 r=[RQKT[i], RC], w=[RBFB[c]])
                            for h in range(4):
                                P.op("pe", lambda e, h=h, po=po, c=c, k=k, b=b: e.matmul(po[:64, h * 512:h * 512 + 256], bfb[c][:, h * 64:(h + 1) * 64], sbbf[k][:, h, :],
                                     start=False, stop=(b == NB - 1), skip_group_check=True), r=[RBFB[c], RSBB[k]], w=ro)
                            a2 = rot("ba", 2)
                            P.op("dve", lambda e, a2=a2, b=b, i=i: e.tensor_scalar(out=bfa[a2][:64, 0:512], in0=ktm[:64, i, :], scalar1=bmpart[:64, b:b + 1], scalar2=None, op0=ALU.mult),
                                 r=[RKTM[i], RC], w=[RBA[a2]])
                            _, pS, rS = banks(2)
                            for h in range(4):
                                P.op("pe", lambda e, h=h, pS=pS, a2=a2, i=i: e.matmul(pS[:, h * 256:(h + 1) * 256], bfa[a2][:64, h * 128:(h + 1) * 128], vtm[:64, i, h * 256:(h + 1) * 256],
                                     start=True, stop=True), r=[RBA[a2], RV[i]], w=rS)
                            P.op("dve", lambda e, pS=pS, k=k, sbout=sbout: e.tensor_tensor(out=sbout[k], in0=pS.rearrange("p (h c) -> p h c", h=4), in1=sbin[k][:], op=ALU.add), r=rS + [RSI[k]], w=RSOA[k])
                            P.op("dve", lambda e, k=k, gcs=gcs, sbout=sbout: e.tensor_tensor(out=sbout[k], in0=sbout[k], in1=gcs, op=ALU.mult), r=RSOA[k] + [RC], w=RSOA[k])
                            P.dma("sp", dr["o_ret_s"][b].rearrange("h d e -> d h e"), sbout[k], RSO[k], r=RSOA[k], store=True)
                            yield
                        yield from ret_finish(i, t, po, ro, 512, (pob0, 4))

                order = [(i, t) for i, t in enumerate(tiles) if t["kind"] == "s"] + [(i, t) for i, t in enumerate(tiles) if t["kind"] == "p"]
                run_pipe([ret_gen(i, t) for i, t in order], 2)

                if st_i == 0: ckpt('R')
                slab, rsl = next_slab("w_in")
                for i, t in enumerate(tiles):
                    r = t["rows"]
                    pb, rb = proj(slab, rsl, 8, hT, [RH[i]], t["col0"], r)
                    P.op("act", lambda e, pb=pb, r=r, i=i: e.activation(out=usg[:r, i, :], in_=pb[:r, 0:512], func=AF.Gelu_apprx_tanh), r=rb, w=[RUS[i]])
                slab, rsl = next_slab("w_in")

                def sgu_gen(i, t, slab=slab, rsl=rsl):
                    r = t["rows"]; c0 = t["col0"]
                    pb, rb = proj(slab, rsl, 8, hT, [RH[i]], t["col0"], r)
                    k1 = rot("t", 2); hold("t", k1)
                    P.op("act", lambda e, pb=pb, r=r, k1=k1: e.activation(out=t1[k1][:r], in_=pb[:r, 0:512], func=AF.Gelu_apprx_tanh), r=rb, w=[RT1[k1]])
                    yield
                    n = rot("bn", 2)
                    P.op("dve", lambda e, n=n, r=r, k1=k1: e.bn_stats(out=bns[n][:r, 0, :], in_=t1[k1][:r]), r=[RT1[k1]], w=[RBN[n]])
                    P.op("dve", lambda e, n=n, r=r: e.bn_aggr(out=mvs[n][:r, 0, :], in_=bns[n][:r, 0, :]), r=[RBN[n]], w=[RMV[n]])
                    m = rot("sm", 2)
                    rstd2(mvs[n][:r, 0, 1:2], sm4[m][:r, 0, 0:1], RMV[n], RSM[m], 1.0)
                    P.op("dve", lambda e, n=n, m=m, r=r, k1=k1: e.tensor_scalar(out=t2[k1][:r], in0=t1[k1][:r], scalar1=mvs[n][:r, 0, 0:1], scalar2=sm4[m][:r, 0, 0:1],
                         op0=ALU.subtract, op1=ALU.mult), r=[RT1[k1], RMV[n], RSM[m]], w=[RT2[k1]])
                    yield
                    P.op("dve", lambda e, r=r, k1=k1, i=i: e.tensor_tensor(out=vnb[:r, i, :], in0=t2[k1][:r], in1=cst["p_ln"][:r], op=ALU.mult), r=[RT2[k1], RC], w=[RVN[i]])
                    if t["kind"] == "s":
                        s = rot("sg", 2)
                        P.op("dve", lambda e, r=r, k1=k1, s=s: e.tensor_tensor(out=stage[s][:r, 0:512], in0=t2[k1][:r], in1=cst["p_ln"][:r], op=ALU.mult), r=[RT2[k1], RC], w=[RSG[s]])
                        P.dma("sp", dr["o_sgv"], stage[s][:r, 0:512], RSG[s], r=[RSG[s]], store=True)
                    drop("t", k1)
                    pmb0, pm, rm = banks(1, reserve=True)
                    for g in range(4):
                        lw = wtb[:, g, :] if t["kind"] == "p" else wsb[:64, g, :]
                        P.op("pe", lambda e, g=g, pm=pm, lw=lw, r=r, i=i: e.matmul(pm[:r, g * 128:(g + 1) * 128], lw, vnb[:r, i, g * 128:(g + 1) * 128], start=True, stop=True),
                             r=[RC, RVN[i]], w=rm)
                    yield
                    a = rot("ba", 2); hold("ba", a)
                    vv = 0 if t["kind"] == "p" else 1
                    for g in range(4):
                        P.op("dve", lambda e, g=g, pm=pm, r=r, i=i, a=a, vv=vv: e.scalar_tensor_tensor(out=bfa[a][:r, g * 128:(g + 1) * 128], in0=pm[:r, g * 128:(g + 1) * 128],
                             scalar=cst["p_sgb"][:r, vv, g:g + 1], in1=usg[:r, i, g * 128:(g + 1) * 128], op0=ALU.add, op1=ALU.mult), r=rm + [RC, RUS[i]], w=[RBA[a]])
                    release(pmb0, 1)
                    yield
                    tr(lambda j, a=a, r=r: bfa[a][:r, j * 128:(j + 1) * 128], 4, r, 128, B6[:, :, c0:c0 + r], [RBA[a]], [ROST[i]])
                    drop("ba", a)


                if st_i == 0: ckpt('S')
                slab, rsl = next_slab("w_in", live=2)

                def xat_gen(i, t, slab=slab, rsl=rsl):
                    r = t["rows"]; c0 = t["col0"]
                    pb, rb = proj(slab, rsl, 8, hT, [RH[i]], t["col0"], r)
                    a = rot("ba", 2)
                    P.op("act", lambda e, pb=pb, r=r, a=a: e.activation(out=bfa[a][:r, 0:512], in_=pb[:r, 0:512], func=AF.Copy, scale=float(128.0 ** -0.5)), r=rb, w=[RBA[a]])
                    yield
                    if t["kind"] == "p":
                        xb = rot("sb", 2); hold("sb", xb)
                        xql = xsb[xb][:, 0:512].rearrange("p (h c) -> p h c", h=4); rxq = RXS[xb]
                    else:
                        xql = xqs[:]; rxq = RXQS
                    tr(lambda j, a=a, r=r: bfa[a][:r, j * 128:(j + 1) * 128], 4, r, 128, xql, [RBA[a]], [rxq])
                    yield
                    if t["kind"] == "p":
                        psb0, ps_, rs_ = banks(2, reserve=True)
                        hs = 256
                        for h in range(4):
                            P.op("pe", lambda e, h=h, ps_=ps_, c0=c0: e.matmul(ps_[:, h * 256:(h + 1) * 256], xql[:, h, :], kmT[:, h, :], start=True, stop=True), r=[rxq, RKM], w=rs_)
                    else:
                        psb0, ps_, rs_ = banks(4, reserve=True)
                        hs = 512
                        for b in range(NB):
                            k = b % 2
                            P.dma("pool", ckb[k], dr["ck"][b].rearrange("(t p) c -> p t c", p=128), RCKO[k], w=[RCK[k]])
                            tr(lambda j, k=k: ckb[k][:, j % 2, (j // 2) * 128:(j // 2 + 1) * 128], 8, 128, 128, kcT[k][:].rearrange("p h (t m) -> p (h t) m", t=2), [RCK[k]], [RKT[k]])
                            c = rot("bb", 2)
                            P.op("dve", lambda e, c=c, b=b, c0=c0: e.tensor_tensor(out=bfb[c][:, 0:256].rearrange("p (h c) -> p h c", h=4), in0=xql,
                                 in1=bmfree[:, b, :].unsqueeze(1).broadcast_to([128, 4, 64]), op=ALU.mult), r=[rxq, RC], w=[RBFB[c]])
                            for h in range(4):
                                P.op("pe", lambda e, h=h, ps_=ps_, c=c, k=k, b=b: e.matmul(ps_[:64, h * 512:h * 512 + 256], bfb[c][:, h * 64:(h + 1) * 64], kcT[k][:, h, :],
                                     start=(b == 0), stop=(b == NB - 1), skip_group_check=True), r=[RBFB[c], RKT[k]], w=rs_)
                            yield
                    if t["kind"] == "p":
                        drop("sb", xb)
                    yield
                    m = rot("sm", 2); hold("sm", m)
                    psv = ps_[:r].rearrange("p (h c) -> p h c", h=4)[:, :, 0:256] if hs == 512 else ps_[:r].rearrange("p (h c) -> p h c", h=4)
                    P.op("dve", lambda e, m=m, psv=psv, r=r: e.tensor_reduce(out=sm4[m][:r, 0, :], in_=psv, axis=AX.X, op=ALU.max), r=rs_, w=[RSM[m]])
                    P.op("dve", lambda e, m=m, r=r: e.tensor_scalar(out=sm4[m][:r, 1, :], in0=sm4[m][:r, 0, :], scalar1=-1.0, scalar2=None, op0=ALU.mult), r=[RSM[m]], w=[RSM[m]])
                    c = rot("bb", 2); hold("bb", c)
                    for h in range(4):
                        P.op("act", lambda e, h=h, m=m, c=c, r=r, ps_=ps_, hs=hs: e.activation(out=bfb[c][:r, h * 256:(h + 1) * 256], in_=ps_[:r, h * hs:h * hs + 256], func=AF.Exp,
                             bias=sm4[m][:r, 1, h:h + 1], accum_out=sm4[m][:r, 2, h:h + 1]), r=rs_ + [RSM[m]], w=[RBFB[c], RSM[m]], multi=True)
                    release(psb0, 2 if t["kind"] == "p" else 4)
                    P.op("dve", lambda e, m=m, r=r: e.reciprocal(out=sm4[m][:r, 0, :], in_=sm4[m][:r, 2, :]), r=[RSM[m]], w=[RSM[m]])
                    yield
                    a = rot("ba", 2); hold("ba", a)
                    tr(lambda j, c=c, r=r: bfb[c][:r, j * 128:(j + 1) * 128], 8, r, 128, bfa[a][:, 0:8 * r].rearrange("p (n c) -> p n c", n=8), [RBFB[c]], [RBA[a]])
                    drop("bb", c)
                    yield
                    if t["kind"] == "p":
                        poxb0, pox, rox = banks(1, reserve=True)
                        for h in range(4):
                            for hf in range(2):
                                P.op("pe", lambda e, h=h, hf=hf, pox=pox, a=a: e.matmul(pox[:, h * 128:(h + 1) * 128], bfa[a][:, (h * 2 + hf) * 128:(h * 2 + hf + 1) * 128],
                                     vmb[:, hf, h * 128:(h + 1) * 128], start=(hf == 0), stop=(hf == 1)), r=[RBA[a], RVM], w=rox)
                        poxv = pox.rearrange("p (h c) -> p h c", h=4)
                    else:
                        poxb0, pox, rox = banks(4, reserve=True)
                        for b in range(NB):
                            k = b % 2
                            P.dma("pool", cvb[k], dr["cv"][b].rearrange("(t p) c -> p t c", p=128), RCKO[k], w=[RCV[k]])
                            c2 = rot("bb", 2)
                            P.op("dve", lambda e, c2=c2, a=a, b=b: e.tensor_tensor(out=bfb[c2][:, 0:512].rearrange("p (n c) -> p n c", n=8), in0=bfa[a][:, 0:512].rearrange("p (n c) -> p n c", n=8),
                                 in1=bmfree[:, b, :].unsqueeze(1).broadcast_to([128, 8, 64]), op=ALU.mult), r=[RBA[a], RC], w=[RBFB[c2]])
                            for h in range(4):
                                for hf in range(2):
                                    P.op("pe", lambda e, h=h, hf=hf, pox=pox, c2=c2, k=k, b=b: e.matmul(pox[:64, h * 512:h * 512 + 128], bfb[c2][:, (h * 2 + hf) * 64:(h * 2 + hf + 1) * 64],
                                         cvb[k][:, hf, h * 128:(h + 1) * 128], start=(b == 0 and hf == 0), stop=(b == NB - 1 and hf == 1), skip_group_check=True), r=[RBFB[c2], RCV[k]], w=rox)
                            yield
                        poxv = pox[:64].rearrange("p (h c) -> p h c", h=4)[:, :, 0:128]
                    drop("ba", a)
                    yield
                    a3 = rot("ba", 2); hold("ba", a3)
                    P.op("dve", lambda e, a3=a3, poxv=poxv, m=m, r=r: e.tensor_tensor(out=bfa[a3][:r, 0:512].rearrange("p (h c) -> p h c", h=4), in0=poxv[:r],
                         in1=sm4[m][:r, 0, :].unsqueeze(2).broadcast_to([r, 4, 128]), op=ALU.mult), r=rox + [RSM[m]], w=[RBA[a3]])
                    release(poxb0, 1 if t["kind"] == "p" else 4)
                    drop("sm", m)
                    yield
                    tr(lambda j, a3=a3, r=r: bfa[a3][:r, j * 128:(j + 1) * 128], 4, r, 128, B7[:, :, c0:c0 + r], [RBA[a3]], [ROXT[i]])
                    drop("ba", a3)

                xorder = [(i, t) for i, t in enumerate(tiles) if t["kind"] == "s"] + [(i, t) for i, t in enumerate(tiles) if t["kind"] == "p"]
                run_pipe([sgu_gen(i, t) for i, t in enumerate(tiles)], 2)
                run_pipe([xat_gen(i, t) for i, t in xorder], 2)

                if st_i == 0: ckpt('X')
                if DBG and st_i == 0:
                    P.dma("sp", dr["dbgT"][:, :, 2:HCOLS], B567[:, :, 2:HCOLS], Res("dbgT"), r=RORT + ROST + ROXT + [RZK], store=True)
                accs = [(f32a[0][:, 0:512], RFA[0]), (f32a[0][:, 512:1024], RFA[0]), (f32a[1][:, 0:512], RFA[1]), (f32a[1][:, 512:1024], RFA[1]), (stage[0][:, 0:512], RSG[0])]
                srcs = [(B5, RORT, 8), (B6, ROST, 4), (B7, ROXT, 4)]
                for j in range(2):
                    for br in range(3):
                        sT, rsT, nk = srcs[br]
                        pslab, prs = next_slab(("w_br_ret", "w_br_sg", "w_br_x")[br])
                        gsl, grs = next_slab("w_in", live=2)
                        for i, t in enumerate(tiles):
                            r = t["rows"]; c0 = t["col0"]
                            acc, racc = accs[i]
                            pP, rP = proj(pslab, prs, nk, sT, [rsT[i]], c0, r)
                            _, pG, rG = banks(1)
                            P.op("pe", lambda e, pG=pG, r=r, br=br, j=j: e.matmul(pG[:r, 0:512], ones1[0:1, 0:r], bgrow[0:1, br * 1024 + j * 512:br * 1024 + (j + 1) * 512], start=True, stop=False), r=[RC], w=rG)
                            for k in range(8):
                                P.op("pe", lambda e, k=k, pG=pG, r=r, gsl=gsl, c0=c0: e.matmul(pG[:r, 0:512], hT[:, k, c0:c0 + r], gsl[:, k, :], start=False, stop=(k == 7)), r=[grs, RH[i]], w=rG)
                            k1 = rot("t", 2)
                            P.op("act", lambda e, pG=pG, r=r, k1=k1: e.activation(out=t1[k1][:r], in_=pG[:r, 0:512], func=AF.Sigmoid), r=rG, w=[RT1[k1]])
                            if br == 0:
                                P.op("dve", lambda e, pP=pP, r=r, k1=k1, acc=acc: e.tensor_tensor(out=acc[:r], in0=pP[:r, 0:512], in1=t1[k1][:r], op=ALU.mult), r=rP + [RT1[k1]], w=[racc])
                            else:
                                P.op("dve", lambda e, pP=pP, r=r, k1=k1: e.tensor_tensor(out=t2[k1][:r], in0=pP[:r, 0:512], in1=t1[k1][:r], op=ALU.mult), r=rP + [RT1[k1]], w=[RT2[k1]])
                                if br == 1:
                                    P.op("dve", lambda e, r=r, k1=k1, acc=acc: e.tensor_tensor(out=acc[:r], in0=acc[:r], in1=t2[k1][:r], op=ALU.add), r=[racc, RT2[k1]], w=[racc])
                                else:
                                    P.op("dve", lambda e, r=r, k1=k1, acc=acc, i=i, j=j: e.tensor_tensor(out=merged[:r, i, j * 512:(j + 1) * 512], in0=acc[:r], in1=t2[k1][:r], op=ALU.add),
                                         r=[racc, RT2[k1]], w=[RMG[i]])
                for i, t in enumerate(tiles):
                    r = t["rows"]; c0 = t["col0"]
                    tr(lambda jj, i=i, r=r: merged[:r, i, jj * 128:(jj + 1) * 128], 8, r, 128, mergedT[:, :, c0:c0 + r], [RMG[i]], [RMT[i]])

                if st_i == 0: ckpt('G')
                if DBG and st_i == 0:
                    for i in range(5):
                        rr = 128 if i < 4 else 64
                        P.dma("sp", dr["dbgM"][:rr, i, :], merged[:rr, i, :], Res("dbgM%d" % i), r=[RMG[i]] + RGT, store=True)
                for j in range(2):
                    slab, rsl = next_slab("w_o")
                    for i, t in enumerate(tiles):
                        r = t["rows"]
                        pb, rb = proj(slab, rsl, 8, mergedT, [RMT[i]], t["col0"], r)
                        P.op("dve", lambda e, pb=pb, r=r, i=i, j=j: e.tensor_tensor(out=xres[:r, i, j * 512:(j + 1) * 512], in0=pb[:r, 0:512], in1=xres[:r, i, j * 512:(j + 1) * 512], op=ALU.add),
                             r=rb + [RX[i]], w=[RX[i]])

                if DBG == "x1" and st_i == 0:
                    for i in range(5):
                        rr = 128 if i < 4 else 64
                        P.dma("sp", dr["dbg"][:rr, i, :], xres[:rr, i, :], Res("dbg%d" % i), r=[RX[i]], store=True)
                if st_i == 0: ckpt('O')
                for i, t in enumerate(tiles):
                    rmsnorm_to_hT(xres[:t["rows"], i, :], RX[i], t["rows"], t["col0"], cst["p_gffn"], hT, RH[i])
                if st_i == 0:
                    P.op("dve", lambda e: e.memset(hT[:, :, 0:2], 0.0), w=[RHH])
                else:
                    P.op("dve", lambda e: e.tensor_copy(out=hT[:, :, 0:2], in_=hsave[:]), r=[RHS], w=[RHH])
                P.op("dve", lambda e: e.tensor_copy(out=hsave[:], in_=hT[:, :, TS:TS + 2]), r=[RH[3]], w=[RHS])
                if DBG and st_i == 0:
                    P.dma("sp", dr["dbgH"], hT[:, :, 514:578], Res("dbgH"), r=[RH[4]], store=True)
                if st_i == 0:
                    for grp in range(11):
                        s = rot("sg", 2)
                        P.dma("sp", stage[s][:32, 0:512], dr["st_conv"][:, grp * 512:(grp + 1) * 512], RSG[s], w=[RSG[s]])
                        tr(lambda jj, s=s: stage[s][:32, jj * 128:(jj + 1) * 128], 4, 32, 128, prevT[:, grp * 4:grp * 4 + 4, :], [RSG[s]], [RPV], dt=F32, evac="dve")
                ncol2 = 2 + (NSAMP if st_i == 0 else 0)
                RHALL = RH[:NT] + [RHH]
                cw = cst["p_convw"]; cb = cst["p_convb"]
                for m_ in range(6):
                    nch = 4 if m_ < 5 else 2
                    sa = next_slab("w_up"); sbb_ = next_slab("w_up", live=2)
                    for sub in range(nch):
                        fidx = m_ * 4 + sub
                        ga_buf = None
                        for half, (slab, rsl) in enumerate((sa, sbb_)):
                            ch = fidx + 22 * half
                            _, U, rU = banks(2)
                            for k in range(8):
                                P.op("pe", lambda e, k=k, U=U, slab=slab, sub=sub: e.matmul(U[:, 0:512], slab[:, k, sub * 128:(sub + 1) * 128], hT[:, k, 0:512], start=(k == 0), stop=(k == 7)), r=[rsl] + RHALL, w=rU)
                                P.op("pe", lambda e, k=k, U=U, slab=slab, sub=sub, ncol2=ncol2: e.matmul(U[:, 512:512 + ncol2], slab[:, k, sub * 128:(sub + 1) * 128], hT[:, k, 512:512 + ncol2], start=(k == 0), stop=(k == 7)), r=[rsl] + RHALL, w=rU)
                            fa = rot("fa", 2)
                            y = f32a[fa]
                            P.op("act", lambda e, U=U, y=y, ch=ch: e.activation(out=y[:, 0:512], in_=U[:, 2:514], func=AF.Identity, scale=cw[:, ch, 2:3], bias=cb[:, ch:ch + 1]), r=rU + [RC], w=[RFA[fa]])
                            P.op("dve", lambda e, U=U, y=y, ch=ch: e.scalar_tensor_tensor(out=y[:, 0:512], in0=U[:, 1:513], scalar=cw[:, ch, 1:2], in1=y[:, 0:512], op0=ALU.mult, op1=ALU.add), r=rU + [RC, RFA[fa]], w=[RFA[fa]])
                            P.op("dve", lambda e, U=U, y=y, ch=ch: e.scalar_tensor_tensor(out=y[:, 0:512], in0=U[:, 0:512], scalar=cw[:, ch, 0:1], in1=y[:, 0:512], op0=ALU.mult, op1=ALU.add), r=rU[0:1] + [RC, RFA[fa]], w=[RFA[fa]])
                            if last_st:
                                P.op("act", lambda e, U=U, ch=ch: e.copy(out=zlast[:, ch, :], in_=U[:, 512:514]), r=rU[1:2], w=[RZL])
                            if st_i == 0:
                                zf = rot("bn", 2)
                                P.op("act", lambda e, U=U, zf=zf: e.copy(out=zfull[zf][:, :, 2:6], in_=U[:, 514:578].rearrange("p (b t) -> p b t", b=16)), r=rU[1:2], w=[RZF[zf]])
                                P.op("act", lambda e, U=U, ch=ch: e.copy(out=zkeep[:, ch, :].rearrange("p (b t) -> p b t", b=16), in_=U[:, 514:578].rearrange("p (b t) -> p b t", b=16)[:, :, 2:4]), r=rU[1:2], w=[RZK])
                                P.op("dve", lambda e, zf=zf, ch=ch: e.tensor_copy(out=zfull[zf][:, :, 0:2], in_=prevT[:, ch, :].rearrange("p (b t) -> p b t", b=16)), r=[RPV], w=[RZF[zf]])
                                ys = ysm[zf]
                                ysv = lambda q, ys=ys: ys[:, q, :].rearrange("p (b t) -> p b t", b=16)
                                P.op("dve", lambda e, zf=zf, ch=ch, ysv=ysv: e.tensor_scalar(out=ysv(0), in0=zfull[zf][:, :, 2:6], scalar1=cw[:, ch, 2:3], scalar2=cb[:, ch:ch + 1], op0=ALU.mult, op1=ALU.add), r=[RZF[zf], RC], w=[RYS[zf]])
                                P.op("dve", lambda e, zf=zf, ch=ch, ysv=ysv: e.scalar_tensor_tensor(out=ysv(1), in0=zfull[zf][:, :, 1:5], scalar=cw[:, ch, 1:2], in1=ysv(0), op0=ALU.mult, op1=ALU.add), r=[RZF[zf], RC, RYS[zf]], w=[RYS[zf]])
                                P.op("dve", lambda e, zf=zf, ch=ch, ysv=ysv: e.scalar_tensor_tensor(out=ysv(2), in0=zfull[zf][:, :, 0:4], scalar=cw[:, ch, 0:1], in1=ysv(1), op0=ALU.mult, op1=ALU.add), r=[RZF[zf], RC, RYS[zf]], w=[RYS[zf]])
                            if half == 0:
                                ga = rot("t", 2)
                                P.op("act", lambda e, y=y, ga=ga: e.activation(out=t1[ga][:, 0:512], in_=y[:, 0:512], func=AF.Gelu_apprx_tanh), r=[RFA[fa]], w=[RT1[ga]])
                                if st_i == 0:
                                    P.op("act", lambda e, zf=zf, ys=ys: e.activation(out=gas[zf][:], in_=ys[:, 2, :], func=AF.Gelu_apprx_tanh), r=[RYS[zf]], w=[RGS[zf]])
                                    gz = zf
                            else:
                                P.op("pool", lambda e, y=y, ga=ga, fidx=fidx: e.tensor_tensor(out=gatedT[:, fidx, 0:512], in0=y[:, 0:512], in1=t1[ga][:, 0:512], op=ALU.mult), r=[RFA[fa], RT1[ga]], w=[RGT[fidx]])
                                if st_i == 0:
                                    P.op("dve", lambda e, ys=ys, gz=gz, fidx=fidx: e.tensor_tensor(out=gatedT[:, fidx, 512:576], in0=ys[:, 2, :], in1=gas[gz][:], op=ALU.mult), r=[RYS[zf], RGS[gz]], w=[RGT[fidx]])
                if DBG and st_i == 0:
                    P.dma("sp", dr["dbgZ"], zkeep, Res("dbgZ"), r=[RZK] + ROST + ROXT, store=True)
                if st_i == 0: ckpt('Fup')
                outs = []
                if st_i == 0:
                    outs.append((zkeep, RZK, 32, "o_conv_s"))
                if last_st:
                    outs.append((zlast, RZL, 2, "o_conv_p"))
                for zb, rz, nr, oname in outs:
                    for grp in range(11):
                        s = rot("sg", 2)
                        tr(lambda jj, grp=grp, zb=zb: zb[:, grp * 4 + jj, :], 4, 128, nr, stage[s][:nr, 0:512].rearrange("p (n c) -> p n c", n=4), [rz], [RSG[s]], dt=F32, evac="dve")
                        P.dma("sp", dr[oname][:, grp * 512:(grp + 1) * 512], stage[s][:nr, 0:512], RSG[s], r=[RSG[s]], store=True)
                if st_i == 0: ckpt('Fout')
                for j in range(2):
                    sl = [next_slab("w_down", live=q + 1) for q in range(3)]
                    for i, t in enumerate(tiles):
                        r = t["rows"]
                        gc0 = 128 * i if t["kind"] == "p" else TS
                        _, pb, rb = banks(1)
                        for f in range(22):
                            slab, rsl = sl[f // 8]
                            P.op("pe", lambda e, f=f, pb=pb, r=r, gc0=gc0, slab=slab: e.matmul(pb[:r, 0:512], gatedT[:, f, gc0:gc0 + r], slab[:, f % 8, :], start=(f == 0), stop=(f == 21)), r=[rsl, RGT[f]], w=rb)
                        P.op("dve", lambda e, pb=pb, r=r, i=i, j=j: e.tensor_tensor(out=xres[:r, i, j * 512:(j + 1) * 512], in0=pb[:r, 0:512], in1=xres[:r, i, j * 512:(j + 1) * 512], op=ALU.add),
                             r=rb + [RX[i]], w=[RX[i]])
                        if j == 0:
                            continue
                        k = rot("st", 4)
                        b2 = rot("sb", 2)
                        P.op("act", lambda e, r=r, i=i, k=k, b2=b2: e.activation(out=xsb[b2][:r], in_=xres[:r, i, :], func=AF.Square, accum_out=st1[k][:r, 0:1]), r=[RX[i]], w=[RXS[b2], RST[k]], multi=True)
                        rstd2(st1[k][:r, 0:1], st1[k][:r, 1:2], RST[k], RST[k], 1.0 / D)
                        P.op("dve", lambda e, r=r, i=i, k=k: e.scalar_tensor_tensor(out=xres[:r, i, :], in0=xres[:r, i, :], scalar=st1[k][:r, 1:2], in1=cst["p_gfin"][:r], op0=ALU.mult, op1=ALU.mult),
                             r=[RX[i], RST[k], RC], w=[RX[i]])
                        dst = dr["y_p"][t["tok0"]:t["tok0"] + 128, :] if t["kind"] == "p" else dr["y_s"][:, :]
                        P.dma("sp", dst, xres[:r, i, :], RX[i], r=[RX[i]], store=True)
                        if (not last_st) and t["kind"] == "p":
                            if i == 0:
                                P.dma("sp", cs[:, 0:4], dr["c_cs"][:, (st_i + 1) * 4:(st_i + 1) * 4 + 4], RCS, w=[RCS])
                            ntok = (st_i + 1) * TS + 128 * i
                            P.dma("sp", xres[:, i, :], dr["x_p"][ntok:ntok + 128, :], RX[i], w=[RX[i]])
                if st_i == 0: ckpt('Fdown')

            assert taken[0] == len(sched), (taken[0], len(sched))
        except _Stop:
            pass
        with nc.Block() as block:
            P.emit(block)
    return nc


_NC_CACHE = {}


def kernel(**inp):
    f = lambda k: np.ascontiguousarray(np.asarray(inp[k], dtype=np.float32))
    cst = _const_tables()
    par = {
        "p_gmix": _fm(f("norm_mix_g")[0]), "p_gffn": _fm(f("norm_ffn_g")[0]), "p_gmem": _fm(f("mem_norm_g")[0]),
        "p_bgate": np.ascontiguousarray(f("b_gate")[0].reshape(1, 3072)), "p_gn": _rep(f("ret_gn_g")[0].reshape(-1)), "p_ln": _rep(f("sg_ln_g")[0]),
        "p_gfin": _rep(f("norm_final_g")),
        "p_convw": np.ascontiguousarray(f("conv_w")[0].reshape(3, NFC, 128).transpose(2, 1, 0)),
        "p_convb": np.ascontiguousarray(f("conv_b")[0].reshape(NFC, 128).T),
        "p_wt": np.ascontiguousarray(f("sg_ws")[0].transpose(2, 0, 1)),
    }
    ws = f("sg_ws")[0]
    idx = np.arange(64) % 4
    wrep = np.zeros((128, 4, 64), np.float32)
    wrep[:64] = ws[:, idx[None, :], idx[:, None]].transpose(1, 0, 2)
    par["p_wrep"] = wrep
    sgb = np.zeros((128, 2, 4), np.float32)
    bs = f("sg_bs")[0]
    sgb[:, 0, :] = bs.T
    sgb[:, 1, :] = bs[:, np.arange(128) % 4].T
    par["p_sgb"] = sgb
    weights = {n: f(n)[0] for n, _ in WEIGHT_SPECS}
    xp, xs, mem = f("x_prompt"), f("x_sample"), f("mem_prompt")
    sr, sc, ck, cv = f("state_ret")[0], f("state_conv")[0], f("cache_mem_k")[0], f("cache_mem_v")[0]
    in_maps = []
    for c in range(NCORES):
        m = {"x_p": xp[c], "x_s": np.ascontiguousarray(xs[16 * c:16 * c + 16].reshape(NSAMP, D)), "mem": mem[c],
             "st_ret": sr[16 * c:16 * c + 16], "st_conv": np.ascontiguousarray(sc[16 * c:16 * c + 16].reshape(32, 2 * DFF)),
             "ck": np.ascontiguousarray(ck[16 * c:16 * c + 16].reshape(NB, 256, 512)), "cv": np.ascontiguousarray(cv[16 * c:16 * c + 16].reshape(NB, 256, 512))}
        m.update(cst); m.update(par); m.update(weights)
        in_maps.append(m)
    if "nc" not in _NC_CACHE:
        _NC_CACHE["nc"] = build_program()
    res = run_bass_kernel_spmd(_NC_CACHE["nc"], in_maps, core_ids=list(range(NCORES)))
    R = res.results
    g = lambda k: np.stack([np.asarray(R[c][k], dtype=np.float32) for c in range(NCORES)])
    y_p = g("y_p")
    y_s = g("y_s").reshape(128, 4, D)
    ret_p = g("o_ret_p")[None]
    conv_p = g("o_conv_p")[None]
    mk = g("o_mk").reshape(1, 8, 256, 4, 128)
    mv = g("o_mv").reshape(1, 8, 256, 4, 128)
    ret_s = g("o_ret_s").reshape(1, 128, 4, 128, 256)
    conv_s = g("o_conv_s").reshape(1, 128, 2, 2 * DFF)
    sgv = g("o_sgv").reshape(1, 128, 4, 512)
    return (y_p, y_s, ret_p, conv_p, mk, mv, ret_s, conv_s, sgv)
```
